# Optimizing a Trainium2 kernel written in Bass

```python
import math
import jax, jax.numpy as jnp
from jax import lax
import numpy as np

D_MODEL = 1024
BATCH = 8
SEQ = 4096
DEPTH = 2

HEAD_DIM = 64
HEADS_PER_GROUP = 8
DILATION_GROUPS = ((128, 1), (512, 4), (2048, 16))
N_GROUPS = len(DILATION_GROUPS)
ATTN_WIDTH = N_GROUPS * HEADS_PER_GROUP * HEAD_DIM
ATTN_OUT_WIDTH = HEADS_PER_GROUP * HEAD_DIM
CONV_CH = D_MODEL
CONV_WIDTH = 31
D_FF = ((8 * D_MODEL // 3 + 255) // 256) * 256
IN_WIDTH = 3 * ATTN_WIDTH + 2 * CONV_CH + 2 * D_MODEL
ROPE_THETA = 10000.0
Q_BLOCK = 128
EPS = 1e-6
NEG = -1e30

kernel_name = "hybrid_dilated_attn_conformer_conv_macaron"


def rmsnorm(x, g):
    xf = x.astype(jnp.float32)
    y = xf * lax.rsqrt(jnp.mean(xf * xf, axis=-1, keepdims=True) + EPS)
    return (y * g.astype(jnp.float32)).astype(x.dtype)


def swiglu_ffn(h, w_up, w_down):
    a, b = jnp.split(h @ w_up, 2, axis=-1)
    return (jax.nn.silu(a) * b) @ w_down


def rope_tables(seq):
    pos = jnp.arange(seq, dtype=jnp.float32)
    inv = ROPE_THETA ** (-jnp.arange(0, HEAD_DIM, 2, dtype=jnp.float32) / HEAD_DIM)
    ang = pos[:, None] * inv[None, :]
    return jnp.cos(ang), jnp.sin(ang)


def apply_rope(t, cos, sin):
    tf = t.astype(jnp.float32)
    t1, t2 = jnp.split(tf, 2, axis=-1)
    c = cos[:, None, None, :]
    s = sin[:, None, None, :]
    return jnp.concatenate([t1 * c - t2 * s, t1 * s + t2 * c], axis=-1).astype(t.dtype)


def dilated_window_attn(q, k, v, dil, w_sub):
    B, S, H, Dh = q.shape
    L = S // dil
    qb_size = min(Q_BLOCK, L)
    nb = -(-L // qb_size)
    Lp = nb * qb_size
    kb_size = qb_size + 2 * w_sub

    def to_sub(t):
        return t.reshape(B, L, dil, H, Dh).transpose(0, 2, 1, 3, 4)

    qs, ks, vs = to_sub(q), to_sub(k), to_sub(v)
    qs = jnp.pad(qs, ((0, 0), (0, 0), (0, Lp - L), (0, 0), (0, 0)))
    kpad = ((0, 0), (0, 0), (w_sub, w_sub + Lp - L), (0, 0), (0, 0))
    ks = jnp.pad(ks, kpad)
    vs = jnp.pad(vs, kpad)

    qblk = qs.reshape(B, dil, nb, qb_size, H, Dh)
    key_idx = jnp.arange(nb)[:, None] * qb_size + jnp.arange(kb_size)[None, :]
    kblk = ks[:, :, key_idx]
    vblk = vs[:, :, key_idx]

    scores = jnp.einsum('brnqhd,brnkhd->brnhqk', qblk, kblk,
                        preferred_element_type=jnp.float32) * (1.0 / math.sqrt(Dh))
    qpos = jnp.arange(nb)[:, None] * qb_size + jnp.arange(qb_size)[None, :]
    kpos = key_idx - w_sub
    valid = (jnp.abs(qpos[:, :, None] - kpos[:, None, :]) <= w_sub) \
        & (kpos[:, None, :] >= 0) & (kpos[:, None, :] < L)
    scores = jnp.where(valid[None, None, :, None], scores, NEG)

    m = jnp.max(scores, axis=-1, keepdims=True)
    p = jnp.exp(scores - m)
    denom = jnp.sum(p, axis=-1)
    out = jnp.einsum('brnhqk,brnkhd->brnqhd', p, vblk.astype(jnp.float32))
    out = out / jnp.moveaxis(denom, -1, -2)[..., None]
    lse = jnp.moveaxis(m[..., 0] + jnp.log(denom), -1, -2)

    out = out.reshape(B, dil, Lp, H, Dh)[:, :, :L].transpose(0, 2, 1, 3, 4).reshape(B, S, H, Dh)
    lse = lse.reshape(B, dil, Lp, H)[:, :, :L].transpose(0, 2, 1, 3).reshape(B, S, H)
    return out.astype(v.dtype), lse


def conv_module(glu_in, conv_w, conv_b, ln_g, ln_b, w_proj, b_proj):
    a, g = jnp.split(glu_in, 2, axis=-1)
    u = a * jax.nn.sigmoid(g)
    u = lax.conv_general_dilated(
        u, conv_w[:, None, :].astype(u.dtype), window_strides=(1,),
        padding=[(CONV_WIDTH // 2, CONV_WIDTH // 2)],
        dimension_numbers=('NWC', 'WIO', 'NWC'),
        feature_group_count=CONV_CH) + conv_b
    uf = u.astype(jnp.float32)
    mu = jnp.mean(uf, axis=-1, keepdims=True)
    var = jnp.mean(jnp.square(uf - mu), axis=-1, keepdims=True)
    u = ((uf - mu) * lax.rsqrt(var + EPS) * ln_g.astype(jnp.float32)
         + ln_b.astype(jnp.float32)).astype(u.dtype)
    return jax.nn.silu(u) @ w_proj + b_proj


def mixer(h, w_in, b_in, q_norm, k_norm, w_attn_proj, conv_w, conv_b, conv_ln_g,
          conv_ln_b, w_conv_proj, b_conv_proj, w_out, cos, sin):
    B, S, _ = h.shape
    z = h @ w_in + b_in
    q, k, v, glu_in, gates = jnp.split(
        z, [ATTN_WIDTH, 2 * ATTN_WIDTH, 3 * ATTN_WIDTH, 3 * ATTN_WIDTH + 2 * CONV_CH], axis=-1)
    shp = (B, S, N_GROUPS, HEADS_PER_GROUP, HEAD_DIM)
    q = apply_rope(rmsnorm(q.reshape(shp), q_norm[:, None, :]), cos, sin)
    k = apply_rope(rmsnorm(k.reshape(shp), k_norm[:, None, :]), cos, sin)
    v = v.reshape(shp)

    outs, lses = [], []
    for g, (window, dil) in enumerate(DILATION_GROUPS):
        o, l = dilated_window_attn(q[:, :, g], k[:, :, g], v[:, :, g], dil, window // (2 * dil))
        outs.append(o)
        lses.append(l)
    wts = jax.nn.softmax(jnp.stack(lses, axis=0), axis=0)
    y_attn = jnp.sum(wts[..., None] * jnp.stack(outs, axis=0).astype(jnp.float32), axis=0)
    y_attn = y_attn.astype(h.dtype).reshape(B, S, ATTN_OUT_WIDTH) @ w_attn_proj

    y_conv = conv_module(glu_in, conv_w, conv_b, conv_ln_g, conv_ln_b, w_conv_proj, b_conv_proj)

    g_attn, g_conv = jnp.split(jax.nn.sigmoid(gates), 2, axis=-1)
    return (g_attn * y_attn + g_conv * y_conv) @ w_out


def setup_inputs(seed: int = 0) -> dict:
    key = jax.random.key(seed)
    ks = jax.random.split(key, 24)
    f32 = jnp.float32
    L = DEPTH

    def w(k, shape, fan_in):
        return jax.random.normal(k, shape, f32) * (fan_in ** -0.5)

    def gain(k, shape):
        return 1.0 + 0.02 * jax.random.normal(k, shape, f32)

    def bias(k, shape):
        return 0.02 * jax.random.normal(k, shape, f32)

    return {
        "x": jax.random.normal(ks[0], (BATCH, SEQ, D_MODEL), f32),
        "ffn1_norm": gain(ks[1], (L, D_MODEL)),
        "ffn1_w_up": w(ks[2], (L, D_MODEL, 2 * D_FF), D_MODEL),
        "ffn1_w_down": w(ks[3], (L, D_FF, D_MODEL), D_FF),
        "mix_norm": gain(ks[4], (L, D_MODEL)),
        "w_in": w(ks[5], (L, D_MODEL, IN_WIDTH), D_MODEL),
        "b_in": bias(ks[6], (L, IN_WIDTH)),
        "q_norm": gain(ks[7], (L, N_GROUPS, HEAD_DIM)),
        "k_norm": gain(ks[8], (L, N_GROUPS, HEAD_DIM)),
        "w_attn_proj": w(ks[9], (L, ATTN_OUT_WIDTH, D_MODEL), ATTN_OUT_WIDTH),
        "conv_w": w(ks[10], (L, CONV_WIDTH, CONV_CH), CONV_WIDTH),
        "conv_b": bias(ks[11], (L, CONV_CH)),
        "conv_ln_g": gain(ks[12], (L, CONV_CH)),
        "conv_ln_b": bias(ks[13], (L, CONV_CH)),
        "w_conv_proj": w(ks[14], (L, CONV_CH, D_MODEL), CONV_CH),
        "b_conv_proj": bias(ks[15], (L, D_MODEL)),
        "w_out": w(ks[16], (L, D_MODEL, D_MODEL), D_MODEL),
        "ffn2_norm": gain(ks[17], (L, D_MODEL)),
        "ffn2_w_up": w(ks[18], (L, D_MODEL, 2 * D_FF), D_MODEL),
        "ffn2_w_down": w(ks[19], (L, D_FF, D_MODEL), D_FF),
        "final_norm": gain(ks[20], (L, D_MODEL)),
    }


def reference(x, ffn1_norm, ffn1_w_up, ffn1_w_down, mix_norm, w_in, b_in, q_norm, k_norm,
              w_attn_proj, conv_w, conv_b, conv_ln_g, conv_ln_b, w_conv_proj, b_conv_proj,
              w_out, ffn2_norm, ffn2_w_up, ffn2_w_down, final_norm):
    cos, sin = rope_tables(x.shape[1])
    for l in range(DEPTH):
        x = x + 0.5 * swiglu_ffn(rmsnorm(x, ffn1_norm[l]), ffn1_w_up[l], ffn1_w_down[l])
        x = x + mixer(rmsnorm(x, mix_norm[l]), w_in[l], b_in[l], q_norm[l], k_norm[l],
                      w_attn_proj[l], conv_w[l], conv_b[l], conv_ln_g[l], conv_ln_b[l],
                      w_conv_proj[l], b_conv_proj[l], w_out[l], cos, sin)
        x = x + 0.5 * swiglu_ffn(rmsnorm(x, ffn2_norm[l]), ffn2_w_up[l], ffn2_w_down[l])
        x = rmsnorm(x, final_norm[l])
    return x
```

```python
import contextlib
import numpy as np
import concourse.bass as bass
import concourse.mybir as mybir
from concourse.bass_utils import run_bass_kernel_spmd

F32 = mybir.dt.float32
BF16 = mybir.dt.bfloat16
AF = mybir.ActivationFunctionType
ALU = mybir.AluOpType

D = 1024
S = 4096
DFF = 2816
NL = 2
T = 512
NT = S // T
KC = 8
NJ = DFF // 128
EPS = 1e-6
INW = 8704
GROUPS = ((128, 1), (512, 4), (2048, 16))
NCORES = 8

SEM_CAP = 12000


def _ss(base, n, step):
    return slice(base, base + (n - 1) * step + 1, step)


class Op:
    __slots__ = ("eng", "fn", "reads", "writes", "dma_key", "waits", "sig", "persist")

    def __init__(self, eng, fn, reads, writes, dma_key, persist):
        self.eng = eng
        self.fn = fn
        self.reads = reads
        self.writes = writes
        self.dma_key = dma_key
        self.waits = []
        self.sig = None
        self.persist = persist


class Sched:
    ENGS = ("sync", "scalar", "vector", "gpsimd", "tensor")

    def __init__(self):
        self.ops = []
        self.inserts = []

    def add(self, eng, fn, reads=(), writes=(), dma_key=None, persist=False):
        op = Op(eng, fn, tuple(reads), tuple(writes), dma_key, persist)
        self.ops.append(op)
        return op

    def add_at(self, pos, eng, fn, reads=(), writes=(), dma_key=None, persist=False):
        op = Op(eng, fn, tuple(reads), tuple(writes), dma_key, persist)
        self.inserts.append((pos, len(self.inserts), op))
        return op

    def fence(self):
        self.ops.append("FENCE")

    def pos(self):
        return len(self.ops)

    def finalize(self):
        ins = sorted(self.inserts, key=lambda z: (z[0], z[1]))
        out = []
        k = 0
        for i, op in enumerate(self.ops):
            while k < len(ins) and ins[k][0] <= i:
                out.append(ins[k][2])
                k += 1
            out.append(op)
        while k < len(ins):
            out.append(ins[k][2])
            k += 1
        self.ops = out
        self.inserts = []

    def emit(self, nc, stack):
        self.finalize()
        ops = self.ops
        last_w = {}
        readers = {}
        last_dma_on_key = {}
        last_on_eng = {}
        deps_of = []
        fence_set = {}
        fence_seen = {}
        fence_gen = 0
        dma_since_fence = {}
        real = []

        def skey(o):
            return ("dma", o.dma_key) if o.dma_key is not None else o.eng

        for op in ops:
            if op == "FENCE":
                fence_gen += 1
                fence_set = {}
                for e_, j in last_on_eng.items():
                    fence_set[e_] = j
                for k_, j in dma_since_fence.items():
                    fence_set[("dma", k_)] = j
                dma_since_fence = {}
                continue
            i = len(real)
            real.append(op)
            d = {}

            def addd(j):
                k = skey(real[j])
                if d.get(k, -1) < j:
                    d[k] = j

            for b in op.reads:
                if b in last_w:
                    addd(last_w[b])
            for b in op.writes:
                if b in last_w:
                    addd(last_w[b])
                rb = readers.get(b)
                if rb:
                    for j in rb.values():
                        addd(j)
            if op.dma_key is not None and op.dma_key in last_dma_on_key:
                addd(last_dma_on_key[op.dma_key])
            if not op.persist and fence_seen.get(op.eng, 0) != fence_gen:
                for j in fence_set.values():
                    addd(j)
                fence_seen[op.eng] = fence_gen
            if op.dma_key is None and op.eng == "tensor":
                d.pop("tensor", None)
            deps_of.append(set(d.values()))
            me = skey(op)
            for b in op.reads:
                readers.setdefault(b, {})[me] = i
            for b in op.writes:
                last_w[b] = i
                readers[b] = {}
            if op.dma_key is not None:
                last_dma_on_key[op.dma_key] = i
                if not op.persist:
                    dma_since_fence[op.dma_key] = i
            else:
                last_on_eng[op.eng] = i
        needs_sig = [False] * len(real)
        for i, op in enumerate(real):
            for j in deps_of[i]:
                needs_sig[j] = True
        eng_sem = {}
        eng_cnt = {}
        dma_sem = {}
        dma_cnt = {}
        nsem = [0]

        def new_sem(tag):
            nsem[0] += 1
            return stack.enter_context(nc.semaphore(f"{tag}{nsem[0]}"))

        for i, op in enumerate(real):
            if op.dma_key is not None:
                if op.dma_key not in dma_sem:
                    dma_sem[op.dma_key] = new_sem("d")
                    dma_cnt[op.dma_key] = 0
                dma_cnt[op.dma_key] += 16
                if dma_cnt[op.dma_key] > SEM_CAP * 4:
                    dma_sem[op.dma_key] = new_sem("d")
                    dma_cnt[op.dma_key] = 16
                op.sig = (dma_sem[op.dma_key], dma_cnt[op.dma_key])
            elif needs_sig[i]:
                e = op.eng
                if e not in eng_sem or eng_cnt[e] >= SEM_CAP:
                    eng_sem[e] = new_sem(e[0])
                    eng_cnt[e] = 0
                eng_cnt[e] += 1
                op.sig = (eng_sem[e], eng_cnt[e])
        waited = {e: {} for e in self.ENGS}
        nwaits = 0
        for i, op in enumerate(real):
            w = waited[op.eng]
            best = {}
            for j in deps_of[i]:
                sem, cnt = real[j].sig
                k = id(sem)
                if w.get(k, 0) >= cnt:
                    continue
                if k not in best or best[k][1] < cnt:
                    best[k] = (sem, cnt)
            for k, (sem, cnt) in best.items():
                w[k] = cnt
                op.waits.append((sem, cnt))
                nwaits += 1
        self.stats = dict(n_ops=len(real), n_sems=nsem[0], n_waits=nwaits,
                          n_sig=sum(1 for o in real if o.sig is not None))
        per_eng = {e: [o for o in real if o.eng == e] for e in self.ENGS}
        tail = []
        for e in self.ENGS:
            if e in eng_sem:
                pass
        last_sigs = {}
        for o in real:
            if o.sig is not None:
                last_sigs[id(o.sig[0])] = o.sig
        with nc.Block() as block:
            def runner(name):
                def body(e):
                    for o in per_eng[name]:
                        for sem, cnt in o.waits:
                            e.wait_ge(sem, cnt)
                        ins = o.fn(e)
                        if o.sig is not None:
                            if o.dma_key is not None:
                                ins.then_inc(o.sig[0], 16)
                            else:
                                ins.then_inc(o.sig[0], 1)
                    if name == "sync":
                        for sem, cnt in last_sigs.values():
                            if waited["sync"].get(id(sem), 0) < cnt:
                                e.wait_ge(sem, cnt)
                return body
            block.sync(runner("sync"))
            block.scalar(runner("scalar"))
            block.vector(runner("vector"))
            block.gpsimd(runner("gpsimd"))
            block.tensor(runner("tensor"))


def _unit_list():
    units = []
    for w in ("f1", "f2"):
        pass
    def ffn(tag):
        u = []
        for i in range(NJ // 2):
            u.append((f"{tag}up{i}", 8 * 512))
        for oc in range(8):
            u.append((f"{tag}dn{oc}", NJ * 128))
        return u
    m1 = [(f"win{i}", 8 * 512) for i in (6, 7, 8, 13, 14, 15, 16, 9, 10, 11, 12, 0, 1, 2, 3, 4, 5)]
    m3 = [(f"cdg{c}", 31 * 128) for c in range(8)]
    m3 += [("cp0", 8 * 512), ("cp1", 8 * 512), ("ap", 4 * 1024), ("wo0", 8 * 512), ("wo1", 8 * 512)]
    return ffn("f1"), m1, m3, ffn("f2")


def _pack_weights(inp):
    offs, total = _weight_offsets()
    out = np.empty(total, np.float32)

    def put(key, arr):
        arr = np.ascontiguousarray(arr, dtype=np.float32)
        assert arr.shape[0] == 128
        off, n = offs[key]
        assert arr.size == 128 * n, (key, arr.shape, n)
        out[off:off + 128 * n] = arr.reshape(-1)

    for l in range(NL):
        for tag, wu, wd in (("f1", inp["ffn1_w_up"][l], inp["ffn1_w_down"][l]),
                            ("f2", inp["ffn2_w_up"][l], inp["ffn2_w_down"][l])):
            wuk = np.asarray(wu, np.float32).reshape(8, 128, 2 * DFF)
            for i in range(NJ // 2):
                j0, j1 = 2 * i, 2 * i + 1
                c = np.concatenate([np.arange(j0 * 128, (j0 + 1) * 128),
                                    np.arange(DFF + j0 * 128, DFF + (j0 + 1) * 128),
                                    np.arange(j1 * 128, (j1 + 1) * 128),
                                    np.arange(DFF + j1 * 128, DFF + (j1 + 1) * 128)])
                put((l, f"{tag}up{i}"), wuk[:, :, c].transpose(1, 0, 2))
            wdk = np.asarray(wd, np.float32).reshape(NJ, 128, D)
            for oc in range(8):
                put((l, f"{tag}dn{oc}"), wdk[:, :, oc * 128:(oc + 1) * 128].transpose(1, 0, 2))
        wk = np.asarray(inp["w_in"][l], np.float32).reshape(8, 128, INW)
        cols = []
        for i in range(9):
            cols.append(np.arange(i * 512, (i + 1) * 512))
        ga = 4608
        gg = 4608 + 1024
        for u in range(4):
            cols.append(np.concatenate([np.arange(ga + u * 256, ga + (u + 1) * 256),
                                        np.arange(gg + u * 256, gg + (u + 1) * 256)]))
        for u in range(4):
            cols.append(np.arange(6656 + u * 512, 6656 + (u + 1) * 512))
        for i, c in enumerate(cols):
            put((l, f"win{i}"), wk[:, :, c].transpose(1, 0, 2))
        cw = np.asarray(inp["conv_w"][l], np.float32)
        for c in range(8):
            dg = np.zeros((128, 31, 128), np.float32)
            idx = np.arange(128)
            dg[idx, :, idx] = cw[:, c * 128:(c + 1) * 128].T
            put((l, f"cdg{c}"), dg)
        wcp = np.asarray(inp["w_conv_proj"][l], np.float32).reshape(8, 128, 1024)
        for u in range(2):
            put((l, f"cp{u}"), wcp[:, :, u * 512:(u + 1) * 512].transpose(1, 0, 2))
        wap = np.asarray(inp["w_attn_proj"][l], np.float32).reshape(4, 128, 1024)
        put((l, "ap"), wap.transpose(1, 0, 2))
        wo = np.asarray(inp["w_out"][l], np.float32).reshape(8, 128, 1024)
        for u in range(2):
            put((l, f"wo{u}"), wo[:, :, u * 512:(u + 1) * 512].transpose(1, 0, 2))
    return out, offs


def _weight_offsets():
    offs = {}
    pos = 0
    f1, m1, m3, f2 = _unit_list()
    for l in range(NL):
        for name, n in f1 + m1 + m3 + f2:
            offs[(l, name)] = (pos, n)
            pos += 128 * n
    return offs, pos


C_ID = 0
C_L = 128
M_R = 0
M_MASK = 128
NMASKCOL = 1664
CL_N = 144
NCOL = C_L + NL * CL_N


def _pack_masks():
    c = np.zeros((128, NMASKCOL), np.float32)
    R = np.zeros((128, 128), np.float32)
    for i in range(128):
        if i % 64 < 32:
            R[i + 32, i] = -1.0
        else:
            R[i - 32, i] = 1.0
    c[:, M_R:M_R + 128] = R
    ii = np.arange(128)[:, None]
    jj = np.arange(128)[None, :]
    m1 = (ii >= jj).astype(np.float32)
    m2 = (ii <= jj).astype(np.float32)
    mF = ((ii < 64) & (jj <= ii + 64)).astype(np.float32)
    mL = ((ii >= 64) & (ii - jj <= 64)).astype(np.float32)
    c[:, M_MASK:M_MASK + 1536] = np.concatenate([m1, m2, m1, m2, mF, m2, mF, m2, m1, mL, m1, mL], axis=1)
    return c


def _pack_consts(inp):
    c = np.zeros((128, NCOL), np.float32)
    c[:, C_ID:C_ID + 128] = np.eye(128, dtype=np.float32)

    def colmajor(v):
        return np.asarray(v, np.float32).reshape(-1, 128).T

    for l in range(NL):
        b = C_L + l * CL_N
        c[:, b + 0:b + 8] = colmajor(inp["ffn1_norm"][l])
        c[:, b + 8:b + 16] = colmajor(inp["mix_norm"][l])
        c[:, b + 16:b + 24] = colmajor(inp["ffn2_norm"][l])
        c[:, b + 24:b + 32] = colmajor(inp["final_norm"][l])
        bi = np.asarray(inp["b_in"][l], np.float32)
        c[:, b + 32:b + 56] = colmajor(bi[0:3072])
        c[:, b + 56:b + 64] = colmajor(bi[4608:5632])
        c[:, b + 64:b + 72] = colmajor(bi[5632:6656])
        c[:, b + 72:b + 88] = colmajor(bi[6656:8704])
        for s_, nm in enumerate(("q_norm", "k_norm")):
            for g in range(3):
                v = np.asarray(inp[nm][l][g], np.float32)
                for hp in range(4):
                    c[:, b + 88 + s_ * 12 + g * 4 + hp] = np.concatenate([v, v])
        c[:, b + 112:b + 120] = colmajor(inp["conv_b"][l])
        c[:, b + 120:b + 128] = colmajor(inp["conv_ln_g"][l])
        c[:, b + 128:b + 136] = colmajor(inp["conv_ln_b"][l])
        c[:, b + 136:b + 144] = colmajor(inp["b_conv_proj"][l])
    return c


def _rope_tables():
    pos = np.arange(S, dtype=np.float32)
    inv = (np.float32(10000.0) ** (-(np.arange(0, 64, 2, dtype=np.float32)) / np.float32(64))).astype(np.float32)
    ang = (pos[:, None] * inv[None, :]).astype(np.float32)
    cos = np.cos(ang).astype(np.float32).T
    sin = np.sin(ang).astype(np.float32).T
    return np.ascontiguousarray(np.tile(cos, (4, 1))), np.ascontiguousarray(np.tile(sin, (4, 1)))


class Prog:
    def __init__(self, cfg):
        self.cfg = cfg
        self.nc = bass.Bass("TRN2", target_bir_lowering=False)
        self.sc = Sched()
        self.stream_n = 0
        self.stream_first_use = []
        self.woffs, self.wtotal = _weight_offsets()

    def carve(self, off_bytes, shape, dtype):
        esz = 4 if dtype == F32 else 2
        n = int(np.prod(shape[1:]))
        assert off_bytes % 4 == 0
        w0 = off_bytes // 4
        nw = (n * esz + 3) // 4
        ap = self.arena[:, w0:w0 + nw]
        if dtype != F32:
            ap = ap.bitcast(dtype)
        ap = ap[:, 0:n]
        if len(shape) == 3:
            ap = ap.rearrange("p (a b) -> p a b", a=shape[1])
        elif len(shape) == 4:
            ap = ap.rearrange("p (a b c) -> p a b c", a=shape[1], b=shape[2])
        return ap

    def wunit(self, l, name, readers_engine="tensor"):
        n = self.stream_n
        self.stream_n += 1
        off, npp = self.woffs[(l, name)]
        slot = n % 3
        self.stream_first_use.append(self.sc.pos())
        ring = self.ring[slot]
        dst = ring[:, 0:npp]
        src = self.wbf[off:off + 128 * npp].rearrange("(p n) -> p n", p=128)
        uid = (l, name)
        pos = max(self.stream_first_use[n - 2] if n >= 2 else 0, self.stream_min_pos)
        self.sc.add_at(pos, "sync", lambda e, d=dst, s=src: e.dma_start(out=d, in_=s),
                       reads=[("wbf", uid)], writes=[("ring", slot)], dma_key=("ring", slot), persist=True)
        return dst, ("ring", slot)

    def build(self):
        nc = self.nc
        cfg = self.cfg
        sc = self.sc
        st = contextlib.ExitStack()
        self.st = st
        self.x_in = nc.dram_tensor("x", [S, D], F32, kind="ExternalInput").ap()
        self.wpack = nc.dram_tensor("wpack", [self.wtotal], F32, kind="ExternalInput").ap()
        self.cpack = nc.dram_tensor("cpack", [128, NCOL], F32, kind="ExternalInput").ap()
        self.cmask = nc.dram_tensor("cmask", [128, NMASKCOL], F32, kind="ExternalInput").ap()
        self.y_out = nc.dram_tensor("y", [S, D], F32, kind="ExternalOutput").ap()
        self.wbf = nc.dram_tensor("wbf", [self.wtotal], BF16, kind="Internal").ap()
        self.bv = nc.dram_tensor("bv", [NL, 1536], F32, kind="ExternalInput").ap()
        self.cosT = nc.dram_tensor("cosT", [128, S], F32, kind="ExternalInput").ap()
        self.sinT = nc.dram_tensor("sinT", [128, S], F32, kind="ExternalInput").ap()
        sk = "ExternalOutput" if cfg.get("dbg") else "Internal"
        self.qT_s = nc.dram_tensor("qT_s", [1536, S], BF16, kind=sk).ap()
        self.kT_s = nc.dram_tensor("kT_s", [1536, S], BF16, kind=sk).ap()
        self.v_s = nc.dram_tensor("v_s", [S, 24 * 65], BF16, kind=sk).ap()
        self.uc_s = nc.dram_tensor("uc_s", [1024, S], BF16, kind=sk).ap()
        self.g_s = nc.dram_tensor("g_s", [2048, S], BF16, kind=sk).ap()
        self.nd_s = nc.dram_tensor("nd_s", [3, S, 520], F32, kind=sk).ap()
        self.xT = st.enter_context(nc.sbuf_tensor("xT", [128, KC, S], F32))
        self.cst = st.enter_context(nc.sbuf_tensor("cst", [128, NCOL], F32))
        self.cbf = st.enter_context(nc.sbuf_tensor("cbf", [128, 2048], BF16))
        self.bgt = st.enter_context(nc.sbuf_tensor("bgt", [128, 48], F32))
        self.epst = st.enter_context(nc.sbuf_tensor("epst", [128, 8], F32))
        self.epsc = self.epst[:, 0:1]
        self.ring = [st.enter_context(nc.sbuf_tensor(f"ring{i}", [128, 4096], BF16)) for i in range(3)]
        ARENA_BYTES = cfg.get("arena_bytes", 49 * 1024)
        self.arena_bytes = ARENA_BYTES
        self.arena_t = st.enter_context(nc.sbuf_tensor("arena", [128, ARENA_BYTES // 4], F32))
        self.arena = self.arena_t[:]
        self.ps = [st.enter_context(nc.psum_tensor(f"ps{i}", [128, 512], F32)) for i in range(8)]
        self.onesD = self.cbf[:, 0:128]
        self.blk = self.cbf[:, 128:256]
        self.Rbf = self.cbf[:, 256:384]
        self.maskbf = [self.cbf[:, 384 + i * 512:384 + (i + 1) * 512] for i in range(3)]
        self.identbf = self.cbf[:, 1920:2048]
        self.ident = self.cst[:, C_ID:C_ID + 128]

        self.phase_setup()
        self.stream_min_pos = sc.pos()
        self.phase_load_x()
        nl = cfg.get("nl", NL)
        for l in range(nl):
            if cfg.get("ffn1", True):
                self.phase_ffn(l, 0)
            if cfg.get("m1", True):
                self.phase_m1(l)
            if cfg.get("m2", True):
                self.phase_m2(l)
            if cfg.get("m3", True):
                self.phase_m3(l)
            if cfg.get("ffn2", True):
                self.phase_ffn(l, 1)
            self.phase_final_norm(l, last=(l == nl - 1))
        sc.emit(nc, st)
        st.close()
        return nc

    def phase_setup(self):
        sc = self.sc
        f1, m1, m3, f2 = _unit_list()
        k = 0
        for l in range(NL):
            for name, n in f1 + m1 + m3 + f2:
                off, npp = self.woffs[(l, name)]
                src = self.wpack[off:off + 128 * npp].rearrange("(p n) -> p n", p=128)
                dst = self.wbf[off:off + 128 * npp].rearrange("(p n) -> p n", p=128)
                sc.add("gpsimd", lambda e, d=dst, s=src: e.dma_start(out=d, in_=s),
                       writes=[("wbf", (l, name))], dma_key=("cv", k % 8), persist=True)
                k += 1
        sc.add("sync", lambda e: e.dma_start(out=self.cst[:], in_=self.cpack[:, :]),
               writes=["cst"], dma_key="cst", persist=True)
        sc.add("vector", lambda e: e.memset(self.onesD, 1.0 / D), writes=["onesD"], persist=True)
        sc.add("vector", lambda e: e.memset(self.epst[:], EPS), writes=["epsc"], persist=True)
        sc.add("vector", lambda e: e.memset(self.blk, 0.0), writes=["blk"], persist=True)
        sc.add("vector", lambda e: e.memset(self.cbf[0:64, 128:192], 1.0 / 64), writes=["blk"], persist=True)
        sc.add("vector", lambda e: e.memset(self.cbf[64:128, 192:256], 1.0 / 64), writes=["blk"], persist=True)
        sc.add("gpsimd", lambda e: e.dma_start(out=self.cbf[:, 256:1920], in_=self.cmask[:, :]),
               writes=["Rbf", "maskbf"], dma_key="cmask", persist=True)
        for l in range(NL):
            b = C_L + l * CL_N
            sc.add("vector", lambda e, l=l, b=b: e.tensor_tensor(out=self.bgt[:, l * 24:(l + 1) * 24], in0=self.cst[:, b + 32:b + 56],
                                                            in1=self.cst[:, b + 88:b + 112], op=ALU.mult),
                   reads=["cst"], writes=["bgt"], persist=True)
        sc.add("vector", lambda e: e.tensor_copy(out=self.identbf, in_=self.ident),
               reads=["cst"], writes=["identbf"], persist=True)

    def phase_load_x(self):
        sc = self.sc
        stg = [self.carve(i * 4096, [128, 1024], F32) for i in range(2)]
        for i in range(S // 128):
            sl = i % 2
            sc.add("sync", lambda e, sl=sl, i=i: e.dma_start(out=stg[sl], in_=self.x_in[i * 128:(i + 1) * 128, :]),
                   writes=[("xstg", sl)], dma_key=("xstg", sl))
            for hb in range(2):
                bank = (2 * i + hb) % 4
                pst = self.ps[bank]
                for q in range(4):
                    kc = hb * 4 + q
                    sc.add("tensor", lambda e, pst=pst, q=q, sl=sl, kc=kc: e.transpose(
                        pst[:, q * 128:(q + 1) * 128], stg[sl][:, kc * 128:(kc + 1) * 128], self.ident),
                        reads=[("xstg", sl), "cst"], writes=[("ps", bank)])
                dst = self.xT[:, hb * 4:hb * 4 + 4, i * 128:(i + 1) * 128]
                src = pst[:].rearrange("p (a b) -> p a b", a=4)
                eng = "scalar" if hb == 0 else "vector"
                if eng == "scalar":
                    fn = lambda e, dst=dst, src=src: e.copy(out=dst, in_=src)
                else:
                    fn = lambda e, dst=dst, src=src: e.tensor_copy(out=dst, in_=src)
                sc.add(eng, fn, reads=[("ps", bank)],
                       writes=[("x", kc, i // 4) for kc in range(hb * 4, hb * 4 + 4)])
        sc.fence()

    def emit_rstd(self, out, in_, in_name, out_name):
        sc = self.sc
        sc.add("scalar", lambda e: e.activation(out=out, in_=in_, func=AF.Sqrt, bias=self.epsc, scale=1.0),
               reads=[in_name, "epsc"], writes=[out_name])
        sc.add("vector", lambda e: e.reciprocal(out=out, in_=out), reads=[out_name], writes=[out_name])

    def emit_norm(self, l, gcol, t, h, hname, sq, rstd):
        sc = self.sc
        tok = slice(t * T, (t + 1) * T)
        for kc in range(KC):
            s2 = kc % 2
            sc.add("scalar", lambda e, kc=kc, s2=s2: e.activation(out=sq[s2], in_=self.xT[:, kc, tok], func=AF.Square),
                   reads=[("x", kc, t)], writes=[("sq", s2)])
            sc.add("tensor", lambda e, kc=kc, s2=s2: e.matmul(self.ps[0][:], lhsT=self.onesD, rhs=sq[s2],
                                                              start=(kc == 0), stop=(kc == KC - 1)),
                   reads=[("sq", s2), "onesD"], writes=[("ps", 0)])
        self.emit_rstd(rstd, self.ps[0][:], ("ps", 0), "rstd")
        cb = C_L + l * CL_N + gcol
        for kc in range(KC):
            sc.add("vector", lambda e, kc=kc: e.scalar_tensor_tensor(
                out=h[:, kc, :], in0=self.xT[:, kc, tok], scalar=self.cst[:, cb + kc:cb + kc + 1], in1=rstd,
                op0=ALU.mult, op1=ALU.mult),
                reads=[("x", kc, t), "rstd", "cst"], writes=[(hname, kc)])

    def phase_ffn(self, l, which):
        sc = self.sc
        tag = "f1" if which == 0 else "f2"
        gcol = 0 if which == 0 else 16
        hb = [self.carve(0, [128, 8, T], BF16), self.carve(8192, [128, 8, T], BF16)]
        gated = self.carve(16384, [128, NJ, T], BF16)
        o = 16384 + NJ * 1024
        sq = [self.carve(o, [128, T], BF16), self.carve(o + 1024, [128, T], BF16)]
        rstd = self.carve(o + 2048, [128, T], F32)
        sl = [self.carve(o + 4096, [128, T], F32), self.carve(o + 6144, [128, T], F32)]
        for t in range(NT):
            hs = t % 2
            h = hb[hs]
            hname = ("h", hs)
            tok = slice(t * T, (t + 1) * T)
            self.emit_norm(l, gcol, t, h, hname, sq, rstd)
            hreads = [(hname, kc) for kc in range(KC)]
            cnt = 0
            for u in range(NJ // 2):
                w, wname = self.wunit(l, f"{tag}up{u}")
                w3 = w.rearrange("p (k f) -> p k f", k=8)
                for jj in range(2):
                    j = 2 * u + jj
                    ab = cnt % 2
                    cnt += 1
                    pa = self.ps[1 + ab]
                    pb = self.ps[3 + ab]
                    for kc in range(KC):
                        sc.add("tensor", lambda e, pa=pa, kc=kc, jj=jj, w3=w3, h=h: e.matmul(
                            pa[:], lhsT=w3[:, kc, jj * 256:jj * 256 + 128], rhs=h[:, kc, :],
                            start=(kc == 0), stop=(kc == KC - 1)),
                            reads=[wname, (hname, kc)], writes=[("ps", 1 + ab)])
                    for kc in range(KC):
                        sc.add("tensor", lambda e, pb=pb, kc=kc, jj=jj, w3=w3, h=h: e.matmul(
                            pb[:], lhsT=w3[:, kc, jj * 256 + 128:jj * 256 + 256], rhs=h[:, kc, :],
                            start=(kc == 0), stop=(kc == KC - 1)),
                            reads=[wname, (hname, kc)], writes=[("ps", 3 + ab)])
                    sc.add("scalar", lambda e, pa=pa, ab=ab: e.activation(out=sl[ab], in_=pa[:], func=AF.Silu),
                           reads=[("ps", 1 + ab)], writes=[("sl", ab)])
                    sc.add("vector", lambda e, pb=pb, ab=ab, j=j: e.tensor_tensor(
                        out=gated[:, j, :], in0=sl[ab], in1=pb[:], op=ALU.mult),
                        reads=[("sl", ab), ("ps", 3 + ab)], writes=[("gated", j)])
            for oc in range(8):
                w, wname = self.wunit(l, f"{tag}dn{oc}")
                w3 = w.rearrange("p (k f) -> p k f", k=NJ)
                pd = self.ps[5 + oc % 2]
                for kc in range(NJ):
                    sc.add("tensor", lambda e, pd=pd, kc=kc, w3=w3: e.matmul(
                        pd[:], lhsT=w3[:, kc, :], rhs=gated[:, kc, :], start=(kc == 0), stop=(kc == NJ - 1)),
                        reads=[wname, ("gated", kc)], writes=[("ps", 5 + oc % 2)])
                sc.add("vector", lambda e, pd=pd, oc=oc, tok=tok: e.scalar_tensor_tensor(
                    out=self.xT[:, oc, tok], in0=pd[:], scalar=0.5, in1=self.xT[:, oc, tok],
                    op0=ALU.mult, op1=ALU.add),
                    reads=[("ps", 5 + oc % 2), ("x", oc, t)], writes=[("x", oc, t)])
        sc.fence()


    def phase_m1(self, l):
        sc = self.sc
        cbase = C_L + l * CL_N
        o = 0
        h = self.carve(o, [128, 8, T], BF16); o += 8192
        sq = [self.carve(o, [128, T], BF16), self.carve(o + 1024, [128, T], BF16)]; o += 2048
        rstd = self.carve(o, [128, T], F32); o += 2048
        sqc = [self.carve(o + i * 1024, [128, T], BF16) for i in range(2)]; o += 2048
        uu = [self.carve(o + i * 1024, [128, T], BF16) for i in range(2)]; o += 2048
        rsc = [self.carve(o + i * 2048, [128, T], F32) for i in range(2)]; o += 4096
        t1 = [self.carve(o + i * 2048, [128, T], F32) for i in range(2)]; o += 4096
        t2 = [self.carve(o + i * 2048, [128, T], F32) for i in range(2)]; o += 4096
        stg = [self.carve(o + i * 1024, [128, T], BF16) for i in range(4)]; o += 4096
        cs = [self.carve(o + i * 2048, [128, T], F32) for i in range(2)]; o += 4096
        bvb = self.carve(o, [128, 1536], BF16); o += 3072
        vstg = [self.carve(o + i * 1280, [128, 8, 65], BF16) for i in range(2)]; o += 2560
        assert o <= self.arena_bytes, o
        sc.add("gpsimd", lambda e: e.dma_start(out=bvb, in_=self.bv[l, :].partition_broadcast(128)),
               writes=["bvb"], dma_key="bvb")
        for i in range(2):
            sc.add("vector", lambda e, i=i: e.memset(vstg[i][:, :, 64:65], 1.0), writes=[("vstg", i)])
        stgn = [0]

        def store(src_name, stg_i, dram_ap):
            sc.add("sync", lambda e, i=stg_i, d=dram_ap: e.dma_start(out=d, in_=stg[i]),
                   reads=[("stg", stg_i)], writes=[src_name], dma_key=("stg", stg_i))

        hnames = [("h", 0, kc) for kc in range(KC)]
        for t in range(NT):
            tok = slice(t * T, (t + 1) * T)
            self.emit_norm(l, 8, t, h, ("h", 0), sq, rstd)
            sc.add("sync", lambda e, tok=tok: e.dma_start(out=cs[0], in_=self.cosT[:, tok]),
                   writes=[("cs", 0)], dma_key=("cs", 0))
            sc.add("sync", lambda e, tok=tok: e.dma_start(out=cs[1], in_=self.sinT[:, tok]),
                   writes=[("cs", 1)], dma_key=("cs", 1))
            zc = [0]

            def zbank():
                b = 1 + zc[0] % 2
                zc[0] += 1
                return b
            vc = 0
            for vb in range(3):
                w, wname = self.wunit(l, f"win{6 + vb}")
                w3 = w.rearrange("p (k f) -> p k f", k=8)
                for s4 in range(4):
                    bk = zbank()
                    pz = self.ps[bk]
                    for kc in range(KC):
                        sc.add("tensor", lambda e, pz=pz, kc=kc, s4=s4, w3=w3: e.matmul(
                            pz[:], lhsT=h[:, kc, s4 * 128:(s4 + 1) * 128], rhs=w3[:, kc, :],
                            start=(kc == 0), stop=(kc == KC - 1)),
                            reads=[wname, (("h", 0), kc)], writes=[("ps", bk)])
                    vs = vc % 2
                    vc += 1
                    sc.add("vector", lambda e, pz=pz, vs=vs, vb=vb: e.tensor_tensor(
                        out=vstg[vs][:, :, 0:64], in0=pz[:].rearrange("p (a b) -> p a b", a=8),
                        in1=bvb[:, vb * 512:(vb + 1) * 512].rearrange("p (a b) -> p a b", a=8), op=ALU.add),
                        reads=[("ps", bk), "bvb"], writes=[("vstg", vs)])
                    r0 = t * T + s4 * 128
                    dst = self.v_s[r0:r0 + 128, vb * 520:(vb + 1) * 520].rearrange("p (a b) -> p a b", a=8)
                    sc.add("sync", lambda e, vs=vs, dst=dst: e.dma_start(out=dst, in_=vstg[vs]),
                           reads=[("vstg", vs)], writes=["v_s"], dma_key=("vstg", vs))
            for gu in range(4):
                w, wname = self.wunit(l, f"win{13 + gu}")
                w3 = w.rearrange("p (k f) -> p k f", k=8)
                for q in range(4):
                    gc = gu * 4 + q
                    bk = zbank()
                    pz = self.ps[bk]
                    for kc in range(KC):
                        sc.add("tensor", lambda e, pz=pz, kc=kc, q=q, w3=w3: e.matmul(
                            pz[:], lhsT=w3[:, kc, q * 128:(q + 1) * 128], rhs=h[:, kc, :],
                            start=(kc == 0), stop=(kc == KC - 1)),
                            reads=[wname, (("h", 0), kc)], writes=[("ps", bk)])
                    si = stgn[0] % 4
                    stgn[0] += 1
                    bc = cbase + 72 + gc
                    sc.add("scalar", lambda e, pz=pz, si=si, bc=bc: e.activation(
                        out=stg[si], in_=pz[:], func=AF.Sigmoid, bias=self.cst[:, bc:bc + 1], scale=1.0),
                        reads=[("ps", bk), "cst"], writes=[("stg", si)])
                    store("g_s", si, self.g_s[gc * 128:(gc + 1) * 128, tok])
            for u in range(4):
                w, wname = self.wunit(l, f"win{9 + u}")
                w3 = w.rearrange("p (k f) -> p k f", k=8)
                for jj in range(2):
                    j = 2 * u + jj
                    ab = j % 2
                    pa = self.ps[1 + ab]
                    pb = self.ps[3 + ab]
                    for kc in range(KC):
                        sc.add("tensor", lambda e, pa=pa, kc=kc, jj=jj, w3=w3: e.matmul(
                            pa[:], lhsT=w3[:, kc, jj * 128:(jj + 1) * 128], rhs=h[:, kc, :],
                            start=(kc == 0), stop=(kc == KC - 1)),
                            reads=[wname, (("h", 0), kc)], writes=[("ps", 1 + ab)])
                    for kc in range(KC):
                        sc.add("tensor", lambda e, pb=pb, kc=kc, jj=jj, w3=w3: e.matmul(
                            pb[:], lhsT=w3[:, kc, 256 + jj * 128:256 + (jj + 1) * 128], rhs=h[:, kc, :],
                            start=(kc == 0), stop=(kc == KC - 1)),
                            reads=[wname, (("h", 0), kc)], writes=[("ps", 3 + ab)])
                    bca = cbase + 56 + j
                    bcg = cbase + 64 + j
                    sc.add("scalar", lambda e, pb=pb, ab=ab, bcg=bcg: e.activation(
                        out=t1[ab], in_=pb[:], func=AF.Sigmoid, bias=self.cst[:, bcg:bcg + 1], scale=1.0),
                        reads=[("ps", 3 + ab), "cst"], writes=[("t1", ab)])
                    si = stgn[0] % 4
                    stgn[0] += 1
                    sc.add("vector", lambda e, pa=pa, ab=ab, si=si, bca=bca: e.scalar_tensor_tensor(
                        out=stg[si], in0=pa[:], scalar=self.cst[:, bca:bca + 1], in1=t1[ab],
                        op0=ALU.add, op1=ALU.mult),
                        reads=[("ps", 1 + ab), ("t1", ab), "cst"], writes=[("stg", si)])
                    store("uc_s", si, self.uc_s[j * 128:(j + 1) * 128, tok])
            zc[0] = 0
            cc = 0
            for qu in range(6):
                w, wname = self.wunit(l, f"win{qu}")
                w3 = w.rearrange("p (k f) -> p k f", k=8)
                for q in range(4):
                    c = qu * 4 + q
                    i2 = cc % 2
                    cc += 1
                    pz = self.ps[1 + i2]
                    pm = self.ps[3 + i2]
                    pr = self.ps[5 + i2]
                    for kc in range(KC):
                        sc.add("tensor", lambda e, pz=pz, kc=kc, q=q, w3=w3: e.matmul(
                            pz[:], lhsT=w3[:, kc, q * 128:(q + 1) * 128], rhs=h[:, kc, :],
                            start=(kc == 0), stop=(kc == KC - 1)),
                            reads=[wname, (("h", 0), kc)], writes=[("ps", 1 + i2)])
                    bc = cbase + 32 + c
                    gcl = cbase + 88 + c
                    bgc = l * 24 + c
                    sc.add("scalar", lambda e, pz=pz, i2=i2, bc=bc: e.activation(
                        out=sqc[i2], in_=pz[:], func=AF.Square, bias=self.cst[:, bc:bc + 1], scale=1.0),
                        reads=[("ps", 1 + i2), "cst"], writes=[("sqc", i2)])
                    sc.add("scalar", lambda e, pz=pz, i2=i2, gcl=gcl, bgc=bgc: e.activation(
                        out=uu[i2], in_=pz[:], func=AF.Identity, bias=self.bgt[:, bgc:bgc + 1],
                        scale=self.cst[:, gcl:gcl + 1]),
                        reads=[("ps", 1 + i2), "cst", "bgt"], writes=[("uu", i2)])
                    sc.add("tensor", lambda e, pm=pm, i2=i2: e.matmul(pm[:], lhsT=self.blk, rhs=sqc[i2], start=True, stop=True),
                           reads=[("sqc", i2), "blk"], writes=[("ps", 3 + i2)])
                    sc.add("tensor", lambda e, pr=pr, i2=i2: e.matmul(pr[:], lhsT=self.Rbf, rhs=uu[i2], start=True, stop=True),
                           reads=[("uu", i2), "Rbf"], writes=[("ps", 5 + i2)])
                    self.emit_rstd(rsc[i2], pm[:], ("ps", 3 + i2), ("rsc", i2))
                    sc.add("gpsimd", lambda e, i2=i2: e.tensor_tensor(out=t1[i2], in0=uu[i2], in1=cs[0], op=ALU.mult),
                           reads=[("uu", i2), ("cs", 0)], writes=[("t1", i2)])
                    sc.add("vector", lambda e, pr=pr, i2=i2: e.tensor_tensor(out=t2[i2], in0=pr[:], in1=cs[1], op=ALU.mult),
                           reads=[("ps", 5 + i2), ("cs", 1)], writes=[("t2", i2)])
                    sc.add("gpsimd", lambda e, i2=i2: e.tensor_tensor(out=t1[i2], in0=t1[i2], in1=t2[i2], op=ALU.add),
                           reads=[("t1", i2), ("t2", i2)], writes=[("t1", i2)])
                    si = stgn[0] % 4
                    stgn[0] += 1
                    sc.add("vector", lambda e, i2=i2, si=si: e.tensor_tensor(out=stg[si], in0=t1[i2], in1=rsc[i2], op=ALU.mult),
                           reads=[("t1", i2), ("rsc", i2)], writes=[("stg", si)])
                    if c < 12:
                        store("qT_s", si, self.qT_s[c * 128:(c + 1) * 128, tok])
                    else:
                        store("kT_s", si, self.kT_s[(c - 12) * 128:(c - 11) * 128, tok])
        sc.fence()


    def phase_m2(self, l):
        sc = self.sc
        o = 0
        qT = [self.carve(o + i * 8192, [128, S], BF16) for i in range(2)]; o += 16384
        kT = [self.carve(o + i * 8192, [128, S], BF16) for i in range(2)]; o += 16384
        vt = [self.carve(o + i * 2432, [128, 9, 130], BF16) for i in range(2)]; o += 4864
        pT = [self.carve(o + i * 1024, [128, 512], BF16) for i in range(2)]; o += 2048
        nds = [self.carve(o + i * 2080, [128, 4, 130], F32) for i in range(2)]; o += 4160
        assert o <= self.arena_bytes, o
        it = 0
        vcn = 0
        blkn = 0
        ndn = 0
        for g, (window, dil) in enumerate(GROUPS):
            L = S // dil
            nblk = L // 128
            if g not in self.cfg.get("m2_groups", (0, 1, 2)):
                continue
            for hp in range(4):
                cq = g * 4 + hp
                for hh in range(2):
                    r0 = cq * 128 + hh * 64
                    sc.add("sync", lambda e, hh=hh, r0=r0: e.dma_start(out=qT[hh][0:64, :], in_=self.qT_s[r0:r0 + 64, :]),
                           reads=["qT_s"], writes=[("qT", hh)], dma_key=("qT", hh))
                    sc.add("sync", lambda e, hh=hh, r0=r0: e.dma_start(out=kT[hh][0:64, :], in_=self.kT_s[r0:r0 + 64, :]),
                           reads=["kT_s"], writes=[("kT", hh)], dma_key=("kT", hh))
                col0 = (g * 8 + hp * 2) * 65
                for r in range(dil):
                    for b0 in range(0, nblk, 8):
                        b1 = min(b0 + 8, nblk)
                        vs = vcn % 2
                        vcn += 1
                        def kbase(j):
                            if j == 0:
                                return 0
                            if j == nblk:
                                return L - 128
                            return j * 128 - 64
                        segs = []
                        jlo, jhi = b0, b1
                        if jlo == 0:
                            segs.append((0, 0))
                            jlo = 1
                        last_special = (jhi == nblk)
                        if last_special:
                            jhi = nblk - 1
                        while jlo <= jhi:
                            je = min(jlo + 3, jhi)
                            segs.append((jlo, je))
                            jlo = je + 1
                        if last_special:
                            segs.append((nblk, nblk))
                        for k_, (ja, jb) in enumerate(segs):
                            nt_ = jb - ja + 1
                            base = kbase(ja) * dil + r
                            src = self.v_s[_ss(base, 128 * nt_, dil), col0:col0 + 130].rearrange("(j i) c -> i j c", i=128)
                            dst = vt[vs][:, ja - b0:ja - b0 + nt_, :]
                            sc.add("sync", lambda e, src=src, dst=dst: e.dma_start(out=dst, in_=src),
                                   reads=["v_s"], writes=[("vt", vs, k_)], dma_key=("vt", vs, k_))
                        vnames = [("vt", vs, k_) for k_ in range(len(segs))]
                        lvl = self.cfg.get("m2_lvl", 4)
                        for nb in range(b0, b1):
                            if lvl < 1:
                                break
                            bi = blkn % 2
                            blkn += 1
                            pS = self.ps[1 + bi]
                            pO = self.ps[3 + bi]
                            q0 = nb * 128 * dil + r
                            for hh in range(2):
                                for tl in range(2):
                                    kb = kbase(nb + tl) * dil + r
                                    sc.add("tensor", lambda e, pS=pS, hh=hh, tl=tl, kb=kb, q0=q0, dil=dil: e.matmul(
                                        pS[:, (hh * 2 + tl) * 128:(hh * 2 + tl + 1) * 128],
                                        lhsT=kT[hh][0:64, _ss(kb, 128, dil)],
                                        rhs=qT[hh][0:64, _ss(q0, 128, dil)],
                                        start=True, stop=True),
                                        reads=[("qT", hh), ("kT", hh)], writes=[("ps", 1 + bi)])
                            if lvl < 2:
                                continue
                            sc.add("scalar", lambda e, pS=pS, bi=bi: e.activation(out=pT[bi], in_=pS[:], func=AF.Exp, scale=0.125),
                                   reads=[("ps", 1 + bi)], writes=[("pT", bi)])
                            mv = 1 if nb == 0 else (2 if nb == nblk - 1 else 0)
                            meng = "gpsimd" if (blkn % 2 == 0) else "vector"
                            sc.add(meng, lambda e, bi=bi, mv=mv: e.tensor_tensor(out=pT[bi], in0=pT[bi], in1=self.maskbf[mv], op=ALU.mult),
                                   reads=[("pT", bi), "maskbf"], writes=[("pT", bi)])
                            if lvl < 3:
                                continue
                            for hh in range(2):
                                for tl in range(2):
                                    jl = nb + tl - b0
                                    sc.add("tensor", lambda e, pO=pO, hh=hh, tl=tl, jl=jl, bi=bi, vs=vs: e.matmul(
                                        pO[:, hh * 65:(hh + 1) * 65],
                                        lhsT=pT[bi][:, (hh * 2 + tl) * 128:(hh * 2 + tl + 1) * 128],
                                        rhs=vt[vs][:, jl, hh * 65:(hh + 1) * 65],
                                        start=(tl == 0), stop=(tl == 1)),
                                        reads=[("pT", bi)] + vnames, writes=[("ps", 3 + bi)])
                            ns = ndn % 2
                            sub = (nb - b0) % 4
                            if (nb - b0) % 2 == 0:
                                sc.add("scalar", lambda e, pO=pO, ns=ns, sub=sub: e.copy(out=nds[ns][:, sub, :], in_=pO[:, 0:130]),
                                       reads=[("ps", 3 + bi)], writes=[("nds", ns, sub)])
                            else:
                                sc.add("vector", lambda e, pO=pO, ns=ns, sub=sub: e.tensor_copy(out=nds[ns][:, sub, :], in_=pO[:, 0:130]),
                                       reads=[("ps", 3 + bi)], writes=[("nds", ns, sub)])
                            if (sub == 3 or nb == b1 - 1) and lvl >= 4:
                                nbs = nb - sub
                                nsub = sub + 1
                                rbase = nbs * 128 * dil + r
                                dst = self.nd_s[g, _ss(rbase, 128 * nsub, dil), hp * 130:(hp + 1) * 130].rearrange(
                                    "(j i) c -> i j c", i=128)
                                sc.add("sync", lambda e, ns=ns, nsub=nsub, dst=dst: e.dma_start(out=dst, in_=nds[ns][:, 0:nsub, :]),
                                       reads=[("nds", ns, q_) for q_ in range(nsub)], writes=["nd_s"], dma_key=("nds", ns))
                                ndn += 1
        sc.fence()

    def phase_m3(self, l):
        sc = self.sc
        cbase = C_L + l * CL_N
        T3 = 256
        o = 0
        ucs = self.carve(o, [128, 8, T3 + 30], BF16); o += 4608
        gts = self.carve(o, [128, 8, T3], BF16); o += 4096
        cv = self.carve(o, [128, 8, T3], F32); o += 8192
        cvb = [self.carve(o + i * 512, [128, T3], BF16) for i in range(2)]; o += 1024
        sqv = [self.carve(o + i * 512, [128, T3], BF16) for i in range(2)]; o += 1024
        mu = self.carve(o, [128, T3], F32); o += 1024
        rs = self.carve(o, [128, T3], F32); o += 1024
        nmr = self.carve(o, [128, T3], F32); o += 1024
        tmpv = self.carve(o, [128, T3], F32); o += 1024
        tn = [self.carve(o + i * 1024, [128, T3], F32) for i in range(2)]; o += 2048
        sact = self.carve(o, [128, 8, T3], BF16); o += 4096
        merged = self.carve(o, [128, 8, T3], BF16); o += 4096
        nd = [self.carve(o + i * 2080, [128, 8, 65], F32) for i in range(2)]; o += 4160
        rec = self.carve(o, [128, 8], F32); o += 64
        ya = self.carve(o, [128, 512], BF16); o += 1024
        yaT = self.carve(o, [128, 4, T3], BF16); o += 2048
        tmpm = [self.carve(o + i * 1024, [128, T3], F32) for i in range(2)]; o += 2048
        assert o <= self.arena_bytes, o
        psT = self.ps[7][:].bitcast(BF16)
        for t in range(S // T3):
            t0 = t * T3
            tok = slice(t0, t0 + T3)
            lo = t0 - 15
            hi = t0 + T3 + 15
            c0 = 0
            c1 = T3 + 30
            if lo < 0:
                sc.add("gpsimd", lambda e: e.memset(ucs[:, :, 0:15], 0.0), writes=["ucs"])
                c0 = 15
                lo = 0
            if hi > S:
                sc.add("gpsimd", lambda e: e.memset(ucs[:, :, T3 + 15:T3 + 30], 0.0), writes=["ucs"])
                c1 = T3 + 15
                hi = S
            for hf in range(2):
                src = self.uc_s[hf * 512:(hf + 1) * 512, lo:hi].rearrange("(c p) t -> p c t", p=128)
                sc.add("sync", lambda e, src=src, c0=c0, c1=c1, hf=hf: e.dma_start(out=ucs[:, hf * 4:hf * 4 + 4, c0:c1], in_=src),
                       reads=["uc_s"], writes=["ucs"], dma_key=("ucs", hf))
                gsrc = self.g_s[1024 + hf * 512:1024 + (hf + 1) * 512, tok].rearrange("(c p) t -> p c t", p=128)
                sc.add("sync", lambda e, gsrc=gsrc, hf=hf: e.dma_start(out=gts[:, hf * 4:hf * 4 + 4, :], in_=gsrc),
                       reads=["g_s"], writes=["gts"], dma_key=("gts", hf))
            for c in range(8):
                w, wname = self.wunit(l, f"cdg{c}")
                w3 = w.rearrange("p (k f) -> p k f", k=31)
                pc = self.ps[1 + c % 2]
                for k in range(31):
                    sc.add("tensor", lambda e, pc=pc, k=k, c=c, w3=w3: e.matmul(
                        pc[:, 0:T3], lhsT=w3[:, k, :], rhs=ucs[:, c, k:k + T3], start=(k == 0), stop=(k == 30)),
                        reads=[wname, "ucs"], writes=[("ps", 1 + c % 2)])
                bc = cbase + 112 + c
                i2 = c % 2
                sc.add("scalar", lambda e, pc=pc, c=c, bc=bc: e.activation(
                    out=cv[:, c, :], in_=pc[:, 0:T3], func=AF.Identity, bias=self.cst[:, bc:bc + 1], scale=1.0),
                    reads=[("ps", 1 + i2), "cst"], writes=[("cv", c)])
                sc.add("scalar", lambda e, pc=pc, i2=i2, bc=bc: e.activation(
                    out=sqv[i2], in_=pc[:, 0:T3], func=AF.Square, bias=self.cst[:, bc:bc + 1], scale=1.0),
                    reads=[("ps", 1 + i2), "cst"], writes=[("sqv", i2)])
                sc.add("vector", lambda e, c=c, i2=i2: e.tensor_copy(out=cvb[i2], in_=cv[:, c, :]),
                       reads=[("cv", c)], writes=[("cvb", i2)])
                sc.add("tensor", lambda e, c=c, i2=i2: e.matmul(self.ps[5][:, 0:T3], lhsT=self.onesD, rhs=cvb[i2],
                                                               start=(c == 0), stop=(c == 7)),
                       reads=[("cvb", i2), "onesD"], writes=[("ps", 5)])
                sc.add("tensor", lambda e, c=c, i2=i2: e.matmul(self.ps[6][:, 0:T3], lhsT=self.onesD, rhs=sqv[i2],
                                                               start=(c == 0), stop=(c == 7)),
                       reads=[("sqv", i2), "onesD"], writes=[("ps", 6)])
            sc.add("vector", lambda e: e.tensor_copy(out=mu, in_=self.ps[5][:, 0:T3]), reads=[("ps", 5)], writes=["mu"])
            sc.add("vector", lambda e: e.tensor_tensor(out=tmpv, in0=mu, in1=mu, op=ALU.mult), reads=["mu"], writes=["tmpv"])
            sc.add("vector", lambda e: e.tensor_tensor(out=tmpv, in0=self.ps[6][:, 0:T3], in1=tmpv, op=ALU.subtract),
                   reads=[("ps", 6), "tmpv"], writes=["tmpv"])
            self.emit_rstd(rs, tmpv, "tmpv", "rs")
            sc.add("vector", lambda e: e.scalar_tensor_tensor(out=nmr, in0=mu, scalar=-1.0, in1=rs, op0=ALU.mult, op1=ALU.mult),
                   reads=["mu", "rs"], writes=["nmr"])
            for c in range(8):
                i2 = c % 2
                sc.add("gpsimd", lambda e, c=c, i2=i2: e.tensor_tensor(out=tn[i2], in0=cv[:, c, :], in1=rs, op=ALU.mult),
                       reads=[("cv", c), "rs"], writes=[("tn", i2)])
                sc.add("vector", lambda e, i2=i2: e.tensor_tensor(out=tn[i2], in0=tn[i2], in1=nmr, op=ALU.add),
                       reads=[("tn", i2), "nmr"], writes=[("tn", i2)])
                gc_ = cbase + 120 + c
                bc_ = cbase + 128 + c
                sc.add("scalar", lambda e, c=c, i2=i2, gc_=gc_, bc_=bc_: e.activation(
                    out=sact[:, c, :], in_=tn[i2], func=AF.Silu, bias=self.cst[:, bc_:bc_ + 1], scale=self.cst[:, gc_:gc_ + 1]),
                    reads=[("tn", i2), "cst"], writes=[("sact", c)])
            pcnt = 0
            for u in range(2):
                w, wname = self.wunit(l, f"cp{u}")
                w3 = w.rearrange("p (k f) -> p k f", k=8)
                for q in range(4):
                    oc = u * 4 + q
                    bk = 3 + pcnt % 2
                    pcnt += 1
                    pp = self.ps[bk]
                    for kc in range(8):
                        sc.add("tensor", lambda e, pp=pp, kc=kc, q=q, w3=w3: e.matmul(
                            pp[:, 0:T3], lhsT=w3[:, kc, q * 128:(q + 1) * 128], rhs=sact[:, kc, :],
                            start=(kc == 0), stop=(kc == 7)),
                            reads=[wname, ("sact", kc)], writes=[("ps", bk)])
                    bcp = cbase + 136 + oc
                    sc.add("vector", lambda e, pp=pp, oc=oc, bcp=bcp: e.scalar_tensor_tensor(
                        out=merged[:, oc, :], in0=pp[:, 0:T3], scalar=self.cst[:, bcp:bcp + 1], in1=gts[:, oc, :],
                        op0=ALU.add, op1=ALU.mult),
                        reads=[("ps", bk), "gts", "cst"], writes=[("merged", oc)])
            for s2 in range(2):
                tt = t0 + s2 * 128
                for g in range(3):
                    di = 0 if g == 0 else 1
                    sc.add("sync", lambda e, g=g, di=di, tt=tt: e.dma_start(
                        out=nd[di], in_=self.nd_s[g, tt:tt + 128, :].rearrange("p (a b) -> p a b", a=8)),
                        reads=["nd_s"], writes=[("nd", di)], dma_key=("nd", di))
                    if g > 0:
                        sc.add("gpsimd", lambda e: e.tensor_tensor(out=nd[0], in0=nd[0], in1=nd[1], op=ALU.add),
                               reads=[("nd", 0), ("nd", 1)], writes=[("nd", 0)])
                sc.add("vector", lambda e: e.reciprocal(out=rec, in_=nd[0][:, :, 64]), reads=[("nd", 0)], writes=["rec"])
                sc.add("vector", lambda e: e.tensor_tensor(
                    out=ya.rearrange("p (a b) -> p a b", a=8), in0=nd[0][:, :, 0:64],
                    in1=rec.unsqueeze(2).to_broadcast([128, 8, 64]), op=ALU.mult),
                    reads=[("nd", 0), "rec"], writes=["ya"])
                for fc in range(4):
                    sc.add("tensor", lambda e, fc=fc: e.transpose(psT[:, fc * 128:(fc + 1) * 128], ya[:, fc * 128:(fc + 1) * 128], self.identbf),
                           reads=["ya", "identbf"], writes=[("ps", 7)])
                sc.add("scalar", lambda e, s2=s2: e.copy(out=yaT[:, :, s2 * 128:(s2 + 1) * 128],
                                                        in_=psT[:, 0:512].rearrange("p (a b) -> p a b", a=4)),
                       reads=[("ps", 7)], writes=[("yaT", s2)])
            for hf in range(2):
                gsrc2 = self.g_s[hf * 512:(hf + 1) * 512, tok].rearrange("(c p) t -> p c t", p=128)
                sc.add("sync", lambda e, gsrc2=gsrc2, hf=hf: e.dma_start(out=gts[:, hf * 4:hf * 4 + 4, :], in_=gsrc2),
                       reads=["g_s"], writes=["gts"], dma_key=("gts", hf))
            w, wname = self.wunit(l, "ap")
            w3 = w.rearrange("p (k f) -> p k f", k=4)
            for oc in range(8):
                bk = 3 + pcnt % 2
                pcnt += 1
                pp = self.ps[bk]
                for kc in range(4):
                    sc.add("tensor", lambda e, pp=pp, kc=kc, oc=oc, w3=w3: e.matmul(
                        pp[:, 0:T3], lhsT=w3[:, kc, oc * 128:(oc + 1) * 128], rhs=yaT[:, kc, :],
                        start=(kc == 0), stop=(kc == 3)),
                        reads=[wname, ("yaT", 0), ("yaT", 1)], writes=[("ps", bk)])
                i2 = oc % 2
                sc.add("vector", lambda e, pp=pp, oc=oc, i2=i2: e.tensor_tensor(out=tmpm[i2], in0=pp[:, 0:T3], in1=gts[:, oc, :], op=ALU.mult),
                       reads=[("ps", bk), "gts"], writes=[("tmpm", i2)])
                sc.add("gpsimd", lambda e, oc=oc, i2=i2: e.tensor_tensor(out=merged[:, oc, :], in0=tmpm[i2], in1=merged[:, oc, :], op=ALU.add),
                       reads=[("tmpm", i2), ("merged", oc)], writes=[("merged", oc)])
            for u in range(2):
                w, wname = self.wunit(l, f"wo{u}")
                w3 = w.rearrange("p (k f) -> p k f", k=8)
                for q in range(4):
                    oc = u * 4 + q
                    bk = 3 + pcnt % 2
                    pcnt += 1
                    pp = self.ps[bk]
                    for kc in range(8):
                        sc.add("tensor", lambda e, pp=pp, kc=kc, q=q, w3=w3: e.matmul(
                            pp[:, 0:T3], lhsT=w3[:, kc, q * 128:(q + 1) * 128], rhs=merged[:, kc, :],
                            start=(kc == 0), stop=(kc == 7)),
                            reads=[wname] + [("merged", k_) for k_ in range(8)], writes=[("ps", bk)])
                    sc.add("vector", lambda e, pp=pp, oc=oc, tok=tok: e.tensor_tensor(
                        out=self.xT[:, oc, tok], in0=pp[:, 0:T3], in1=self.xT[:, oc, tok], op=ALU.add),
                        reads=[("ps", bk), ("x", oc, t // 2)], writes=[("x", oc, t // 2)])
        sc.fence()

    def phase_final_norm(self, l, last):
        sc = self.sc
        sq = [self.carve(0, [128, T], BF16), self.carve(1024, [128, T], BF16)]
        rstd = self.carve(2048, [128, T], F32)
        ostg = [self.carve(4096 + i * 4096, [128, 1024], F32) for i in range(2)]
        cb = C_L + l * CL_N + 24
        for t in range(NT):
            tok = slice(t * T, (t + 1) * T)
            for kc in range(KC):
                s2 = kc % 2
                sc.add("scalar", lambda e, kc=kc, s2=s2, tok=tok: e.activation(out=sq[s2], in_=self.xT[:, kc, tok], func=AF.Square),
                       reads=[("x", kc, t)], writes=[("sq", s2)])
                sc.add("tensor", lambda e, kc=kc, s2=s2: e.matmul(self.ps[0][:], lhsT=self.onesD, rhs=sq[s2],
                                                                  start=(kc == 0), stop=(kc == KC - 1)),
                       reads=[("sq", s2), "onesD"], writes=[("ps", 0)])
            self.emit_rstd(rstd, self.ps[0][:], ("ps", 0), "rstd")
            for kc in range(KC):
                sc.add("vector", lambda e, kc=kc, tok=tok: e.scalar_tensor_tensor(
                    out=self.xT[:, kc, tok], in0=self.xT[:, kc, tok], scalar=self.cst[:, cb + kc:cb + kc + 1],
                    in1=rstd, op0=ALU.mult, op1=ALU.mult),
                    reads=[("x", kc, t), "rstd", "cst"], writes=[("x", kc, t)])
            if not last:
                continue
            for s4 in range(4):
                i = t * 4 + s4
                osl = i % 2
                for hb in range(2):
                    bank = 1 + (2 * i + hb) % 4
                    pst = self.ps[bank]
                    for q in range(4):
                        kc = hb * 4 + q
                        sc.add("tensor", lambda e, pst=pst, q=q, kc=kc, i=i: e.transpose(
                            pst[:, q * 128:(q + 1) * 128], self.xT[:, kc, i * 128:(i + 1) * 128], self.ident),
                            reads=[("x", kc, t), "cst"], writes=[("ps", bank)])
                    dst = ostg[osl][:, hb * 512:(hb + 1) * 512]
                    if hb == 0:
                        fn = lambda e, dst=dst, pst=pst: e.copy(out=dst, in_=pst[:])
                        eng = "scalar"
                    else:
                        fn = lambda e, dst=dst, pst=pst: e.tensor_copy(out=dst, in_=pst[:])
                        eng = "vector"
                    sc.add(eng, fn, reads=[("ps", bank)], writes=[("ostg", osl, hb)])
                sc.add("sync", lambda e, osl=osl, i=i: e.dma_start(out=self.y_out[i * 128:(i + 1) * 128, :], in_=ostg[osl]),
                       reads=[("ostg", osl, 0), ("ostg", osl, 1)], writes=[("yout", i)], dma_key=("ostg", osl))
        sc.fence()


_CACHE = {}


def _get_prog(cfg_key, cfg):
    if cfg_key not in _CACHE:
        p = Prog(cfg)
        p.build()
        _CACHE[cfg_key] = p
    return _CACHE[cfg_key]


def kernel(**inputs):
    cfg = inputs.pop("_cfg", None) or {}
    ncores = cfg.get("ncores", NCORES)
    inp = {k: np.asarray(v) for k, v in inputs.items()}
    wpack, _ = _pack_weights(inp)
    cpack = _pack_consts(inp)
    cmask = _pack_masks()
    bvh = np.ascontiguousarray(np.asarray(inp["b_in"], np.float32)[:, 3072:4608])
    cosT, sinT = _rope_tables()
    prog = _get_prog(tuple(sorted(cfg.items())), cfg)
    assert wpack.size == prog.wtotal
    x = np.asarray(inp["x"], np.float32)
    in_maps = []
    for b in range(ncores):
        in_maps.append({"x": np.ascontiguousarray(x[b]), "wpack": wpack, "cpack": cpack, "cmask": cmask,
                        "bv": bvh, "cosT": cosT, "sinT": sinT})
    if cfg.get("trace"):
        res = run_bass_kernel_spmd(prog.nc, in_maps, core_ids=list(range(ncores)), trace=True)
        print("EXEC_TIME_NS", res.exec_time_ns)
    else:
        res = run_bass_kernel_spmd(prog.nc, in_maps, core_ids=list(range(ncores)))
    if cfg.get("dbg"):
        global _DBG
        _DBG = res.results
    out = np.stack([np.asarray(r["y"], np.float32).reshape(S, D) for r in res.results], axis=0)
    return out
```

```python
import contextlib
import numpy as np
import concourse.bass as bass
import concourse.mybir as mybir
from concourse.bass_utils import run_bass_kernel_spmd

F32 = mybir.dt.float32
BF16 = mybir.dt.bfloat16
AF = mybir.ActivationFunctionType
ALU = mybir.AluOpType

D = 1024
S = 4096
DFF = 2816
NL = 2
T = 512
NT = S // T
KC = 8
NJ = DFF // 128
EPS = 1e-6
INW = 8704
GROUPS = ((128, 1), (512, 4), (2048, 16))
NCORES = 8

SEM_CAP = 12000


def _ss(base, n, step):
    return slice(base, base + (n - 1) * step + 1, step)


class Op:
    __slots__ = ("eng", "fn", "reads", "writes", "dma_key", "waits", "sig", "persist")

    def __init__(self, eng, fn, reads, writes, dma_key, persist):
        self.eng = eng
        self.fn = fn
        self.reads = reads
        self.writes = writes
        self.dma_key = dma_key
        self.waits = []
        self.sig = None
        self.persist = persist


class Sched:
    ENGS = ("sync", "scalar", "vector", "gpsimd", "tensor")

    def __init__(self):
        self.ops = []
        self.inserts = []

    def add(self, eng, fn, reads=(), writes=(), dma_key=None, persist=False):
        op = Op(eng, fn, tuple(reads), tuple(writes), dma_key, persist)
        self.ops.append(op)
        return op

    def add_at(self, pos, eng, fn, reads=(), writes=(), dma_key=None, persist=False):
        op = Op(eng, fn, tuple(reads), tuple(writes), dma_key, persist)
        self.inserts.append((pos, len(self.inserts), op))
        return op

    def fence(self):
        self.ops.append("FENCE")

    def pos(self):
        return len(self.ops)

    def finalize(self):
        ins = sorted(self.inserts, key=lambda z: (z[0], z[1]))
        out = []
        k = 0
        for i, op in enumerate(self.ops):
            while k < len(ins) and ins[k][0] <= i:
                out.append(ins[k][2])
                k += 1
            out.append(op)
        while k < len(ins):
            out.append(ins[k][2])
            k += 1
        self.ops = out
        self.inserts = []

    def emit(self, nc, stack):
        self.finalize()
        ops = self.ops
        last_w = {}
        readers = {}
        last_dma_on_key = {}
        last_on_eng = {}
        deps_of = []
        fence_set = {}
        fence_seen = {}
        fence_gen = 0
        dma_since_fence = {}
        real = []

        def skey(o):
            return ("dma", o.dma_key) if o.dma_key is not None else o.eng

        for op in ops:
            if op == "FENCE":
                fence_gen += 1
                fence_set = {}
                for e_, j in last_on_eng.items():
                    fence_set[e_] = j
                for k_, j in dma_since_fence.items():
                    fence_set[("dma", k_)] = j
                dma_since_fence = {}
                continue
            i = len(real)
            real.append(op)
            d = {}

            def addd(j):
                k = skey(real[j])
                if d.get(k, -1) < j:
                    d[k] = j

            for b in op.reads:
                if b in last_w:
                    addd(last_w[b])
            for b in op.writes:
                if b in last_w:
                    addd(last_w[b])
                rb = readers.get(b)
                if rb:
                    for j in rb.values():
                        addd(j)
            if op.dma_key is not None and op.dma_key in last_dma_on_key:
                addd(last_dma_on_key[op.dma_key])
            if not op.persist and fence_seen.get(op.eng, 0) != fence_gen:
                for j in fence_set.values():
                    addd(j)
                fence_seen[op.eng] = fence_gen
            if op.dma_key is None and op.eng == "tensor":
                d.pop("tensor", None)
            deps_of.append(set(d.values()))
            me = skey(op)
            for b in op.reads:
                readers.setdefault(b, {})[me] = i
            for b in op.writes:
                last_w[b] = i
                readers[b] = {}
            if op.dma_key is not None:
                last_dma_on_key[op.dma_key] = i
                if not op.persist:
                    dma_since_fence[op.dma_key] = i
            else:
                last_on_eng[op.eng] = i
        needs_sig = [False] * len(real)
        for i, op in enumerate(real):
            for j in deps_of[i]:
                needs_sig[j] = True
        eng_sem = {}
        eng_cnt = {}
        dma_sem = {}
        dma_cnt = {}
        nsem = [0]

        def new_sem(tag):
            nsem[0] += 1
            return stack.enter_context(nc.semaphore(f"{tag}{nsem[0]}"))

        for i, op in enumerate(real):
            if op.dma_key is not None:
                if op.dma_key not in dma_sem:
                    dma_sem[op.dma_key] = new_sem("d")
                    dma_cnt[op.dma_key] = 0
                dma_cnt[op.dma_key] += 16
                if dma_cnt[op.dma_key] > SEM_CAP * 4:
                    dma_sem[op.dma_key] = new_sem("d")
                    dma_cnt[op.dma_key] = 16
                op.sig = (dma_sem[op.dma_key], dma_cnt[op.dma_key])
            elif needs_sig[i]:
                e = op.eng
                if e not in eng_sem or eng_cnt[e] >= SEM_CAP:
                    eng_sem[e] = new_sem(e[0])
                    eng_cnt[e] = 0
                eng_cnt[e] += 1
                op.sig = (eng_sem[e], eng_cnt[e])
        waited = {e: {} for e in self.ENGS}
        nwaits = 0
        for i, op in enumerate(real):
            w = waited[op.eng]
            best = {}
            for j in deps_of[i]:
                sem, cnt = real[j].sig
                k = id(sem)
                if w.get(k, 0) >= cnt:
                    continue
                if k not in best or best[k][1] < cnt:
                    best[k] = (sem, cnt)
            for k, (sem, cnt) in best.items():
                w[k] = cnt
                op.waits.append((sem, cnt))
                nwaits += 1
        self.stats = dict(n_ops=len(real), n_sems=nsem[0], n_waits=nwaits,
                          n_sig=sum(1 for o in real if o.sig is not None))
        per_eng = {e: [o for o in real if o.eng == e] for e in self.ENGS}
        tail = []
        for e in self.ENGS:
            if e in eng_sem:
                pass
        last_sigs = {}
        for o in real:
            if o.sig is not None:
                last_sigs[id(o.sig[0])] = o.sig
        with nc.Block() as block:
            def runner(name):
                def body(e):
                    for o in per_eng[name]:
                        for sem, cnt in o.waits:
                            e.wait_ge(sem, cnt)
                        ins = o.fn(e)
                        if o.sig is not None:
                            if o.dma_key is not None:
                                ins.then_inc(o.sig[0], 16)
                            else:
                                ins.then_inc(o.sig[0], 1)
                    if name == "sync":
                        for sem, cnt in last_sigs.values():
                            if waited["sync"].get(id(sem), 0) < cnt:
                                e.wait_ge(sem, cnt)
                return body
            block.sync(runner("sync"))
            block.scalar(runner("scalar"))
            block.vector(runner("vector"))
            block.gpsimd(runner("gpsimd"))
            block.tensor(runner("tensor"))


def _unit_list():
    units = []
    for w in ("f1", "f2"):
        pass
    def ffn(tag):
        u = []
        for i in range(NJ // 2):
            u.append((f"{tag}up{i}", 8 * 512))
        for oc in range(8):
            u.append((f"{tag}dn{oc}", NJ * 128))
        return u
    m1 = [(f"win{i}", 8 * 512) for i in (6, 7, 8, 13, 14, 15, 16, 9, 10, 11, 12, 0, 1, 2, 3, 4, 5)]
    m3 = [(f"cdg{c}", 31 * 128) for c in range(8)]
    m3 += [("cp0", 8 * 512), ("cp1", 8 * 512), ("ap", 4 * 1024), ("wo0", 8 * 512), ("wo1", 8 * 512)]
    return ffn("f1"), m1, m3, ffn("f2")


def _pack_weights(inp):
    offs, total = _weight_offsets()
    out = np.empty(total, np.float32)

    def put(key, arr):
        arr = np.ascontiguousarray(arr, dtype=np.float32)
        assert arr.shape[0] == 128
        off, n = offs[key]
        assert arr.size == 128 * n, (key, arr.shape, n)
        out[off:off + 128 * n] = arr.reshape(-1)

    for l in range(NL):
        for tag, wu, wd in (("f1", inp["ffn1_w_up"][l], inp["ffn1_w_down"][l]),
                            ("f2", inp["ffn2_w_up"][l], inp["ffn2_w_down"][l])):
            wuk = np.asarray(wu, np.float32).reshape(8, 128, 2 * DFF)
            for i in range(NJ // 2):
                j0, j1 = 2 * i, 2 * i + 1
                c = np.concatenate([np.arange(j0 * 128, (j0 + 1) * 128),
                                    np.arange(DFF + j0 * 128, DFF + (j0 + 1) * 128),
                                    np.arange(j1 * 128, (j1 + 1) * 128),
                                    np.arange(DFF + j1 * 128, DFF + (j1 + 1) * 128)])
                put((l, f"{tag}up{i}"), wuk[:, :, c].transpose(1, 0, 2))
            wdk = np.asarray(wd, np.float32).reshape(NJ, 128, D)
            for oc in range(8):
                put((l, f"{tag}dn{oc}"), wdk[:, :, oc * 128:(oc + 1) * 128].transpose(1, 0, 2))
        wk = np.asarray(inp["w_in"][l], np.float32).reshape(8, 128, INW)
        cols = []
        for i in range(9):
            cols.append(np.arange(i * 512, (i + 1) * 512))
        ga = 4608
        gg = 4608 + 1024
        for u in range(4):
            cols.append(np.concatenate([np.arange(ga + u * 256, ga + (u + 1) * 256),
                                        np.arange(gg + u * 256, gg + (u + 1) * 256)]))
        for u in range(4):
            cols.append(np.arange(6656 + u * 512, 6656 + (u + 1) * 512))
        for i, c in enumerate(cols):
            put((l, f"win{i}"), wk[:, :, c].transpose(1, 0, 2))
        cw = np.asarray(inp["conv_w"][l], np.float32)
        for c in range(8):
            dg = np.zeros((128, 31, 128), np.float32)
            idx = np.arange(128)
            dg[idx, :, idx] = cw[:, c * 128:(c + 1) * 128].T
            put((l, f"cdg{c}"), dg)
        wcp = np.asarray(inp["w_conv_proj"][l], np.float32).reshape(8, 128, 1024)
        for u in range(2):
            put((l, f"cp{u}"), wcp[:, :, u * 512:(u + 1) * 512].transpose(1, 0, 2))
        wap = np.asarray(inp["w_attn_proj"][l], np.float32).reshape(4, 128, 1024)
        put((l, "ap"), wap.transpose(1, 0, 2))
        wo = np.asarray(inp["w_out"][l], np.float32).reshape(8, 128, 1024)
        for u in range(2):
            put((l, f"wo{u}"), wo[:, :, u * 512:(u + 1) * 512].transpose(1, 0, 2))
    return out, offs


def _weight_offsets():
    offs = {}
    pos = 0
    f1, m1, m3, f2 = _unit_list()
    for l in range(NL):
        for name, n in f1 + m1 + m3 + f2:
            offs[(l, name)] = (pos, n)
            pos += 128 * n
    return offs, pos


C_ID = 0
C_L = 128
M_R = 0
M_MASK = 128
NMASKCOL = 1664
CL_N = 144
NCOL = C_L + NL * CL_N


def _pack_masks():
    c = np.zeros((128, NMASKCOL), np.float32)
    R = np.zeros((128, 128), np.float32)
    for i in range(128):
        if i % 64 < 32:
            R[i + 32, i] = -1.0
        else:
            R[i - 32, i] = 1.0
    c[:, M_R:M_R + 128] = R
    ii = np.arange(128)[:, None]
    jj = np.arange(128)[None, :]
    m1 = (ii >= jj).astype(np.float32)
    m2 = (ii <= jj).astype(np.float32)
    mF = ((ii < 64) & (jj <= ii + 64)).astype(np.float32)
    mL = ((ii >= 64) & (ii - jj <= 64)).astype(np.float32)
    c[:, M_MASK:M_MASK + 1536] = np.concatenate([m1, m2, m1, m2, mF, m2, mF, m2, m1, mL, m1, mL], axis=1)
    return c


def _pack_consts(inp):
    c = np.zeros((128, NCOL), np.float32)
    c[:, C_ID:C_ID + 128] = np.eye(128, dtype=np.float32)

    def colmajor(v):
        return np.asarray(v, np.float32).reshape(-1, 128).T

    for l in range(NL):
        b = C_L + l * CL_N
        c[:, b + 0:b + 8] = colmajor(inp["ffn1_norm"][l])
        c[:, b + 8:b + 16] = colmajor(inp["mix_norm"][l])
        c[:, b + 16:b + 24] = colmajor(inp["ffn2_norm"][l])
        c[:, b + 24:b + 32] = colmajor(inp["final_norm"][l])
        bi = np.asarray(inp["b_in"][l], np.float32)
        c[:, b + 32:b + 56] = colmajor(bi[0:3072])
        c[:, b + 56:b + 64] = colmajor(bi[4608:5632])
        c[:, b + 64:b + 72] = colmajor(bi[5632:6656])
        c[:, b + 72:b + 88] = colmajor(bi[6656:8704])
        for s_, nm in enumerate(("q_norm", "k_norm")):
            for g in range(3):
                v = np.asarray(inp[nm][l][g], np.float32)
                for hp in range(4):
                    c[:, b + 88 + s_ * 12 + g * 4 + hp] = np.concatenate([v, v])
        c[:, b + 112:b + 120] = colmajor(inp["conv_b"][l])
        c[:, b + 120:b + 128] = colmajor(inp["conv_ln_g"][l])
        c[:, b + 128:b + 136] = colmajor(inp["conv_ln_b"][l])
        c[:, b + 136:b + 144] = colmajor(inp["b_conv_proj"][l])
    return c


def _rope_tables():
    pos = np.arange(S, dtype=np.float32)
    inv = (np.float32(10000.0) ** (-(np.arange(0, 64, 2, dtype=np.float32)) / np.float32(64))).astype(np.float32)
    ang = (pos[:, None] * inv[None, :]).astype(np.float32)
    cos = np.cos(ang).astype(np.float32).T
    sin = np.sin(ang).astype(np.float32).T
    return np.ascontiguousarray(np.tile(cos, (4, 1))), np.ascontiguousarray(np.tile(sin, (4, 1)))


class Prog:
    def __init__(self, cfg):
        self.cfg = cfg
        self.nc = bass.Bass("TRN2", target_bir_lowering=False)
        self.sc = Sched()
        self.stream_n = 0
        self.stream_first_use = []
        self.woffs, self.wtotal = _weight_offsets()

    def carve(self, off_bytes, shape, dtype):
        esz = 4 if dtype == F32 else 2
        n = int(np.prod(shape[1:]))
        assert off_bytes % 4 == 0
        w0 = off_bytes // 4
        nw = (n * esz + 3) // 4
        ap = self.arena[:, w0:w0 + nw]
        if dtype != F32:
            ap = ap.bitcast(dtype)
        ap = ap[:, 0:n]
        if len(shape) == 3:
            ap = ap.rearrange("p (a b) -> p a b", a=shape[1])
        elif len(shape) == 4:
            ap = ap.rearrange("p (a b c) -> p a b c", a=shape[1], b=shape[2])
        return ap

    def wunit(self, l, name, readers_engine="tensor"):
        n = self.stream_n
        self.stream_n += 1
        off, npp = self.woffs[(l, name)]
        slot = n % 3
        self.stream_first_use.append(self.sc.pos())
        ring = self.ring[slot]
        dst = ring[:, 0:npp]
        src = self.wbf[off:off + 128 * npp].rearrange("(p n) -> p n", p=128)
        uid = (l, name)
        pos = max(self.stream_first_use[n - 2] if n >= 2 else 0, self.stream_min_pos)
        assert uid in self.conv_pos and self.conv_pos[uid] <= pos, ("weight unit used before conversion", uid)
        self.sc.add_at(pos, "sync", lambda e, d=dst, s=src: e.dma_start(out=d, in_=s),
                       reads=[("wbf", uid)], writes=[("ring", slot)], dma_key=("ring", slot), persist=True)
        return dst, ("ring", slot)

    def build(self):
        nc = self.nc
        cfg = self.cfg
        sc = self.sc
        st = contextlib.ExitStack()
        self.st = st
        self.x_in = nc.dram_tensor("x", [S, D], F32, kind="ExternalInput").ap()
        self.wpack = nc.dram_tensor("wpack", [self.wtotal], F32, kind="ExternalInput").ap()
        self.cpack = nc.dram_tensor("cpack", [128, NCOL], F32, kind="ExternalInput").ap()
        self.cmask = nc.dram_tensor("cmask", [128, NMASKCOL], F32, kind="ExternalInput").ap()
        self.y_out = nc.dram_tensor("y", [S, D], F32, kind="ExternalOutput").ap()
        self.wbf = nc.dram_tensor("wbf", [self.wtotal], BF16, kind="Internal").ap()
        self.bv = nc.dram_tensor("bv", [NL, 1536], F32, kind="ExternalInput").ap()
        self.cosT = nc.dram_tensor("cosT", [128, S], F32, kind="ExternalInput").ap()
        self.sinT = nc.dram_tensor("sinT", [128, S], F32, kind="ExternalInput").ap()
        sk = "ExternalOutput" if cfg.get("dbg") else "Internal"
        self.qT_s = nc.dram_tensor("qT_s", [1536, S], BF16, kind=sk).ap()
        self.kT_s = nc.dram_tensor("kT_s", [1536, S], BF16, kind=sk).ap()
        self.v_s = nc.dram_tensor("v_s", [S, 24 * 65], BF16, kind=sk).ap()
        self.uc_s = nc.dram_tensor("uc_s", [1024, S], BF16, kind=sk).ap()
        self.g_s = nc.dram_tensor("g_s", [2048, S], BF16, kind=sk).ap()
        self.nd_s = nc.dram_tensor("nd_s", [3, S, 520], F32, kind=sk).ap()
        self.xT = st.enter_context(nc.sbuf_tensor("xT", [128, KC, S], F32))
        self.cst = st.enter_context(nc.sbuf_tensor("cst", [128, NCOL], F32))
        self.cbf = st.enter_context(nc.sbuf_tensor("cbf", [128, 2048], BF16))
        self.bgt = st.enter_context(nc.sbuf_tensor("bgt", [128, 48], F32))
        self.epst = st.enter_context(nc.sbuf_tensor("epst", [128, 8], F32))
        self.epsc = self.epst[:, 0:1]
        self.ring = [st.enter_context(nc.sbuf_tensor(f"ring{i}", [128, 4096], BF16)) for i in range(3)]
        ARENA_BYTES = cfg.get("arena_bytes", 49 * 1024)
        self.arena_bytes = ARENA_BYTES
        self.arena_t = st.enter_context(nc.sbuf_tensor("arena", [128, ARENA_BYTES // 4], F32))
        self.arena = self.arena_t[:]
        self.ps = [st.enter_context(nc.psum_tensor(f"ps{i}", [128, 512], F32)) for i in range(8)]
        self.onesD = self.cbf[:, 0:128]
        self.blk = self.cbf[:, 128:256]
        self.Rbf = self.cbf[:, 256:384]
        self.maskbf = [self.cbf[:, 384 + i * 512:384 + (i + 1) * 512] for i in range(3)]
        self.identbf = self.cbf[:, 1920:2048]
        self.ident = self.cst[:, C_ID:C_ID + 128]

        self.phase_setup()
        self.stream_min_pos = sc.pos()
        self.phase_load_x()
        nl = cfg.get("nl", NL)
        for l in range(nl):
            if cfg.get("ffn1", True):
                self.phase_ffn(l, 0)
            if cfg.get("m1", True):
                self.phase_m1(l)
            if cfg.get("m2", True):
                self.phase_m2(l)
            if cfg.get("m3", True):
                self.phase_m3(l)
            if cfg.get("ffn2", True):
                self.phase_ffn(l, 1)
            self.phase_final_norm(l, last=(l == nl - 1))
        sc.emit(nc, st)
        st.close()
        return nc

    def emit_conv(self, n, dep=()):
        sc = self.sc
        for _ in range(n):
            if not self.pending_conv:
                return
            l, name = self.pending_conv.pop(0)
            off, npp = self.woffs[(l, name)]
            src = self.wpack[off:off + 128 * npp].rearrange("(p n) -> p n", p=128)
            dst = self.wbf[off:off + 128 * npp].rearrange("(p n) -> p n", p=128)
            self.conv_pos[(l, name)] = sc.pos()
            sc.add("gpsimd", lambda e, d=dst, s=src: e.dma_start(out=d, in_=s), reads=list(dep),
                   writes=[("wbf", (l, name))], dma_key=("cv", self.conv_k % 8), persist=True)
            self.conv_k += 1

    def phase_setup(self):
        sc = self.sc
        f1, m1, m3, f2 = _unit_list()
        self.pending_conv = []
        self.conv_pos = {}
        self.conv_k = 0
        for l in range(NL):
            for name, n in f1 + m1 + m3 + f2:
                self.pending_conv.append((l, name))
        self.emit_conv(len(f1))
        sc.add("sync", lambda e: e.dma_start(out=self.cst[:], in_=self.cpack[:, :]),
               writes=["cst"], dma_key="cst", persist=True)
        sc.add("vector", lambda e: e.memset(self.onesD, 1.0 / D), writes=["onesD"], persist=True)
        sc.add("vector", lambda e: e.memset(self.epst[:], EPS), writes=["epsc"], persist=True)
        sc.add("vector", lambda e: e.memset(self.blk, 0.0), writes=["blk"], persist=True)
        sc.add("vector", lambda e: e.memset(self.cbf[0:64, 128:192], 1.0 / 64), writes=["blk"], persist=True)
        sc.add("vector", lambda e: e.memset(self.cbf[64:128, 192:256], 1.0 / 64), writes=["blk"], persist=True)
        sc.add("gpsimd", lambda e: e.dma_start(out=self.cbf[:, 256:1920], in_=self.cmask[:, :]),
               writes=["Rbf", "maskbf"], dma_key="cmask", persist=True)
        for l in range(NL):
            b = C_L + l * CL_N
            sc.add("vector", lambda e, l=l, b=b: e.tensor_tensor(out=self.bgt[:, l * 24:(l + 1) * 24], in0=self.cst[:, b + 32:b + 56],
                                                            in1=self.cst[:, b + 88:b + 112], op=ALU.mult),
                   reads=["cst"], writes=["bgt"], persist=True)
        sc.add("vector", lambda e: e.tensor_copy(out=self.identbf, in_=self.ident),
               reads=["cst"], writes=["identbf"], persist=True)

    def phase_load_x(self):
        sc = self.sc
        stg = [self.carve(i * 4096, [128, 1024], F32) for i in range(2)]
        for i in range(S // 128):
            sl = i % 2
            sc.add("sync", lambda e, sl=sl, i=i: e.dma_start(out=stg[sl], in_=self.x_in[i * 128:(i + 1) * 128, :]),
                   writes=[("xstg", sl)], dma_key=("xstg", sl))
            for hb in range(2):
                bank = (2 * i + hb) % 4
                pst = self.ps[bank]
                for q in range(4):
                    kc = hb * 4 + q
                    sc.add("tensor", lambda e, pst=pst, q=q, sl=sl, kc=kc: e.transpose(
                        pst[:, q * 128:(q + 1) * 128], stg[sl][:, kc * 128:(kc + 1) * 128], self.ident),
                        reads=[("xstg", sl), "cst"], writes=[("ps", bank)])
                dst = self.xT[:, hb * 4:hb * 4 + 4, i * 128:(i + 1) * 128]
                src = pst[:].rearrange("p (a b) -> p a b", a=4)
                eng = "scalar" if hb == 0 else "vector"
                if eng == "scalar":
                    fn = lambda e, dst=dst, src=src: e.copy(out=dst, in_=src)
                else:
                    fn = lambda e, dst=dst, src=src: e.tensor_copy(out=dst, in_=src)
                sc.add(eng, fn, reads=[("ps", bank)],
                       writes=[("x", kc, i // 4) for kc in range(hb * 4, hb * 4 + 4)])
        sc.fence()

    def emit_rstd(self, out, in_, in_name, out_name):
        sc = self.sc
        sc.add("scalar", lambda e: e.activation(out=out, in_=in_, func=AF.Sqrt, bias=self.epsc, scale=1.0),
               reads=[in_name, "epsc"], writes=[out_name])
        sc.add("vector", lambda e: e.reciprocal(out=out, in_=out), reads=[out_name], writes=[out_name])

    def emit_norm(self, l, gcol, t, h, hname, sq, rstd):
        sc = self.sc
        tok = slice(t * T, (t + 1) * T)
        for kc in range(KC):
            s2 = kc % 2
            sc.add("scalar", lambda e, kc=kc, s2=s2: e.activation(out=sq[s2], in_=self.xT[:, kc, tok], func=AF.Square),
                   reads=[("x", kc, t)], writes=[("sq", s2)])
            sc.add("tensor", lambda e, kc=kc, s2=s2: e.matmul(self.ps[0][:], lhsT=self.onesD, rhs=sq[s2],
                                                              start=(kc == 0), stop=(kc == KC - 1)),
                   reads=[("sq", s2), "onesD"], writes=[("ps", 0)])
        self.emit_rstd(rstd, self.ps[0][:], ("ps", 0), "rstd")
        cb = C_L + l * CL_N + gcol
        for kc in range(KC):
            sc.add("vector", lambda e, kc=kc: e.scalar_tensor_tensor(
                out=h[:, kc, :], in0=self.xT[:, kc, tok], scalar=self.cst[:, cb + kc:cb + kc + 1], in1=rstd,
                op0=ALU.mult, op1=ALU.mult),
                reads=[("x", kc, t), "rstd", "cst"], writes=[(hname, kc)])

    def phase_ffn(self, l, which):
        sc = self.sc
        tag = "f1" if which == 0 else "f2"
        gcol = 0 if which == 0 else 16
        hb = [self.carve(0, [128, 8, T], BF16), self.carve(8192, [128, 8, T], BF16)]
        gated = self.carve(16384, [128, NJ, T], BF16)
        o = 16384 + NJ * 1024
        sq = [self.carve(o, [128, T], BF16), self.carve(o + 1024, [128, T], BF16)]
        rstd = self.carve(o + 2048, [128, T], F32)
        sl = [self.carve(o + 4096, [128, T], F32), self.carve(o + 6144, [128, T], F32)]
        for t in range(NT):
            hs = t % 2
            h = hb[hs]
            hname = ("h", hs)
            tok = slice(t * T, (t + 1) * T)
            self.emit_norm(l, gcol, t, h, hname, sq, rstd)
            hreads = [(hname, kc) for kc in range(KC)]
            cnt = 0
            for u in range(NJ // 2):
                w, wname = self.wunit(l, f"{tag}up{u}")
                w3 = w.rearrange("p (k f) -> p k f", k=8)
                for jj in range(2):
                    j = 2 * u + jj
                    ab = cnt % 2
                    cnt += 1
                    pa = self.ps[1 + ab]
                    pb = self.ps[3 + ab]
                    for kc in range(KC):
                        sc.add("tensor", lambda e, pa=pa, kc=kc, jj=jj, w3=w3, h=h: e.matmul(
                            pa[:], lhsT=w3[:, kc, jj * 256:jj * 256 + 128], rhs=h[:, kc, :],
                            start=(kc == 0), stop=(kc == KC - 1)),
                            reads=[wname, (hname, kc)], writes=[("ps", 1 + ab)])
                    for kc in range(KC):
                        sc.add("tensor", lambda e, pb=pb, kc=kc, jj=jj, w3=w3, h=h: e.matmul(
                            pb[:], lhsT=w3[:, kc, jj * 256 + 128:jj * 256 + 256], rhs=h[:, kc, :],
                            start=(kc == 0), stop=(kc == KC - 1)),
                            reads=[wname, (hname, kc)], writes=[("ps", 3 + ab)])
                    sc.add("scalar", lambda e, pa=pa, ab=ab: e.activation(out=sl[ab], in_=pa[:], func=AF.Silu),
                           reads=[("ps", 1 + ab)], writes=[("sl", ab)])
                    sc.add("vector", lambda e, pb=pb, ab=ab, j=j: e.tensor_tensor(
                        out=gated[:, j, :], in0=sl[ab], in1=pb[:], op=ALU.mult),
                        reads=[("sl", ab), ("ps", 3 + ab)], writes=[("gated", j)])
            for oc in range(8):
                w, wname = self.wunit(l, f"{tag}dn{oc}")
                w3 = w.rearrange("p (k f) -> p k f", k=NJ)
                pd = self.ps[5 + oc % 2]
                for kc in range(NJ):
                    sc.add("tensor", lambda e, pd=pd, kc=kc, w3=w3: e.matmul(
                        pd[:], lhsT=w3[:, kc, :], rhs=gated[:, kc, :], start=(kc == 0), stop=(kc == NJ - 1)),
                        reads=[wname, ("gated", kc)], writes=[("ps", 5 + oc % 2)])
                sc.add("vector", lambda e, pd=pd, oc=oc, tok=tok: e.scalar_tensor_tensor(
                    out=self.xT[:, oc, tok], in0=pd[:], scalar=0.5, in1=self.xT[:, oc, tok],
                    op0=ALU.mult, op1=ALU.add),
                    reads=[("ps", 5 + oc % 2), ("x", oc, t)], writes=[("x", oc, t)])
            self.emit_conv(7, dep=[("x", 7, t)])
        sc.fence()


    def phase_m1(self, l):
        sc = self.sc
        cbase = C_L + l * CL_N
        o = 0
        h = self.carve(o, [128, 8, T], BF16); o += 8192
        sq = [self.carve(o, [128, T], BF16), self.carve(o + 1024, [128, T], BF16)]; o += 2048
        rstd = self.carve(o, [128, T], F32); o += 2048
        sqc = [self.carve(o + i * 1024, [128, T], BF16) for i in range(2)]; o += 2048
        uu = [self.carve(o + i * 1024, [128, T], BF16) for i in range(2)]; o += 2048
        rsc = [self.carve(o + i * 2048, [128, T], F32) for i in range(2)]; o += 4096
        t1 = [self.carve(o + i * 2048, [128, T], F32) for i in range(2)]; o += 4096
        t2 = [self.carve(o + i * 2048, [128, T], F32) for i in range(2)]; o += 4096
        stg = [self.carve(o + i * 1024, [128, T], BF16) for i in range(4)]; o += 4096
        cs = [self.carve(o + i * 2048, [128, T], F32) for i in range(2)]; o += 4096
        bvb = self.carve(o, [128, 1536], BF16); o += 3072
        vstg = [self.carve(o + i * 1280, [128, 8, 65], BF16) for i in range(2)]; o += 2560
        assert o <= self.arena_bytes, o
        sc.add("gpsimd", lambda e: e.dma_start(out=bvb, in_=self.bv[l, :].partition_broadcast(128)),
               writes=["bvb"], dma_key="bvb")
        for i in range(2):
            sc.add("vector", lambda e, i=i: e.memset(vstg[i][:, :, 64:65], 1.0), writes=[("vstg", i)])
        stgn = [0]

        def store(src_name, stg_i, dram_ap):
            sc.add("sync", lambda e, i=stg_i, d=dram_ap: e.dma_start(out=d, in_=stg[i]),
                   reads=[("stg", stg_i)], writes=[src_name], dma_key=("stg", stg_i))

        hnames = [("h", 0, kc) for kc in range(KC)]
        for t in range(NT):
            tok = slice(t * T, (t + 1) * T)
            self.emit_norm(l, 8, t, h, ("h", 0), sq, rstd)
            sc.add("sync", lambda e, tok=tok: e.dma_start(out=cs[0], in_=self.cosT[:, tok]),
                   writes=[("cs", 0)], dma_key=("cs", 0))
            sc.add("sync", lambda e, tok=tok: e.dma_start(out=cs[1], in_=self.sinT[:, tok]),
                   writes=[("cs", 1)], dma_key=("cs", 1))
            zc = [0]

            def zbank():
                b = 1 + zc[0] % 2
                zc[0] += 1
                return b
            vc = 0
            for vb in range(3):
                w, wname = self.wunit(l, f"win{6 + vb}")
                w3 = w.rearrange("p (k f) -> p k f", k=8)
                for s4 in range(4):
                    bk = zbank()
                    pz = self.ps[bk]
                    for kc in range(KC):
                        sc.add("tensor", lambda e, pz=pz, kc=kc, s4=s4, w3=w3: e.matmul(
                            pz[:], lhsT=h[:, kc, s4 * 128:(s4 + 1) * 128], rhs=w3[:, kc, :],
                            start=(kc == 0), stop=(kc == KC - 1)),
                            reads=[wname, (("h", 0), kc)], writes=[("ps", bk)])
                    vs = vc % 2
                    vc += 1
                    sc.add("vector", lambda e, pz=pz, vs=vs, vb=vb: e.tensor_tensor(
                        out=vstg[vs][:, :, 0:64], in0=pz[:].rearrange("p (a b) -> p a b", a=8),
                        in1=bvb[:, vb * 512:(vb + 1) * 512].rearrange("p (a b) -> p a b", a=8), op=ALU.add),
                        reads=[("ps", bk), "bvb"], writes=[("vstg", vs)])
                    r0 = t * T + s4 * 128
                    dst = self.v_s[r0:r0 + 128, vb * 520:(vb + 1) * 520].rearrange("p (a b) -> p a b", a=8)
                    sc.add("sync", lambda e, vs=vs, dst=dst: e.dma_start(out=dst, in_=vstg[vs]),
                           reads=[("vstg", vs)], writes=["v_s"], dma_key=("vstg", vs))
            for gu in range(4):
                w, wname = self.wunit(l, f"win{13 + gu}")
                w3 = w.rearrange("p (k f) -> p k f", k=8)
                for q in range(4):
                    gc = gu * 4 + q
                    bk = zbank()
                    pz = self.ps[bk]
                    for kc in range(KC):
                        sc.add("tensor", lambda e, pz=pz, kc=kc, q=q, w3=w3: e.matmul(
                            pz[:], lhsT=w3[:, kc, q * 128:(q + 1) * 128], rhs=h[:, kc, :],
                            start=(kc == 0), stop=(kc == KC - 1)),
                            reads=[wname, (("h", 0), kc)], writes=[("ps", bk)])
                    si = stgn[0] % 4
                    stgn[0] += 1
                    bc = cbase + 72 + gc
                    sc.add("scalar", lambda e, pz=pz, si=si, bc=bc: e.activation(
                        out=stg[si], in_=pz[:], func=AF.Sigmoid, bias=self.cst[:, bc:bc + 1], scale=1.0),
                        reads=[("ps", bk), "cst"], writes=[("stg", si)])
                    store("g_s", si, self.g_s[gc * 128:(gc + 1) * 128, tok])
            for u in range(4):
                w, wname = self.wunit(l, f"win{9 + u}")
                w3 = w.rearrange("p (k f) -> p k f", k=8)
                for jj in range(2):
                    j = 2 * u + jj
                    ab = j % 2
                    pa = self.ps[1 + ab]
                    pb = self.ps[3 + ab]
                    for kc in range(KC):
                        sc.add("tensor", lambda e, pa=pa, kc=kc, jj=jj, w3=w3: e.matmul(
                            pa[:], lhsT=w3[:, kc, jj * 128:(jj + 1) * 128], rhs=h[:, kc, :],
                            start=(kc == 0), stop=(kc == KC - 1)),
                            reads=[wname, (("h", 0), kc)], writes=[("ps", 1 + ab)])
                    for kc in range(KC):
                        sc.add("tensor", lambda e, pb=pb, kc=kc, jj=jj, w3=w3: e.matmul(
                            pb[:], lhsT=w3[:, kc, 256 + jj * 128:256 + (jj + 1) * 128], rhs=h[:, kc, :],
                            start=(kc == 0), stop=(kc == KC - 1)),
                            reads=[wname, (("h", 0), kc)], writes=[("ps", 3 + ab)])
                    bca = cbase + 56 + j
                    bcg = cbase + 64 + j
                    sc.add("scalar", lambda e, pb=pb, ab=ab, bcg=bcg: e.activation(
                        out=t1[ab], in_=pb[:], func=AF.Sigmoid, bias=self.cst[:, bcg:bcg + 1], scale=1.0),
                        reads=[("ps", 3 + ab), "cst"], writes=[("t1", ab)])
                    si = stgn[0] % 4
                    stgn[0] += 1
                    sc.add("vector", lambda e, pa=pa, ab=ab, si=si, bca=bca: e.scalar_tensor_tensor(
                        out=stg[si], in0=pa[:], scalar=self.cst[:, bca:bca + 1], in1=t1[ab],
                        op0=ALU.add, op1=ALU.mult),
                        reads=[("ps", 1 + ab), ("t1", ab), "cst"], writes=[("stg", si)])
                    store("uc_s", si, self.uc_s[j * 128:(j + 1) * 128, tok])
            zc[0] = 0
            wcur = [None, None]

            def stage1(c):
                q = c % 4
                if q == 0:
                    w, wname = self.wunit(l, f"win{c // 4}")
                    wcur[0] = w.rearrange("p (k f) -> p k f", k=8)
                    wcur[1] = wname
                w3, wname = wcur
                i2 = c % 2
                pz = self.ps[1 + i2]
                for kc in range(KC):
                    sc.add("tensor", lambda e, pz=pz, kc=kc, q=q, w3=w3: e.matmul(
                        pz[:], lhsT=w3[:, kc, q * 128:(q + 1) * 128], rhs=h[:, kc, :],
                        start=(kc == 0), stop=(kc == KC - 1)),
                        reads=[wname, (("h", 0), kc)], writes=[("ps", 1 + i2)])
                bc = cbase + 32 + c
                gcl = cbase + 88 + c
                bgc = l * 24 + c
                sc.add("scalar", lambda e, pz=pz, i2=i2, bc=bc: e.activation(
                    out=sqc[i2], in_=pz[:], func=AF.Square, bias=self.cst[:, bc:bc + 1], scale=1.0),
                    reads=[("ps", 1 + i2), "cst"], writes=[("sqc", i2)])
                sc.add("scalar", lambda e, pz=pz, i2=i2, gcl=gcl, bgc=bgc: e.activation(
                    out=uu[i2], in_=pz[:], func=AF.Identity, bias=self.bgt[:, bgc:bgc + 1],
                    scale=self.cst[:, gcl:gcl + 1]),
                    reads=[("ps", 1 + i2), "cst", "bgt"], writes=[("uu", i2)])

            def stage2(c):
                i2 = c % 2
                pm = self.ps[3 + i2]
                pr = self.ps[5 + i2]
                sc.add("tensor", lambda e, pm=pm, i2=i2: e.matmul(pm[:], lhsT=self.blk, rhs=sqc[i2], start=True, stop=True),
                       reads=[("sqc", i2), "blk"], writes=[("ps", 3 + i2)])
                sc.add("tensor", lambda e, pr=pr, i2=i2: e.matmul(pr[:], lhsT=self.Rbf, rhs=uu[i2], start=True, stop=True),
                       reads=[("uu", i2), "Rbf"], writes=[("ps", 5 + i2)])
                self.emit_rstd(rsc[i2], pm[:], ("ps", 3 + i2), ("rsc", i2))
                sc.add("gpsimd", lambda e, i2=i2: e.tensor_tensor(out=t1[i2], in0=uu[i2], in1=cs[0], op=ALU.mult),
                       reads=[("uu", i2), ("cs", 0)], writes=[("t1", i2)])
                sc.add("vector", lambda e, pr=pr, i2=i2: e.tensor_tensor(out=t2[i2], in0=pr[:], in1=cs[1], op=ALU.mult),
                       reads=[("ps", 5 + i2), ("cs", 1)], writes=[("t2", i2)])
                sc.add("gpsimd", lambda e, i2=i2: e.tensor_tensor(out=t1[i2], in0=t1[i2], in1=t2[i2], op=ALU.add),
                       reads=[("t1", i2), ("t2", i2)], writes=[("t1", i2)])
                si = stgn[0] % 4
                stgn[0] += 1
                sc.add("vector", lambda e, i2=i2, si=si: e.tensor_tensor(out=stg[si], in0=t1[i2], in1=rsc[i2], op=ALU.mult),
                       reads=[("t1", i2), ("rsc", i2)], writes=[("stg", si)])
                if c < 12:
                    store("qT_s", si, self.qT_s[c * 128:(c + 1) * 128, tok])
                else:
                    store("kT_s", si, self.kT_s[(c - 12) * 128:(c - 11) * 128, tok])

            for c in range(25):
                if c < 24:
                    stage1(c)
                if c >= 1:
                    stage2(c - 1)
            if l == 0:
                self.emit_conv(8)
        sc.fence()

    def phase_m2(self, l):
        sc = self.sc
        o = 0
        qT = [self.carve(o + i * 8192, [128, S], BF16) for i in range(2)]; o += 16384
        kT = [self.carve(o + i * 8192, [128, S], BF16) for i in range(2)]; o += 16384
        vt = [self.carve(o + i * 2432, [128, 9, 130], BF16) for i in range(2)]; o += 4864
        pT = [self.carve(o + i * 1024, [128, 512], BF16) for i in range(3)]; o += 3072
        nds = [self.carve(o + i * 2080, [128, 4, 130], F32) for i in range(2)]; o += 4160
        assert o <= self.arena_bytes, o
        it = 0
        vcn = 0
        blkn = 0
        ndn = 0
        for g, (window, dil) in enumerate(GROUPS):
            L = S // dil
            nblk = L // 128
            if g not in self.cfg.get("m2_groups", (0, 1, 2)):
                continue
            for hp in range(4):
                cq = g * 4 + hp
                for hh in range(2):
                    r0 = cq * 128 + hh * 64
                    sc.add("sync", lambda e, hh=hh, r0=r0: e.dma_start(out=qT[hh][0:64, :], in_=self.qT_s[r0:r0 + 64, :]),
                           reads=["qT_s"], writes=[("qT", hh)], dma_key=("qT", hh))
                    sc.add("sync", lambda e, hh=hh, r0=r0: e.dma_start(out=kT[hh][0:64, :], in_=self.kT_s[r0:r0 + 64, :]),
                           reads=["kT_s"], writes=[("kT", hh)], dma_key=("kT", hh))
                col0 = (g * 8 + hp * 2) * 65
                lvl = self.cfg.get("m2_lvl", 4)

                def kbase(j, nblk=nblk, L=L):
                    if j == 0:
                        return 0
                    if j == nblk:
                        return L - 128
                    return j * 128 - 64

                blocks = []
                for r in range(dil):
                    for b0 in range(0, nblk, 8):
                        b1 = min(b0 + 8, nblk)
                        for nb in range(b0, b1):
                            blocks.append(dict(r=r, b0=b0, b1=b1, nb=nb))

                def stageA(B):
                    nonlocal vcn, blkn
                    r, b0, b1, nb = B["r"], B["b0"], B["b1"], B["nb"]
                    if nb == b0:
                        vs = vcn % 2
                        vcn += 1
                        segs = []
                        jlo, jhi = b0, b1
                        if jlo == 0:
                            segs.append((0, 0))
                            jlo = 1
                        last_special = (jhi == nblk)
                        if last_special:
                            jhi = nblk - 1
                        while jlo <= jhi:
                            je = min(jlo + 3, jhi)
                            segs.append((jlo, je))
                            jlo = je + 1
                        if last_special:
                            segs.append((nblk, nblk))
                        for k_, (ja, jb) in enumerate(segs):
                            nt_ = jb - ja + 1
                            base = kbase(ja) * dil + r
                            src = self.v_s[_ss(base, 128 * nt_, dil), col0:col0 + 130].rearrange("(j i) c -> i j c", i=128)
                            dst = vt[vs][:, ja - b0:ja - b0 + nt_, :]
                            sc.add("sync", lambda e, src=src, dst=dst: e.dma_start(out=dst, in_=src),
                                   reads=["v_s"], writes=[("vt", vs, k_)], dma_key=("vt", vs, k_))
                        self._m2_chunk = (vs, [("vt", vs, k_) for k_ in range(len(segs))])
                    B["vs"], B["vnames"] = self._m2_chunk
                    bi = blkn % 3
                    blkn += 1
                    B["bi"] = bi
                    pS = self.ps[(1, 2, 5)[bi]]
                    q0 = nb * 128 * dil + r
                    for hh in range(2):
                        for tl in range(2):
                            kb = kbase(nb + tl) * dil + r
                            sc.add("tensor", lambda e, pS=pS, hh=hh, tl=tl, kb=kb, q0=q0, dil=dil: e.matmul(
                                pS[:, (hh * 2 + tl) * 128:(hh * 2 + tl + 1) * 128],
                                lhsT=kT[hh][0:64, _ss(kb, 128, dil)],
                                rhs=qT[hh][0:64, _ss(q0, 128, dil)],
                                start=True, stop=True),
                                reads=[("qT", hh), ("kT", hh)], writes=[("ps", (1, 2, 5)[bi])])
                    sc.add("scalar", lambda e, pS=pS, bi=bi: e.activation(out=pT[bi], in_=pS[:], func=AF.Exp, scale=0.125),
                           reads=[("ps", (1, 2, 5)[bi])], writes=[("pT", bi)])
                    mv = 1 if nb == 0 else (2 if nb == nblk - 1 else 0)
                    meng = "gpsimd" if (blkn % 2 == 0) else "vector"
                    sc.add(meng, lambda e, bi=bi, mv=mv: e.tensor_tensor(out=pT[bi], in0=pT[bi], in1=self.maskbf[mv], op=ALU.mult),
                           reads=[("pT", bi), "maskbf"], writes=[("pT", bi)])

                def stageB(B):
                    nonlocal ndn
                    r, b0, b1, nb, bi, vs, vnames = B["r"], B["b0"], B["b1"], B["nb"], B["bi"], B["vs"], B["vnames"]
                    pO = self.ps[(3, 4, 6)[bi]]
                    for hh in range(2):
                        for tl in range(2):
                            jl = nb + tl - b0
                            sc.add("tensor", lambda e, pO=pO, hh=hh, tl=tl, jl=jl, bi=bi, vs=vs: e.matmul(
                                pO[:, hh * 65:(hh + 1) * 65],
                                lhsT=pT[bi][:, (hh * 2 + tl) * 128:(hh * 2 + tl + 1) * 128],
                                rhs=vt[vs][:, jl, hh * 65:(hh + 1) * 65],
                                start=(tl == 0), stop=(tl == 1)),
                                reads=[("pT", bi)] + vnames, writes=[("ps", (3, 4, 6)[bi])])
                    ns = ndn % 2
                    sub = (nb - b0) % 4
                    if (nb - b0) % 2 == 0:
                        sc.add("scalar", lambda e, pO=pO, ns=ns, sub=sub: e.copy(out=nds[ns][:, sub, :], in_=pO[:, 0:130]),
                               reads=[("ps", (3, 4, 6)[bi])], writes=[("nds", ns, sub)])
                    else:
                        sc.add("vector", lambda e, pO=pO, ns=ns, sub=sub: e.tensor_copy(out=nds[ns][:, sub, :], in_=pO[:, 0:130]),
                               reads=[("ps", (3, 4, 6)[bi])], writes=[("nds", ns, sub)])
                    if sub == 3 or nb == b1 - 1:
                        nbs = nb - sub
                        nsub = sub + 1
                        rbase = nbs * 128 * dil + r
                        dst = self.nd_s[g, _ss(rbase, 128 * nsub, dil), hp * 130:(hp + 1) * 130].rearrange(
                            "(j i) c -> i j c", i=128)
                        sc.add("sync", lambda e, ns=ns, nsub=nsub, dst=dst: e.dma_start(out=dst, in_=nds[ns][:, 0:nsub, :]),
                               reads=[("nds", ns, q_) for q_ in range(nsub)], writes=["nd_s"], dma_key=("nds", ns))
                        ndn += 1

                for i in range(len(blocks) + 2):
                    if i < len(blocks):
                        stageA(blocks[i])
                    if i >= 2:
                        stageB(blocks[i - 2])
        sc.fence()

    def phase_m3(self, l):
        sc = self.sc
        cbase = C_L + l * CL_N
        T3 = 256
        o = 0
        ucs = self.carve(o, [128, 8, T3 + 30], BF16); o += 4608
        gts = self.carve(o, [128, 8, T3], BF16); o += 4096
        cv = self.carve(o, [128, 8, T3], F32); o += 8192
        cvb = [self.carve(o + i * 512, [128, T3], BF16) for i in range(2)]; o += 1024
        sqv = [self.carve(o + i * 512, [128, T3], BF16) for i in range(2)]; o += 1024
        mu = self.carve(o, [128, T3], F32); o += 1024
        rs = self.carve(o, [128, T3], F32); o += 1024
        nmr = self.carve(o, [128, T3], F32); o += 1024
        tmpv = self.carve(o, [128, T3], F32); o += 1024
        tn = [self.carve(o + i * 1024, [128, T3], F32) for i in range(2)]; o += 2048
        sact = self.carve(o, [128, 8, T3], BF16); o += 4096
        merged = self.carve(o, [128, 8, T3], BF16); o += 4096
        nd = [self.carve(o + i * 2080, [128, 8, 65], F32) for i in range(2)]; o += 4160
        rec = self.carve(o, [128, 8], F32); o += 64
        ya = [self.carve(o + i * 1024, [128, 512], BF16) for i in range(2)]; o += 2048
        yaT = self.carve(o, [128, 4, T3], BF16); o += 2048
        tmpm = [self.carve(o + i * 1024, [128, T3], F32) for i in range(2)]; o += 2048
        assert o <= self.arena_bytes, o
        psT = self.ps[7][:].bitcast(BF16)
        psT0 = self.ps[0][:].bitcast(BF16)
        for t in range(S // T3):
            t0 = t * T3
            tok = slice(t0, t0 + T3)
            lo = t0 - 15
            hi = t0 + T3 + 15
            c0 = 0
            c1 = T3 + 30
            if lo < 0:
                sc.add("gpsimd", lambda e: e.memset(ucs[:, :, 0:15], 0.0), writes=["ucs"])
                c0 = 15
                lo = 0
            if hi > S:
                sc.add("gpsimd", lambda e: e.memset(ucs[:, :, T3 + 15:T3 + 30], 0.0), writes=["ucs"])
                c1 = T3 + 15
                hi = S
            for hf in range(2):
                src = self.uc_s[hf * 512:(hf + 1) * 512, lo:hi].rearrange("(c p) t -> p c t", p=128)
                sc.add("sync", lambda e, src=src, c0=c0, c1=c1, hf=hf: e.dma_start(out=ucs[:, hf * 4:hf * 4 + 4, c0:c1], in_=src),
                       reads=["uc_s"], writes=["ucs"], dma_key=("ucs", hf))
                gsrc = self.g_s[hf * 512:(hf + 1) * 512, tok].rearrange("(c p) t -> p c t", p=128)
                sc.add("sync", lambda e, gsrc=gsrc, hf=hf: e.dma_start(out=gts[:, hf * 4:hf * 4 + 4, :], in_=gsrc),
                       reads=["g_s"], writes=["gts"], dma_key=("gts", hf))
            for s2 in range(2):
                tt = t0 + s2 * 128
                for g in range(3):
                    di = 0 if g == 0 else 1
                    sc.add("sync", lambda e, g=g, di=di, tt=tt: e.dma_start(
                        out=nd[di], in_=self.nd_s[g, tt:tt + 128, :].rearrange("p (a b) -> p a b", a=8)),
                        reads=["nd_s"], writes=[("nd", di)], dma_key=("nd", di))
                    if g > 0:
                        sc.add("gpsimd", lambda e: e.tensor_tensor(out=nd[0], in0=nd[0], in1=nd[1], op=ALU.add),
                               reads=[("nd", 0), ("nd", 1)], writes=[("nd", 0)])
                sc.add("vector", lambda e: e.reciprocal(out=rec, in_=nd[0][:, :, 64]), reads=[("nd", 0)], writes=["rec"])
                sc.add("vector", lambda e, s2=s2: e.tensor_tensor(
                    out=ya[s2].rearrange("p (a b) -> p a b", a=8), in0=nd[0][:, :, 0:64],
                    in1=rec.unsqueeze(2).to_broadcast([128, 8, 64]), op=ALU.mult),
                    reads=[("nd", 0), "rec"], writes=[("ya", s2)])
            for c in range(8):
                w, wname = self.wunit(l, f"cdg{c}")
                w3 = w.rearrange("p (k f) -> p k f", k=31)
                pc = self.ps[1 + c % 2]
                for k in range(31):
                    sc.add("tensor", lambda e, pc=pc, k=k, c=c, w3=w3: e.matmul(
                        pc[:, 0:T3], lhsT=w3[:, k, :], rhs=ucs[:, c, k:k + T3], start=(k == 0), stop=(k == 30)),
                        reads=[wname, "ucs"], writes=[("ps", 1 + c % 2)])
                bc = cbase + 112 + c
                i2 = c % 2
                sc.add("scalar", lambda e, pc=pc, c=c, bc=bc: e.activation(
                    out=cv[:, c, :], in_=pc[:, 0:T3], func=AF.Identity, bias=self.cst[:, bc:bc + 1], scale=1.0),
                    reads=[("ps", 1 + i2), "cst"], writes=[("cv", c)])
                sc.add("scalar", lambda e, pc=pc, i2=i2, bc=bc: e.activation(
                    out=sqv[i2], in_=pc[:, 0:T3], func=AF.Square, bias=self.cst[:, bc:bc + 1], scale=1.0),
                    reads=[("ps", 1 + i2), "cst"], writes=[("sqv", i2)])
                sc.add("vector", lambda e, c=c, i2=i2: e.tensor_copy(out=cvb[i2], in_=cv[:, c, :]),
                       reads=[("cv", c)], writes=[("cvb", i2)])
                sc.add("tensor", lambda e, c=c, i2=i2: e.matmul(self.ps[5][:, 0:T3], lhsT=self.onesD, rhs=cvb[i2],
                                                               start=(c == 0), stop=(c == 7)),
                       reads=[("cvb", i2), "onesD"], writes=[("ps", 5)])
                sc.add("tensor", lambda e, c=c, i2=i2: e.matmul(self.ps[6][:, 0:T3], lhsT=self.onesD, rhs=sqv[i2],
                                                               start=(c == 0), stop=(c == 7)),
                       reads=[("sqv", i2), "onesD"], writes=[("ps", 6)])
            for s2 in range(2):
                pT_ = psT if s2 == 0 else psT0
                bkt = 7 if s2 == 0 else 0
                for fc in range(4):
                    sc.add("tensor", lambda e, fc=fc, s2=s2, pT_=pT_: e.transpose(
                        pT_[:, fc * 128:(fc + 1) * 128], ya[s2][:, fc * 128:(fc + 1) * 128], self.identbf),
                        reads=[("ya", s2), "identbf"], writes=[("ps", bkt)])
                sc.add("scalar", lambda e, s2=s2, pT_=pT_: e.copy(out=yaT[:, :, s2 * 128:(s2 + 1) * 128],
                                                               in_=pT_[:, 0:512].rearrange("p (a b) -> p a b", a=4)),
                       reads=[("ps", bkt)], writes=[("yaT", s2)])
            pcnt = 0
            w, wname = self.wunit(l, "ap")
            w3 = w.rearrange("p (k f) -> p k f", k=4)
            for oc in range(8):
                bk = 3 + pcnt % 2
                pcnt += 1
                pp = self.ps[bk]
                for kc in range(4):
                    sc.add("tensor", lambda e, pp=pp, kc=kc, oc=oc, w3=w3: e.matmul(
                        pp[:, 0:T3], lhsT=w3[:, kc, oc * 128:(oc + 1) * 128], rhs=yaT[:, kc, :],
                        start=(kc == 0), stop=(kc == 3)),
                        reads=[wname, ("yaT", 0), ("yaT", 1)], writes=[("ps", bk)])
                sc.add("vector", lambda e, pp=pp, oc=oc: e.tensor_tensor(out=merged[:, oc, :], in0=pp[:, 0:T3], in1=gts[:, oc, :], op=ALU.mult),
                       reads=[("ps", bk), "gts"], writes=[("merged", oc)])
            sc.add("vector", lambda e: e.tensor_copy(out=mu, in_=self.ps[5][:, 0:T3]), reads=[("ps", 5)], writes=["mu"])
            sc.add("vector", lambda e: e.tensor_tensor(out=tmpv, in0=mu, in1=mu, op=ALU.mult), reads=["mu"], writes=["tmpv"])
            sc.add("vector", lambda e: e.tensor_tensor(out=tmpv, in0=self.ps[6][:, 0:T3], in1=tmpv, op=ALU.subtract),
                   reads=[("ps", 6), "tmpv"], writes=["tmpv"])
            self.emit_rstd(rs, tmpv, "tmpv", "rs")
            sc.add("vector", lambda e: e.scalar_tensor_tensor(out=nmr, in0=mu, scalar=-1.0, in1=rs, op0=ALU.mult, op1=ALU.mult),
                   reads=["mu", "rs"], writes=["nmr"])
            for c in range(8):
                i2 = c % 2
                sc.add("gpsimd", lambda e, c=c, i2=i2: e.tensor_tensor(out=tn[i2], in0=cv[:, c, :], in1=rs, op=ALU.mult),
                       reads=[("cv", c), "rs"], writes=[("tn", i2)])
                sc.add("vector", lambda e, i2=i2: e.tensor_tensor(out=tn[i2], in0=tn[i2], in1=nmr, op=ALU.add),
                       reads=[("tn", i2), "nmr"], writes=[("tn", i2)])
                gc_ = cbase + 120 + c
                bc_ = cbase + 128 + c
                sc.add("scalar", lambda e, c=c, i2=i2, gc_=gc_, bc_=bc_: e.activation(
                    out=sact[:, c, :], in_=tn[i2], func=AF.Silu, bias=self.cst[:, bc_:bc_ + 1], scale=self.cst[:, gc_:gc_ + 1]),
                    reads=[("tn", i2), "cst"], writes=[("sact", c)])
            for hf in range(2):
                gsrc2 = self.g_s[1024 + hf * 512:1024 + (hf + 1) * 512, tok].rearrange("(c p) t -> p c t", p=128)
                sc.add("sync", lambda e, gsrc2=gsrc2, hf=hf: e.dma_start(out=gts[:, hf * 4:hf * 4 + 4, :], in_=gsrc2),
                       reads=["g_s"], writes=["gts"], dma_key=("gts", hf))
            for u in range(2):
                w, wname = self.wunit(l, f"cp{u}")
                w3 = w.rearrange("p (k f) -> p k f", k=8)
                for q in range(4):
                    oc = u * 4 + q
                    bk = 3 + pcnt % 2
                    pcnt += 1
                    pp = self.ps[bk]
                    for kc in range(8):
                        sc.add("tensor", lambda e, pp=pp, kc=kc, q=q, w3=w3: e.matmul(
                            pp[:, 0:T3], lhsT=w3[:, kc, q * 128:(q + 1) * 128], rhs=sact[:, kc, :],
                            start=(kc == 0), stop=(kc == 7)),
                            reads=[wname, ("sact", kc)], writes=[("ps", bk)])
                    bcp = cbase + 136 + oc
                    i2 = oc % 2
                    sc.add("vector", lambda e, pp=pp, oc=oc, bcp=bcp, i2=i2: e.scalar_tensor_tensor(
                        out=tmpm[i2], in0=pp[:, 0:T3], scalar=self.cst[:, bcp:bcp + 1], in1=gts[:, oc, :],
                        op0=ALU.add, op1=ALU.mult),
                        reads=[("ps", bk), "gts", "cst"], writes=[("tmpm", i2)])
                    sc.add("gpsimd", lambda e, oc=oc, i2=i2: e.tensor_tensor(out=merged[:, oc, :], in0=tmpm[i2], in1=merged[:, oc, :], op=ALU.add),
                           reads=[("tmpm", i2), ("merged", oc)], writes=[("merged", oc)])
            for u in range(2):
                w, wname = self.wunit(l, f"wo{u}")
                w3 = w.rearrange("p (k f) -> p k f", k=8)
                for q in range(4):
                    oc = u * 4 + q
                    bk = 3 + pcnt % 2
                    pcnt += 1
                    pp = self.ps[bk]
                    for kc in range(8):
                        sc.add("tensor", lambda e, pp=pp, kc=kc, q=q, w3=w3: e.matmul(
                            pp[:, 0:T3], lhsT=w3[:, kc, q * 128:(q + 1) * 128], rhs=merged[:, kc, :],
                            start=(kc == 0), stop=(kc == 7)),
                            reads=[wname] + [("merged", k_) for k_ in range(8)], writes=[("ps", bk)])
                    sc.add("vector", lambda e, pp=pp, oc=oc, tok=tok: e.tensor_tensor(
                        out=self.xT[:, oc, tok], in0=pp[:, 0:T3], in1=self.xT[:, oc, tok], op=ALU.add),
                        reads=[("ps", bk), ("x", oc, t // 2)], writes=[("x", oc, t // 2)])
        sc.fence()

    def phase_final_norm(self, l, last):
        sc = self.sc
        sq = [self.carve(0, [128, T], BF16), self.carve(1024, [128, T], BF16)]
        rstd = self.carve(2048, [128, T], F32)
        ostg = [self.carve(4096 + i * 4096, [128, 1024], F32) for i in range(2)]
        cb = C_L + l * CL_N + 24
        for t in range(NT):
            tok = slice(t * T, (t + 1) * T)
            for kc in range(KC):
                s2 = kc % 2
                sc.add("scalar", lambda e, kc=kc, s2=s2, tok=tok: e.activation(out=sq[s2], in_=self.xT[:, kc, tok], func=AF.Square),
                       reads=[("x", kc, t)], writes=[("sq", s2)])
                sc.add("tensor", lambda e, kc=kc, s2=s2: e.matmul(self.ps[0][:], lhsT=self.onesD, rhs=sq[s2],
                                                                  start=(kc == 0), stop=(kc == KC - 1)),
                       reads=[("sq", s2), "onesD"], writes=[("ps", 0)])
            self.emit_rstd(rstd, self.ps[0][:], ("ps", 0), "rstd")
            for kc in range(KC):
                sc.add("vector", lambda e, kc=kc, tok=tok: e.scalar_tensor_tensor(
                    out=self.xT[:, kc, tok], in0=self.xT[:, kc, tok], scalar=self.cst[:, cb + kc:cb + kc + 1],
                    in1=rstd, op0=ALU.mult, op1=ALU.mult),
                    reads=[("x", kc, t), "rstd", "cst"], writes=[("x", kc, t)])
            if not last:
                continue
            for s4 in range(4):
                i = t * 4 + s4
                osl = i % 2
                for hb in range(2):
                    bank = 1 + (2 * i + hb) % 4
                    pst = self.ps[bank]
                    for q in range(4):
                        kc = hb * 4 + q
                        sc.add("tensor", lambda e, pst=pst, q=q, kc=kc, i=i: e.transpose(
                            pst[:, q * 128:(q + 1) * 128], self.xT[:, kc, i * 128:(i + 1) * 128], self.ident),
                            reads=[("x", kc, t), "cst"], writes=[("ps", bank)])
                    dst = ostg[osl][:, hb * 512:(hb + 1) * 512]
                    if hb == 0:
                        fn = lambda e, dst=dst, pst=pst: e.copy(out=dst, in_=pst[:])
                        eng = "scalar"
                    else:
                        fn = lambda e, dst=dst, pst=pst: e.tensor_copy(out=dst, in_=pst[:])
                        eng = "vector"
                    sc.add(eng, fn, reads=[("ps", bank)], writes=[("ostg", osl, hb)])
                sc.add("sync", lambda e, osl=osl, i=i: e.dma_start(out=self.y_out[i * 128:(i + 1) * 128, :], in_=ostg[osl]),
                       reads=[("ostg", osl, 0), ("ostg", osl, 1)], writes=[("yout", i)], dma_key=("ostg", osl))
        sc.fence()


_CACHE = {}


def _get_prog(cfg_key, cfg):
    if cfg_key not in _CACHE:
        p = Prog(cfg)
        p.build()
        _CACHE[cfg_key] = p
    return _CACHE[cfg_key]


def kernel(**inputs):
    cfg = inputs.pop("_cfg", None) or {}
    ncores = cfg.get("ncores", NCORES)
    inp = {k: np.asarray(v) for k, v in inputs.items()}
    wpack, _ = _pack_weights(inp)
    cpack = _pack_consts(inp)
    cmask = _pack_masks()
    bvh = np.ascontiguousarray(np.asarray(inp["b_in"], np.float32)[:, 3072:4608])
    cosT, sinT = _rope_tables()
    prog = _get_prog(tuple(sorted(cfg.items())), cfg)
    assert wpack.size == prog.wtotal
    x = np.asarray(inp["x"], np.float32)
    in_maps = []
    for b in range(ncores):
        in_maps.append({"x": np.ascontiguousarray(x[b]), "wpack": wpack, "cpack": cpack, "cmask": cmask,
                        "bv": bvh, "cosT": cosT, "sinT": sinT})
    if cfg.get("trace"):
        res = run_bass_kernel_spmd(prog.nc, in_maps, core_ids=list(range(ncores)), trace=True)
        print("EXEC_TIME_NS", res.exec_time_ns)
    else:
        res = run_bass_kernel_spmd(prog.nc, in_maps, core_ids=list(range(ncores)))
    if cfg.get("dbg"):
        global _DBG
        _DBG = res.results
    out = np.stack([np.asarray(r["y"], np.float32).reshape(S, D) for r in res.results], axis=0)
    return out
```

```python
import contextlib
import numpy as np
import concourse.bass as bass
import concourse.mybir as mybir
from concourse.bass_utils import run_bass_kernel_spmd

F32 = mybir.dt.float32
BF16 = mybir.dt.bfloat16
AF = mybir.ActivationFunctionType
ALU = mybir.AluOpType

D = 1024
S = 4096
DFF = 2816
NL = 2
T = 512
NT = S // T
KC = 8
NJ = DFF // 128
EPS = 1e-6
INW = 8704
GROUPS = ((128, 1), (512, 4), (2048, 16))
NCORES = 8

SEM_CAP = 12000


def _ss(base, n, step):
    return slice(base, base + (n - 1) * step + 1, step)


class Op:
    __slots__ = ("eng", "fn", "reads", "writes", "dma_key", "waits", "sig", "persist")

    def __init__(self, eng, fn, reads, writes, dma_key, persist):
        self.eng = eng
        self.fn = fn
        self.reads = reads
        self.writes = writes
        self.dma_key = dma_key
        self.waits = []
        self.sig = None
        self.persist = persist


class Sched:
    ENGS = ("sync", "scalar", "vector", "gpsimd", "tensor")

    def __init__(self):
        self.ops = []
        self.inserts = []

    def add(self, eng, fn, reads=(), writes=(), dma_key=None, persist=False):
        op = Op(eng, fn, tuple(reads), tuple(writes), dma_key, persist)
        self.ops.append(op)
        return op

    def add_at(self, pos, eng, fn, reads=(), writes=(), dma_key=None, persist=False):
        op = Op(eng, fn, tuple(reads), tuple(writes), dma_key, persist)
        self.inserts.append((pos, len(self.inserts), op))
        return op

    def fence(self):
        self.ops.append("FENCE")

    def pos(self):
        return len(self.ops)

    def finalize(self):
        ins = sorted(self.inserts, key=lambda z: (z[0], z[1]))
        out = []
        k = 0
        for i, op in enumerate(self.ops):
            while k < len(ins) and ins[k][0] <= i:
                out.append(ins[k][2])
                k += 1
            out.append(op)
        while k < len(ins):
            out.append(ins[k][2])
            k += 1
        self.ops = out
        self.inserts = []

    def emit(self, nc, stack):
        self.finalize()
        ops = self.ops
        last_w = {}
        readers = {}
        last_dma_on_key = {}
        last_on_eng = {}
        deps_of = []
        fence_set = {}
        fence_seen = {}
        fence_gen = 0
        dma_since_fence = {}
        real = []

        def skey(o):
            return ("dma", o.dma_key) if o.dma_key is not None else o.eng

        for op in ops:
            if op == "FENCE":
                fence_gen += 1
                fence_set = {}
                for e_, j in last_on_eng.items():
                    fence_set[e_] = j
                for k_, j in dma_since_fence.items():
                    fence_set[("dma", k_)] = j
                dma_since_fence = {}
                continue
            i = len(real)
            real.append(op)
            d = {}

            def addd(j):
                k = skey(real[j])
                if d.get(k, -1) < j:
                    d[k] = j

            for b in op.reads:
                if b in last_w:
                    addd(last_w[b])
            for b in op.writes:
                if b in last_w:
                    addd(last_w[b])
                rb = readers.get(b)
                if rb:
                    for j in rb.values():
                        addd(j)
            if op.dma_key is not None and op.dma_key in last_dma_on_key:
                addd(last_dma_on_key[op.dma_key])
            if not op.persist and fence_seen.get(op.eng, 0) != fence_gen:
                for j in fence_set.values():
                    addd(j)
                fence_seen[op.eng] = fence_gen
            if op.dma_key is None and op.eng == "tensor":
                d.pop("tensor", None)
            deps_of.append(set(d.values()))
            me = skey(op)
            for b in op.reads:
                readers.setdefault(b, {})[me] = i
            for b in op.writes:
                last_w[b] = i
                readers[b] = {}
            if op.dma_key is not None:
                last_dma_on_key[op.dma_key] = i
                if not op.persist:
                    dma_since_fence[op.dma_key] = i
            else:
                last_on_eng[op.eng] = i
        needs_sig = [False] * len(real)
        for i, op in enumerate(real):
            for j in deps_of[i]:
                needs_sig[j] = True
        eng_sem = {}
        eng_cnt = {}
        dma_sem = {}
        dma_cnt = {}
        nsem = [0]

        def new_sem(tag):
            nsem[0] += 1
            return stack.enter_context(nc.semaphore(f"{tag}{nsem[0]}"))

        for i, op in enumerate(real):
            if op.dma_key is not None:
                if op.dma_key not in dma_sem:
                    dma_sem[op.dma_key] = new_sem("d")
                    dma_cnt[op.dma_key] = 0
                dma_cnt[op.dma_key] += 16
                if dma_cnt[op.dma_key] > SEM_CAP * 4:
                    dma_sem[op.dma_key] = new_sem("d")
                    dma_cnt[op.dma_key] = 16
                op.sig = (dma_sem[op.dma_key], dma_cnt[op.dma_key])
            elif needs_sig[i]:
                e = op.eng
                if e not in eng_sem or eng_cnt[e] >= SEM_CAP:
                    eng_sem[e] = new_sem(e[0])
                    eng_cnt[e] = 0
                eng_cnt[e] += 1
                op.sig = (eng_sem[e], eng_cnt[e])
        waited = {e: {} for e in self.ENGS}
        nwaits = 0
        for i, op in enumerate(real):
            w = waited[op.eng]
            best = {}
            for j in deps_of[i]:
                sem, cnt = real[j].sig
                k = id(sem)
                if w.get(k, 0) >= cnt:
                    continue
                if k not in best or best[k][1] < cnt:
                    best[k] = (sem, cnt)
            for k, (sem, cnt) in best.items():
                w[k] = cnt
                op.waits.append((sem, cnt))
                nwaits += 1
        self.stats = dict(n_ops=len(real), n_sems=nsem[0], n_waits=nwaits,
                          n_sig=sum(1 for o in real if o.sig is not None))
        per_eng = {e: [o for o in real if o.eng == e] for e in self.ENGS}
        tail = []
        for e in self.ENGS:
            if e in eng_sem:
                pass
        last_sigs = {}
        for o in real:
            if o.sig is not None:
                last_sigs[id(o.sig[0])] = o.sig
        with nc.Block() as block:
            def runner(name):
                def body(e):
                    for o in per_eng[name]:
                        for sem, cnt in o.waits:
                            e.wait_ge(sem, cnt)
                        ins = o.fn(e)
                        if o.sig is not None:
                            if o.dma_key is not None:
                                ins.then_inc(o.sig[0], 16)
                            else:
                                ins.then_inc(o.sig[0], 1)
                    if name == "sync":
                        for sem, cnt in last_sigs.values():
                            if waited["sync"].get(id(sem), 0) < cnt:
                                e.wait_ge(sem, cnt)
                return body
            block.sync(runner("sync"))
            block.scalar(runner("scalar"))
            block.vector(runner("vector"))
            block.gpsimd(runner("gpsimd"))
            block.tensor(runner("tensor"))


def _unit_list():
    units = []
    for w in ("f1", "f2"):
        pass
    def ffn(tag):
        u = []
        for i in range(NJ // 2):
            u.append((f"{tag}up{i}", 8 * 512))
        for oc in range(8):
            u.append((f"{tag}dn{oc}", NJ * 128))
        return u
    m1 = [(f"win{i}", 8 * 512) for i in (6, 7, 8, 13, 14, 15, 16, 9, 10, 11, 12, 0, 1, 2, 3, 4, 5)]
    m3 = [(f"cdg{c}", 31 * 128) for c in range(8)]
    m3 += [("cp0", 8 * 512), ("cp1", 8 * 512), ("ap", 4 * 1024), ("wo0", 8 * 512), ("wo1", 8 * 512)]
    return ffn("f1"), m1, m3, ffn("f2")


def _pack_weights(inp):
    offs, total = _weight_offsets()
    out = np.empty(total, np.float32)

    def put(key, arr):
        arr = np.ascontiguousarray(arr, dtype=np.float32)
        assert arr.shape[0] == 128
        off, n = offs[key]
        assert arr.size == 128 * n, (key, arr.shape, n)
        out[off:off + 128 * n] = arr.reshape(-1)

    for l in range(NL):
        for tag, wu, wd in (("f1", inp["ffn1_w_up"][l], inp["ffn1_w_down"][l]),
                            ("f2", inp["ffn2_w_up"][l], inp["ffn2_w_down"][l])):
            wuk = np.asarray(wu, np.float32).reshape(8, 128, 2 * DFF)
            for i in range(NJ // 2):
                j0, j1 = 2 * i, 2 * i + 1
                c = np.concatenate([np.arange(j0 * 128, (j0 + 1) * 128),
                                    np.arange(DFF + j0 * 128, DFF + (j0 + 1) * 128),
                                    np.arange(j1 * 128, (j1 + 1) * 128),
                                    np.arange(DFF + j1 * 128, DFF + (j1 + 1) * 128)])
                put((l, f"{tag}up{i}"), wuk[:, :, c].transpose(1, 0, 2))
            wdk = np.asarray(wd, np.float32).reshape(NJ, 128, D)
            for oc in range(8):
                put((l, f"{tag}dn{oc}"), wdk[:, :, oc * 128:(oc + 1) * 128].transpose(1, 0, 2))
        wk = np.asarray(inp["w_in"][l], np.float32).reshape(8, 128, INW)
        cols = []
        for i in range(9):
            cols.append(np.arange(i * 512, (i + 1) * 512))
        ga = 4608
        gg = 4608 + 1024
        for u in range(4):
            cols.append(np.concatenate([np.arange(ga + u * 256, ga + (u + 1) * 256),
                                        np.arange(gg + u * 256, gg + (u + 1) * 256)]))
        for u in range(4):
            cols.append(np.arange(6656 + u * 512, 6656 + (u + 1) * 512))
        for i, c in enumerate(cols):
            put((l, f"win{i}"), wk[:, :, c].transpose(1, 0, 2))
        cw = np.asarray(inp["conv_w"][l], np.float32)
        for c in range(8):
            dg = np.zeros((128, 31, 128), np.float32)
            idx = np.arange(128)
            dg[idx, :, idx] = cw[:, c * 128:(c + 1) * 128].T
            put((l, f"cdg{c}"), dg)
        wcp = np.asarray(inp["w_conv_proj"][l], np.float32).reshape(8, 128, 1024)
        for u in range(2):
            put((l, f"cp{u}"), wcp[:, :, u * 512:(u + 1) * 512].transpose(1, 0, 2))
        wap = np.asarray(inp["w_attn_proj"][l], np.float32).reshape(4, 128, 1024)
        put((l, "ap"), wap.transpose(1, 0, 2))
        wo = np.asarray(inp["w_out"][l], np.float32).reshape(8, 128, 1024)
        for u in range(2):
            put((l, f"wo{u}"), wo[:, :, u * 512:(u + 1) * 512].transpose(1, 0, 2))
    return out, offs


def _weight_offsets():
    offs = {}
    pos = 0
    f1, m1, m3, f2 = _unit_list()
    for l in range(NL):
        for name, n in f1 + m1 + m3 + f2:
            offs[(l, name)] = (pos, n)
            pos += 128 * n
    return offs, pos


C_ID = 0
C_L = 128
M_R = 0
M_MASK = 128
NMASKCOL = 1664
CL_N = 144
NCOL = C_L + NL * CL_N


def _pack_masks():
    c = np.zeros((128, NMASKCOL), np.float32)
    R = np.zeros((128, 128), np.float32)
    for i in range(128):
        if i % 64 < 32:
            R[i + 32, i] = -1.0
        else:
            R[i - 32, i] = 1.0
    c[:, M_R:M_R + 128] = R
    ii = np.arange(128)[:, None]
    jj = np.arange(128)[None, :]
    m1 = (ii >= jj).astype(np.float32)
    m2 = (ii <= jj).astype(np.float32)
    mF = ((ii < 64) & (jj <= ii + 64)).astype(np.float32)
    mL = ((ii >= 64) & (ii - jj <= 64)).astype(np.float32)
    c[:, M_MASK:M_MASK + 1536] = np.concatenate([m1, m2, m1, m2, mF, m2, mF, m2, m1, mL, m1, mL], axis=1)
    return c


def _pack_consts(inp):
    c = np.zeros((128, NCOL), np.float32)
    c[:, C_ID:C_ID + 128] = np.eye(128, dtype=np.float32)

    def colmajor(v):
        return np.asarray(v, np.float32).reshape(-1, 128).T

    for l in range(NL):
        b = C_L + l * CL_N
        c[:, b + 0:b + 8] = colmajor(inp["ffn1_norm"][l])
        c[:, b + 8:b + 16] = colmajor(inp["mix_norm"][l])
        c[:, b + 16:b + 24] = colmajor(inp["ffn2_norm"][l])
        c[:, b + 24:b + 32] = colmajor(inp["final_norm"][l])
        bi = np.asarray(inp["b_in"][l], np.float32)
        c[:, b + 32:b + 56] = colmajor(bi[0:3072])
        c[:, b + 56:b + 64] = colmajor(bi[4608:5632])
        c[:, b + 64:b + 72] = colmajor(bi[5632:6656])
        c[:, b + 72:b + 88] = colmajor(bi[6656:8704])
        for s_, nm in enumerate(("q_norm", "k_norm")):
            for g in range(3):
                v = np.asarray(inp[nm][l][g], np.float32)
                for hp in range(4):
                    c[:, b + 88 + s_ * 12 + g * 4 + hp] = np.concatenate([v, v])
        c[:, b + 112:b + 120] = colmajor(inp["conv_b"][l])
        c[:, b + 120:b + 128] = colmajor(inp["conv_ln_g"][l])
        c[:, b + 128:b + 136] = colmajor(inp["conv_ln_b"][l])
        c[:, b + 136:b + 144] = colmajor(inp["b_conv_proj"][l])
    return c


def _rope_tables():
    pos = np.arange(S, dtype=np.float32)
    inv = (np.float32(10000.0) ** (-(np.arange(0, 64, 2, dtype=np.float32)) / np.float32(64))).astype(np.float32)
    ang = (pos[:, None] * inv[None, :]).astype(np.float32)
    cos = np.cos(ang).astype(np.float32).T
    sin = np.sin(ang).astype(np.float32).T
    return np.ascontiguousarray(np.tile(cos, (4, 1))), np.ascontiguousarray(np.tile(sin, (4, 1)))


class Prog:
    def __init__(self, cfg):
        self.cfg = cfg
        self.nc = bass.Bass("TRN2", target_bir_lowering=False)
        self.sc = Sched()
        self.stream_n = 0
        self.stream_first_use = []
        self.woffs, self.wtotal = _weight_offsets()

    def carve(self, off_bytes, shape, dtype):
        esz = 4 if dtype == F32 else 2
        n = int(np.prod(shape[1:]))
        assert off_bytes % 4 == 0
        w0 = off_bytes // 4
        nw = (n * esz + 3) // 4
        ap = self.arena[:, w0:w0 + nw]
        if dtype != F32:
            ap = ap.bitcast(dtype)
        ap = ap[:, 0:n]
        if len(shape) == 3:
            ap = ap.rearrange("p (a b) -> p a b", a=shape[1])
        elif len(shape) == 4:
            ap = ap.rearrange("p (a b c) -> p a b c", a=shape[1], b=shape[2])
        return ap

    def wunit(self, l, name, readers_engine="tensor"):
        n = self.stream_n
        self.stream_n += 1
        off, npp = self.woffs[(l, name)]
        slot = n % 3
        self.stream_first_use.append(self.sc.pos())
        ring = self.ring[slot]
        dst = ring[:, 0:npp]
        src = self.wbf[off:off + 128 * npp].rearrange("(p n) -> p n", p=128)
        uid = (l, name)
        pos = max(self.stream_first_use[n - 2] if n >= 2 else 0, self.stream_min_pos)
        assert uid in self.conv_pos and self.conv_pos[uid] <= pos, ("weight unit used before conversion", uid)
        self.sc.add_at(pos, "sync", lambda e, d=dst, s=src: e.dma_start(out=d, in_=s),
                       reads=[("wbf", uid)], writes=[("ring", slot)], dma_key=("ring", slot), persist=True)
        return dst, ("ring", slot)

    def build(self):
        nc = self.nc
        cfg = self.cfg
        sc = self.sc
        st = contextlib.ExitStack()
        self.st = st
        self.x_in = nc.dram_tensor("x", [S, D], F32, kind="ExternalInput").ap()
        self.wpack = nc.dram_tensor("wpack", [self.wtotal], F32, kind="ExternalInput").ap()
        self.cpack = nc.dram_tensor("cpack", [128, NCOL], F32, kind="ExternalInput").ap()
        self.cmask = nc.dram_tensor("cmask", [128, NMASKCOL], F32, kind="ExternalInput").ap()
        self.y_out = nc.dram_tensor("y", [S, D], F32, kind="ExternalOutput").ap()
        self.wbf = nc.dram_tensor("wbf", [self.wtotal], BF16, kind="Internal").ap()
        self.bv = nc.dram_tensor("bv", [NL, 1536], F32, kind="ExternalInput").ap()
        self.cosT = nc.dram_tensor("cosT", [128, S], F32, kind="ExternalInput").ap()
        self.sinT = nc.dram_tensor("sinT", [128, S], F32, kind="ExternalInput").ap()
        sk = "ExternalOutput" if cfg.get("dbg") else "Internal"
        self.qT_s = nc.dram_tensor("qT_s", [1536, S], BF16, kind=sk).ap()
        self.kT_s = nc.dram_tensor("kT_s", [1536, S], BF16, kind=sk).ap()
        self.v_s = nc.dram_tensor("v_s", [S, 24 * 65], BF16, kind=sk).ap()
        self.uc_s = nc.dram_tensor("uc_s", [1024, S], BF16, kind=sk).ap()
        self.g_s = nc.dram_tensor("g_s", [2048, S], BF16, kind=sk).ap()
        self.nd_s = nc.dram_tensor("nd_s", [3, S, 520], F32, kind=sk).ap()
        self.xT = st.enter_context(nc.sbuf_tensor("xT", [128, KC, S], F32))
        self.cst = st.enter_context(nc.sbuf_tensor("cst", [128, NCOL], F32))
        self.cbf = st.enter_context(nc.sbuf_tensor("cbf", [128, 2048], BF16))
        self.bgt = st.enter_context(nc.sbuf_tensor("bgt", [128, 48], F32))
        self.epst = st.enter_context(nc.sbuf_tensor("epst", [128, 8], F32))
        self.epsc = self.epst[:, 0:1]
        self.ring = [st.enter_context(nc.sbuf_tensor(f"ring{i}", [128, 4096], BF16)) for i in range(3)]
        ARENA_BYTES = cfg.get("arena_bytes", 49 * 1024)
        self.arena_bytes = ARENA_BYTES
        self.arena_t = st.enter_context(nc.sbuf_tensor("arena", [128, ARENA_BYTES // 4], F32))
        self.arena = self.arena_t[:]
        self.ps = [st.enter_context(nc.psum_tensor(f"ps{i}", [128, 512], F32)) for i in range(8)]
        self.onesD = self.cbf[:, 0:128]
        self.blk = self.cbf[:, 128:256]
        self.Rbf = self.cbf[:, 256:384]
        self.maskbf = [self.cbf[:, 384 + i * 512:384 + (i + 1) * 512] for i in range(3)]
        self.identbf = self.cbf[:, 1920:2048]
        self.ident = self.cst[:, C_ID:C_ID + 128]

        self.phase_setup()
        self.stream_min_pos = sc.pos()
        self.phase_load_x()
        nl = cfg.get("nl", NL)
        for l in range(nl):
            if cfg.get("ffn1", True):
                self.phase_ffn(l, 0)
            if cfg.get("m1", True):
                self.phase_m1(l)
            if cfg.get("m2", True):
                self.phase_m2(l)
            if cfg.get("m3", True):
                self.phase_m3(l)
            if cfg.get("ffn2", True):
                self.phase_ffn(l, 1)
            self.phase_final_norm(l, last=(l == nl - 1))
        sc.emit(nc, st)
        st.close()
        return nc

    def emit_conv(self, n, dep=()):
        sc = self.sc
        for _ in range(n):
            if not self.pending_conv:
                return
            l, name = self.pending_conv.pop(0)
            off, npp = self.woffs[(l, name)]
            src = self.wpack[off:off + 128 * npp].rearrange("(p n) -> p n", p=128)
            dst = self.wbf[off:off + 128 * npp].rearrange("(p n) -> p n", p=128)
            self.conv_pos[(l, name)] = sc.pos()
            sc.add("gpsimd", lambda e, d=dst, s=src: e.dma_start(out=d, in_=s), reads=list(dep),
                   writes=[("wbf", (l, name))], dma_key=("cv", self.conv_k % 8), persist=True)
            self.conv_k += 1

    def phase_setup(self):
        sc = self.sc
        f1, m1, m3, f2 = _unit_list()
        self.pending_conv = []
        self.conv_pos = {}
        self.conv_k = 0
        for l in range(NL):
            for name, n in f1 + m1 + m3 + f2:
                self.pending_conv.append((l, name))
        self.emit_conv(len(f1))
        sc.add("sync", lambda e: e.dma_start(out=self.cst[:], in_=self.cpack[:, :]),
               writes=["cst"], dma_key="cst", persist=True)
        sc.add("vector", lambda e: e.memset(self.onesD, 1.0 / D), writes=["onesD"], persist=True)
        sc.add("vector", lambda e: e.memset(self.epst[:], EPS), writes=["epsc"], persist=True)
        sc.add("vector", lambda e: e.memset(self.blk, 0.0), writes=["blk"], persist=True)
        sc.add("vector", lambda e: e.memset(self.cbf[0:64, 128:192], 1.0 / 64), writes=["blk"], persist=True)
        sc.add("vector", lambda e: e.memset(self.cbf[64:128, 192:256], 1.0 / 64), writes=["blk"], persist=True)
        sc.add("gpsimd", lambda e: e.dma_start(out=self.cbf[:, 256:1920], in_=self.cmask[:, :]),
               writes=["Rbf", "maskbf"], dma_key="cmask", persist=True)
        for l in range(NL):
            b = C_L + l * CL_N
            sc.add("vector", lambda e, l=l, b=b: e.tensor_tensor(out=self.bgt[:, l * 24:(l + 1) * 24], in0=self.cst[:, b + 32:b + 56],
                                                            in1=self.cst[:, b + 88:b + 112], op=ALU.mult),
                   reads=["cst"], writes=["bgt"], persist=True)
        sc.add("vector", lambda e: e.tensor_copy(out=self.identbf, in_=self.ident),
               reads=["cst"], writes=["identbf"], persist=True)

    def phase_load_x(self):
        sc = self.sc
        stg = [self.carve(i * 4096, [128, 1024], F32) for i in range(2)]
        for i in range(S // 128):
            sl = i % 2
            sc.add("sync", lambda e, sl=sl, i=i: e.dma_start(out=stg[sl], in_=self.x_in[i * 128:(i + 1) * 128, :]),
                   writes=[("xstg", sl)], dma_key=("xstg", sl))
            for hb in range(2):
                bank = (2 * i + hb) % 4
                pst = self.ps[bank]
                for q in range(4):
                    kc = hb * 4 + q
                    sc.add("tensor", lambda e, pst=pst, q=q, sl=sl, kc=kc: e.transpose(
                        pst[:, q * 128:(q + 1) * 128], stg[sl][:, kc * 128:(kc + 1) * 128], self.ident),
                        reads=[("xstg", sl), "cst"], writes=[("ps", bank)])
                dst = self.xT[:, hb * 4:hb * 4 + 4, i * 128:(i + 1) * 128]
                src = pst[:].rearrange("p (a b) -> p a b", a=4)
                eng = "scalar" if hb == 0 else "vector"
                if eng == "scalar":
                    fn = lambda e, dst=dst, src=src: e.copy(out=dst, in_=src)
                else:
                    fn = lambda e, dst=dst, src=src: e.tensor_copy(out=dst, in_=src)
                sc.add(eng, fn, reads=[("ps", bank)],
                       writes=[("x", kc, i // 4) for kc in range(hb * 4, hb * 4 + 4)])
        sc.fence()

    def emit_rstd(self, out, in_, in_name, out_name):
        sc = self.sc
        sc.add("scalar", lambda e: e.activation(out=out, in_=in_, func=AF.Ln, bias=self.epsc, scale=1.0),
               reads=[in_name, "epsc"], writes=[out_name])
        sc.add("scalar", lambda e: e.activation(out=out, in_=out, func=AF.Exp, scale=-0.5),
               reads=[out_name], writes=[out_name])

    def emit_norm(self, l, gcol, t, h, hname, sq, rstd):
        sc = self.sc
        tok = slice(t * T, (t + 1) * T)
        for kc in range(KC):
            s2 = kc % 2
            sc.add("scalar", lambda e, kc=kc, s2=s2: e.activation(out=sq[s2], in_=self.xT[:, kc, tok], func=AF.Square),
                   reads=[("x", kc, t)], writes=[("sq", s2)])
            sc.add("tensor", lambda e, kc=kc, s2=s2: e.matmul(self.ps[0][:], lhsT=self.onesD, rhs=sq[s2],
                                                              start=(kc == 0), stop=(kc == KC - 1)),
                   reads=[("sq", s2), "onesD"], writes=[("ps", 0)])
        self.emit_rstd(rstd, self.ps[0][:], ("ps", 0), "rstd")
        cb = C_L + l * CL_N + gcol
        for kc in range(KC):
            sc.add("vector", lambda e, kc=kc: e.scalar_tensor_tensor(
                out=h[:, kc, :], in0=self.xT[:, kc, tok], scalar=self.cst[:, cb + kc:cb + kc + 1], in1=rstd,
                op0=ALU.mult, op1=ALU.mult),
                reads=[("x", kc, t), "rstd", "cst"], writes=[(hname, kc)])

    def phase_ffn(self, l, which):
        sc = self.sc
        tag = "f1" if which == 0 else "f2"
        gcol = 0 if which == 0 else 16
        hb = [self.carve(0, [128, 8, T], BF16), self.carve(8192, [128, 8, T], BF16)]
        gated = self.carve(16384, [128, NJ, T], BF16)
        o = 16384 + NJ * 1024
        sq = [self.carve(o, [128, T], BF16), self.carve(o + 1024, [128, T], BF16)]
        rstd = self.carve(o + 2048, [128, T], F32)
        sl = [self.carve(o + 4096, [128, T], F32), self.carve(o + 6144, [128, T], F32)]
        for t in range(NT):
            hs = t % 2
            h = hb[hs]
            hname = ("h", hs)
            tok = slice(t * T, (t + 1) * T)
            self.emit_norm(l, gcol, t, h, hname, sq, rstd)
            hreads = [(hname, kc) for kc in range(KC)]
            cnt = 0
            for u in range(NJ // 2):
                w, wname = self.wunit(l, f"{tag}up{u}")
                w3 = w.rearrange("p (k f) -> p k f", k=8)
                for jj in range(2):
                    j = 2 * u + jj
                    ab = cnt % 2
                    cnt += 1
                    pa = self.ps[1 + ab]
                    pb = self.ps[3 + ab]
                    for kc in range(KC):
                        sc.add("tensor", lambda e, pa=pa, kc=kc, jj=jj, w3=w3, h=h: e.matmul(
                            pa[:], lhsT=w3[:, kc, jj * 256:jj * 256 + 128], rhs=h[:, kc, :],
                            start=(kc == 0), stop=(kc == KC - 1)),
                            reads=[wname, (hname, kc)], writes=[("ps", 1 + ab)])
                    for kc in range(KC):
                        sc.add("tensor", lambda e, pb=pb, kc=kc, jj=jj, w3=w3, h=h: e.matmul(
                            pb[:], lhsT=w3[:, kc, jj * 256 + 128:jj * 256 + 256], rhs=h[:, kc, :],
                            start=(kc == 0), stop=(kc == KC - 1)),
                            reads=[wname, (hname, kc)], writes=[("ps", 3 + ab)])
                    sc.add("scalar", lambda e, pa=pa, ab=ab: e.activation(out=sl[ab], in_=pa[:], func=AF.Silu),
                           reads=[("ps", 1 + ab)], writes=[("sl", ab)])
                    sc.add("vector", lambda e, pb=pb, ab=ab, j=j: e.tensor_tensor(
                        out=gated[:, j, :], in0=sl[ab], in1=pb[:], op=ALU.mult),
                        reads=[("sl", ab), ("ps", 3 + ab)], writes=[("gated", j)])
            for oc in range(8):
                w, wname = self.wunit(l, f"{tag}dn{oc}")
                w3 = w.rearrange("p (k f) -> p k f", k=NJ)
                pd = self.ps[5 + oc % 2]
                for kc in range(NJ):
                    sc.add("tensor", lambda e, pd=pd, kc=kc, w3=w3: e.matmul(
                        pd[:], lhsT=w3[:, kc, :], rhs=gated[:, kc, :], start=(kc == 0), stop=(kc == NJ - 1)),
                        reads=[wname, ("gated", kc)], writes=[("ps", 5 + oc % 2)])
                sc.add("vector", lambda e, pd=pd, oc=oc, tok=tok: e.scalar_tensor_tensor(
                    out=self.xT[:, oc, tok], in0=pd[:], scalar=0.5, in1=self.xT[:, oc, tok],
                    op0=ALU.mult, op1=ALU.add),
                    reads=[("ps", 5 + oc % 2), ("x", oc, t)], writes=[("x", oc, t)])
            self.emit_conv(7, dep=[("x", 7, t)])
        sc.fence()


    def phase_m1(self, l):
        sc = self.sc
        cbase = C_L + l * CL_N
        o = 0
        h = self.carve(o, [128, 8, T], BF16); o += 8192
        sq = [self.carve(o, [128, T], BF16), self.carve(o + 1024, [128, T], BF16)]; o += 2048
        rstd = self.carve(o, [128, T], F32); o += 2048
        sqc = [self.carve(o + i * 1024, [128, T], BF16) for i in range(2)]; o += 2048
        uu = [self.carve(o + i * 1024, [128, T], BF16) for i in range(2)]; o += 2048
        rsc = [self.carve(o + i * 2048, [128, T], F32) for i in range(2)]; o += 4096
        t1 = [self.carve(o + i * 2048, [128, T], F32) for i in range(2)]; o += 4096
        t2 = [self.carve(o + i * 2048, [128, T], F32) for i in range(2)]; o += 4096
        stg = [self.carve(o + i * 1024, [128, T], BF16) for i in range(4)]; o += 4096
        cs = [self.carve(o + i * 2048, [128, T], F32) for i in range(2)]; o += 4096
        bvb = self.carve(o, [128, 1536], BF16); o += 3072
        vstg = [self.carve(o + i * 1280, [128, 8, 65], BF16) for i in range(2)]; o += 2560
        assert o <= self.arena_bytes, o
        sc.add("gpsimd", lambda e: e.dma_start(out=bvb, in_=self.bv[l, :].partition_broadcast(128)),
               writes=["bvb"], dma_key="bvb")
        for i in range(2):
            sc.add("vector", lambda e, i=i: e.memset(vstg[i][:, :, 64:65], 1.0), writes=[("vstg", i)])
        stgn = [0]

        def store(src_name, stg_i, dram_ap):
            sc.add("sync", lambda e, i=stg_i, d=dram_ap: e.dma_start(out=d, in_=stg[i]),
                   reads=[("stg", stg_i)], writes=[src_name], dma_key=("stg", stg_i))

        hnames = [("h", 0, kc) for kc in range(KC)]
        for t in range(NT):
            tok = slice(t * T, (t + 1) * T)
            self.emit_norm(l, 8, t, h, ("h", 0), sq, rstd)
            sc.add("sync", lambda e, tok=tok: e.dma_start(out=cs[0], in_=self.cosT[:, tok]),
                   writes=[("cs", 0)], dma_key=("cs", 0))
            sc.add("sync", lambda e, tok=tok: e.dma_start(out=cs[1], in_=self.sinT[:, tok]),
                   writes=[("cs", 1)], dma_key=("cs", 1))
            zc = [0]

            def zbank():
                b = 1 + zc[0] % 2
                zc[0] += 1
                return b
            vc = 0
            for vb in range(3):
                w, wname = self.wunit(l, f"win{6 + vb}")
                w3 = w.rearrange("p (k f) -> p k f", k=8)
                for s4 in range(4):
                    bk = zbank()
                    pz = self.ps[bk]
                    for kc in range(KC):
                        sc.add("tensor", lambda e, pz=pz, kc=kc, s4=s4, w3=w3: e.matmul(
                            pz[:], lhsT=h[:, kc, s4 * 128:(s4 + 1) * 128], rhs=w3[:, kc, :],
                            start=(kc == 0), stop=(kc == KC - 1)),
                            reads=[wname, (("h", 0), kc)], writes=[("ps", bk)])
                    vs = vc % 2
                    vc += 1
                    sc.add("vector", lambda e, pz=pz, vs=vs, vb=vb: e.tensor_tensor(
                        out=vstg[vs][:, :, 0:64], in0=pz[:].rearrange("p (a b) -> p a b", a=8),
                        in1=bvb[:, vb * 512:(vb + 1) * 512].rearrange("p (a b) -> p a b", a=8), op=ALU.add),
                        reads=[("ps", bk), "bvb"], writes=[("vstg", vs)])
                    r0 = t * T + s4 * 128
                    dst = self.v_s[r0:r0 + 128, vb * 520:(vb + 1) * 520].rearrange("p (a b) -> p a b", a=8)
                    sc.add("sync", lambda e, vs=vs, dst=dst: e.dma_start(out=dst, in_=vstg[vs]),
                           reads=[("vstg", vs)], writes=["v_s"], dma_key=("vstg", vs))
            for gu in range(4):
                w, wname = self.wunit(l, f"win{13 + gu}")
                w3 = w.rearrange("p (k f) -> p k f", k=8)
                for q in range(4):
                    gc = gu * 4 + q
                    bk = zbank()
                    pz = self.ps[bk]
                    for kc in range(KC):
                        sc.add("tensor", lambda e, pz=pz, kc=kc, q=q, w3=w3: e.matmul(
                            pz[:], lhsT=w3[:, kc, q * 128:(q + 1) * 128], rhs=h[:, kc, :],
                            start=(kc == 0), stop=(kc == KC - 1)),
                            reads=[wname, (("h", 0), kc)], writes=[("ps", bk)])
                    si = stgn[0] % 4
                    stgn[0] += 1
                    bc = cbase + 72 + gc
                    sc.add("scalar", lambda e, pz=pz, si=si, bc=bc: e.activation(
                        out=stg[si], in_=pz[:], func=AF.Sigmoid, bias=self.cst[:, bc:bc + 1], scale=1.0),
                        reads=[("ps", bk), "cst"], writes=[("stg", si)])
                    store("g_s", si, self.g_s[gc * 128:(gc + 1) * 128, tok])
            for u in range(4):
                w, wname = self.wunit(l, f"win{9 + u}")
                w3 = w.rearrange("p (k f) -> p k f", k=8)
                for jj in range(2):
                    j = 2 * u + jj
                    ab = j % 2
                    pa = self.ps[1 + ab]
                    pb = self.ps[3 + ab]
                    for kc in range(KC):
                        sc.add("tensor", lambda e, pa=pa, kc=kc, jj=jj, w3=w3: e.matmul(
                            pa[:], lhsT=w3[:, kc, jj * 128:(jj + 1) * 128], rhs=h[:, kc, :],
                            start=(kc == 0), stop=(kc == KC - 1)),
                            reads=[wname, (("h", 0), kc)], writes=[("ps", 1 + ab)])
                    for kc in range(KC):
                        sc.add("tensor", lambda e, pb=pb, kc=kc, jj=jj, w3=w3: e.matmul(
                            pb[:], lhsT=w3[:, kc, 256 + jj * 128:256 + (jj + 1) * 128], rhs=h[:, kc, :],
                            start=(kc == 0), stop=(kc == KC - 1)),
                            reads=[wname, (("h", 0), kc)], writes=[("ps", 3 + ab)])
                    bca = cbase + 56 + j
                    bcg = cbase + 64 + j
                    sc.add("scalar", lambda e, pb=pb, ab=ab, bcg=bcg: e.activation(
                        out=t1[ab], in_=pb[:], func=AF.Sigmoid, bias=self.cst[:, bcg:bcg + 1], scale=1.0),
                        reads=[("ps", 3 + ab), "cst"], writes=[("t1", ab)])
                    si = stgn[0] % 4
                    stgn[0] += 1
                    sc.add("vector", lambda e, pa=pa, ab=ab, si=si, bca=bca: e.scalar_tensor_tensor(
                        out=stg[si], in0=pa[:], scalar=self.cst[:, bca:bca + 1], in1=t1[ab],
                        op0=ALU.add, op1=ALU.mult),
                        reads=[("ps", 1 + ab), ("t1", ab), "cst"], writes=[("stg", si)])
                    store("uc_s", si, self.uc_s[j * 128:(j + 1) * 128, tok])
            zc[0] = 0
            wcur = [None, None]

            def stage1(c):
                q = c % 4
                if q == 0:
                    w, wname = self.wunit(l, f"win{c // 4}")
                    wcur[0] = w.rearrange("p (k f) -> p k f", k=8)
                    wcur[1] = wname
                w3, wname = wcur
                i2 = c % 2
                pz = self.ps[1 + i2]
                for kc in range(KC):
                    sc.add("tensor", lambda e, pz=pz, kc=kc, q=q, w3=w3: e.matmul(
                        pz[:], lhsT=w3[:, kc, q * 128:(q + 1) * 128], rhs=h[:, kc, :],
                        start=(kc == 0), stop=(kc == KC - 1)),
                        reads=[wname, (("h", 0), kc)], writes=[("ps", 1 + i2)])
                bc = cbase + 32 + c
                gcl = cbase + 88 + c
                bgc = l * 24 + c
                sc.add("scalar", lambda e, pz=pz, i2=i2, bc=bc: e.activation(
                    out=sqc[i2], in_=pz[:], func=AF.Square, bias=self.cst[:, bc:bc + 1], scale=1.0),
                    reads=[("ps", 1 + i2), "cst"], writes=[("sqc", i2)])
                sc.add("scalar", lambda e, pz=pz, i2=i2, gcl=gcl, bgc=bgc: e.activation(
                    out=uu[i2], in_=pz[:], func=AF.Identity, bias=self.bgt[:, bgc:bgc + 1],
                    scale=self.cst[:, gcl:gcl + 1]),
                    reads=[("ps", 1 + i2), "cst", "bgt"], writes=[("uu", i2)])

            def stage2(c):
                i2 = c % 2
                pm = self.ps[3 + i2]
                pr = self.ps[5 + i2]
                sc.add("tensor", lambda e, pm=pm, i2=i2: e.matmul(pm[:], lhsT=self.blk, rhs=sqc[i2], start=True, stop=True),
                       reads=[("sqc", i2), "blk"], writes=[("ps", 3 + i2)])
                sc.add("tensor", lambda e, pr=pr, i2=i2: e.matmul(pr[:], lhsT=self.Rbf, rhs=uu[i2], start=True, stop=True),
                       reads=[("uu", i2), "Rbf"], writes=[("ps", 5 + i2)])
                self.emit_rstd(rsc[i2], pm[:], ("ps", 3 + i2), ("rsc", i2))
                sc.add("gpsimd", lambda e, i2=i2: e.tensor_tensor(out=t1[i2], in0=uu[i2], in1=cs[0], op=ALU.mult),
                       reads=[("uu", i2), ("cs", 0)], writes=[("t1", i2)])
                sc.add("vector", lambda e, pr=pr, i2=i2: e.tensor_tensor(out=t2[i2], in0=pr[:], in1=cs[1], op=ALU.mult),
                       reads=[("ps", 5 + i2), ("cs", 1)], writes=[("t2", i2)])
                sc.add("gpsimd", lambda e, i2=i2: e.tensor_tensor(out=t1[i2], in0=t1[i2], in1=t2[i2], op=ALU.add),
                       reads=[("t1", i2), ("t2", i2)], writes=[("t1", i2)])
                si = stgn[0] % 4
                stgn[0] += 1
                sc.add("vector", lambda e, i2=i2, si=si: e.tensor_tensor(out=stg[si], in0=t1[i2], in1=rsc[i2], op=ALU.mult),
                       reads=[("t1", i2), ("rsc", i2)], writes=[("stg", si)])
                if c < 12:
                    store("qT_s", si, self.qT_s[c * 128:(c + 1) * 128, tok])
                else:
                    store("kT_s", si, self.kT_s[(c - 12) * 128:(c - 11) * 128, tok])

            for c in range(25):
                if c < 24:
                    stage1(c)
                if c >= 1:
                    stage2(c - 1)
            if l == 0:
                self.emit_conv(8)
        sc.fence()

    def phase_m2(self, l):
        sc = self.sc
        o = 0
        qT = [self.carve(o + i * 8192, [128, S], BF16) for i in range(2)]; o += 16384
        kT = [self.carve(o + i * 8192, [128, S], BF16) for i in range(2)]; o += 16384
        vt = [self.carve(o + i * 2432, [128, 9, 130], BF16) for i in range(2)]; o += 4864
        pT = [self.carve(o + i * 1024, [128, 512], BF16) for i in range(3)]; o += 3072
        nds = [self.carve(o + i * 2080, [128, 4, 130], F32) for i in range(2)]; o += 4160
        assert o <= self.arena_bytes, o
        it = 0
        vcn = 0
        blkn = 0
        ndn = 0
        for g, (window, dil) in enumerate(GROUPS):
            L = S // dil
            nblk = L // 128
            if g not in self.cfg.get("m2_groups", (0, 1, 2)):
                continue
            for hp in range(4):
                cq = g * 4 + hp
                for hh in range(2):
                    r0 = cq * 128 + hh * 64
                    sc.add("sync", lambda e, hh=hh, r0=r0: e.dma_start(out=qT[hh][0:64, :], in_=self.qT_s[r0:r0 + 64, :]),
                           reads=["qT_s"], writes=[("qT", hh)], dma_key=("qT", hh))
                    sc.add("sync", lambda e, hh=hh, r0=r0: e.dma_start(out=kT[hh][0:64, :], in_=self.kT_s[r0:r0 + 64, :]),
                           reads=["kT_s"], writes=[("kT", hh)], dma_key=("kT", hh))
                col0 = (g * 8 + hp * 2) * 65
                lvl = self.cfg.get("m2_lvl", 4)

                def kbase(j, nblk=nblk, L=L):
                    if j == 0:
                        return 0
                    if j == nblk:
                        return L - 128
                    return j * 128 - 64

                blocks = []
                for r in range(dil):
                    for b0 in range(0, nblk, 8):
                        b1 = min(b0 + 8, nblk)
                        for nb in range(b0, b1):
                            blocks.append(dict(r=r, b0=b0, b1=b1, nb=nb))

                def stageA(B):
                    nonlocal vcn, blkn
                    r, b0, b1, nb = B["r"], B["b0"], B["b1"], B["nb"]
                    if nb == b0:
                        vs = vcn % 2
                        vcn += 1
                        segs = []
                        jlo, jhi = b0, b1
                        if jlo == 0:
                            segs.append((0, 0))
                            jlo = 1
                        last_special = (jhi == nblk)
                        if last_special:
                            jhi = nblk - 1
                        while jlo <= jhi:
                            je = min(jlo + 3, jhi)
                            segs.append((jlo, je))
                            jlo = je + 1
                        if last_special:
                            segs.append((nblk, nblk))
                        for k_, (ja, jb) in enumerate(segs):
                            nt_ = jb - ja + 1
                            base = kbase(ja) * dil + r
                            src = self.v_s[_ss(base, 128 * nt_, dil), col0:col0 + 130].rearrange("(j i) c -> i j c", i=128)
                            dst = vt[vs][:, ja - b0:ja - b0 + nt_, :]
                            sc.add("sync", lambda e, src=src, dst=dst: e.dma_start(out=dst, in_=src),
                                   reads=["v_s"], writes=[("vt", vs, k_)], dma_key=("vt", vs, k_))
                        self._m2_chunk = (vs, [("vt", vs, k_) for k_ in range(len(segs))])
                    B["vs"], B["vnames"] = self._m2_chunk
                    bi = blkn % 3
                    blkn += 1
                    B["bi"] = bi
                    pS = self.ps[(1, 2, 5)[bi]]
                    q0 = nb * 128 * dil + r
                    for hh in range(2):
                        for tl in range(2):
                            kb = kbase(nb + tl) * dil + r
                            sc.add("tensor", lambda e, pS=pS, hh=hh, tl=tl, kb=kb, q0=q0, dil=dil: e.matmul(
                                pS[:, (hh * 2 + tl) * 128:(hh * 2 + tl + 1) * 128],
                                lhsT=kT[hh][0:64, _ss(kb, 128, dil)],
                                rhs=qT[hh][0:64, _ss(q0, 128, dil)],
                                start=True, stop=True),
                                reads=[("qT", hh), ("kT", hh)], writes=[("ps", (1, 2, 5)[bi])])
                    sc.add("scalar", lambda e, pS=pS, bi=bi: e.activation(out=pT[bi], in_=pS[:], func=AF.Exp, scale=0.125),
                           reads=[("ps", (1, 2, 5)[bi])], writes=[("pT", bi)])
                    mv = 1 if nb == 0 else (2 if nb == nblk - 1 else 0)
                    meng = "gpsimd" if (blkn % 2 == 0) else "vector"
                    sc.add(meng, lambda e, bi=bi, mv=mv: e.tensor_tensor(out=pT[bi], in0=pT[bi], in1=self.maskbf[mv], op=ALU.mult),
                           reads=[("pT", bi), "maskbf"], writes=[("pT", bi)])

                def stageB(B):
                    nonlocal ndn
                    r, b0, b1, nb, bi, vs, vnames = B["r"], B["b0"], B["b1"], B["nb"], B["bi"], B["vs"], B["vnames"]
                    pO = self.ps[(3, 4, 6)[bi]]
                    for hh in range(2):
                        for tl in range(2):
                            jl = nb + tl - b0
                            sc.add("tensor", lambda e, pO=pO, hh=hh, tl=tl, jl=jl, bi=bi, vs=vs: e.matmul(
                                pO[:, hh * 65:(hh + 1) * 65],
                                lhsT=pT[bi][:, (hh * 2 + tl) * 128:(hh * 2 + tl + 1) * 128],
                                rhs=vt[vs][:, jl, hh * 65:(hh + 1) * 65],
                                start=(tl == 0), stop=(tl == 1)),
                                reads=[("pT", bi)] + vnames, writes=[("ps", (3, 4, 6)[bi])])
                    ns = ndn % 2
                    sub = (nb - b0) % 4
                    if (nb - b0) % 2 == 0:
                        sc.add("scalar", lambda e, pO=pO, ns=ns, sub=sub: e.copy(out=nds[ns][:, sub, :], in_=pO[:, 0:130]),
                               reads=[("ps", (3, 4, 6)[bi])], writes=[("nds", ns, sub)])
                    else:
                        sc.add("vector", lambda e, pO=pO, ns=ns, sub=sub: e.tensor_copy(out=nds[ns][:, sub, :], in_=pO[:, 0:130]),
                               reads=[("ps", (3, 4, 6)[bi])], writes=[("nds", ns, sub)])
                    if sub == 3 or nb == b1 - 1:
                        nbs = nb - sub
                        nsub = sub + 1
                        rbase = nbs * 128 * dil + r
                        dst = self.nd_s[g, _ss(rbase, 128 * nsub, dil), hp * 130:(hp + 1) * 130].rearrange(
                            "(j i) c -> i j c", i=128)
                        sc.add("sync", lambda e, ns=ns, nsub=nsub, dst=dst: e.dma_start(out=dst, in_=nds[ns][:, 0:nsub, :]),
                               reads=[("nds", ns, q_) for q_ in range(nsub)], writes=["nd_s"], dma_key=("nds", ns))
                        ndn += 1

                for i in range(len(blocks) + 2):
                    if i < len(blocks):
                        stageA(blocks[i])
                    if i >= 2:
                        stageB(blocks[i - 2])
        sc.fence()

    def phase_m3(self, l):
        sc = self.sc
        cbase = C_L + l * CL_N
        T3 = 256
        o = 0
        ucs = self.carve(o, [128, 8, T3 + 30], BF16); o += 4608
        gts = self.carve(o, [128, 8, T3], BF16); o += 4096
        cv = self.carve(o, [128, 8, T3], F32); o += 8192
        cvb = [self.carve(o + i * 512, [128, T3], BF16) for i in range(2)]; o += 1024
        sqv = [self.carve(o + i * 512, [128, T3], BF16) for i in range(2)]; o += 1024
        mu = self.carve(o, [128, T3], F32); o += 1024
        rs = self.carve(o, [128, T3], F32); o += 1024
        nmr = self.carve(o, [128, T3], F32); o += 1024
        tmpv = self.carve(o, [128, T3], F32); o += 1024
        tn = [self.carve(o + i * 1024, [128, T3], F32) for i in range(2)]; o += 2048
        sact = self.carve(o, [128, 8, T3], BF16); o += 4096
        merged = self.carve(o, [128, 8, T3], BF16); o += 4096
        nd = [self.carve(o + i * 2080, [128, 8, 65], F32) for i in range(2)]; o += 4160
        rec = self.carve(o, [128, 8], F32); o += 64
        ya = [self.carve(o + i * 1024, [128, 512], BF16) for i in range(2)]; o += 2048
        yaT = self.carve(o, [128, 4, T3], BF16); o += 2048
        tmpm = [self.carve(o + i * 1024, [128, T3], F32) for i in range(2)]; o += 2048
        assert o <= self.arena_bytes, o
        psT = self.ps[7][:].bitcast(BF16)
        psT0 = self.ps[0][:].bitcast(BF16)
        for t in range(S // T3):
            t0 = t * T3
            tok = slice(t0, t0 + T3)
            lo = t0 - 15
            hi = t0 + T3 + 15
            c0 = 0
            c1 = T3 + 30
            if lo < 0:
                sc.add("gpsimd", lambda e: e.memset(ucs[:, :, 0:15], 0.0), writes=["ucs"])
                c0 = 15
                lo = 0
            if hi > S:
                sc.add("gpsimd", lambda e: e.memset(ucs[:, :, T3 + 15:T3 + 30], 0.0), writes=["ucs"])
                c1 = T3 + 15
                hi = S
            for hf in range(2):
                src = self.uc_s[hf * 512:(hf + 1) * 512, lo:hi].rearrange("(c p) t -> p c t", p=128)
                sc.add("sync", lambda e, src=src, c0=c0, c1=c1, hf=hf: e.dma_start(out=ucs[:, hf * 4:hf * 4 + 4, c0:c1], in_=src),
                       reads=["uc_s"], writes=["ucs"], dma_key=("ucs", hf))
                gsrc = self.g_s[hf * 512:(hf + 1) * 512, tok].rearrange("(c p) t -> p c t", p=128)
                sc.add("sync", lambda e, gsrc=gsrc, hf=hf: e.dma_start(out=gts[:, hf * 4:hf * 4 + 4, :], in_=gsrc),
                       reads=["g_s"], writes=["gts"], dma_key=("gts", hf))
            for s2 in range(2):
                tt = t0 + s2 * 128
                for g in range(3):
                    di = 0 if g == 0 else 1
                    sc.add("sync", lambda e, g=g, di=di, tt=tt: e.dma_start(
                        out=nd[di], in_=self.nd_s[g, tt:tt + 128, :].rearrange("p (a b) -> p a b", a=8)),
                        reads=["nd_s"], writes=[("nd", di)], dma_key=("nd", di))
                    if g > 0:
                        sc.add("gpsimd", lambda e: e.tensor_tensor(out=nd[0], in0=nd[0], in1=nd[1], op=ALU.add),
                               reads=[("nd", 0), ("nd", 1)], writes=[("nd", 0)])
                sc.add("vector", lambda e: e.reciprocal(out=rec, in_=nd[0][:, :, 64]), reads=[("nd", 0)], writes=["rec"])
                sc.add("vector", lambda e, s2=s2: e.tensor_tensor(
                    out=ya[s2].rearrange("p (a b) -> p a b", a=8), in0=nd[0][:, :, 0:64],
                    in1=rec.unsqueeze(2).to_broadcast([128, 8, 64]), op=ALU.mult),
                    reads=[("nd", 0), "rec"], writes=[("ya", s2)])
            for c in range(8):
                w, wname = self.wunit(l, f"cdg{c}")
                w3 = w.rearrange("p (k f) -> p k f", k=31)
                pc = self.ps[1 + c % 2]
                for k in range(31):
                    sc.add("tensor", lambda e, pc=pc, k=k, c=c, w3=w3: e.matmul(
                        pc[:, 0:T3], lhsT=w3[:, k, :], rhs=ucs[:, c, k:k + T3], start=(k == 0), stop=(k == 30)),
                        reads=[wname, "ucs"], writes=[("ps", 1 + c % 2)])
                bc = cbase + 112 + c
                i2 = c % 2
                sc.add("scalar", lambda e, pc=pc, c=c, bc=bc: e.activation(
                    out=cv[:, c, :], in_=pc[:, 0:T3], func=AF.Identity, bias=self.cst[:, bc:bc + 1], scale=1.0),
                    reads=[("ps", 1 + i2), "cst"], writes=[("cv", c)])
                sc.add("scalar", lambda e, pc=pc, i2=i2, bc=bc: e.activation(
                    out=sqv[i2], in_=pc[:, 0:T3], func=AF.Square, bias=self.cst[:, bc:bc + 1], scale=1.0),
                    reads=[("ps", 1 + i2), "cst"], writes=[("sqv", i2)])
                sc.add("vector", lambda e, c=c, i2=i2: e.tensor_copy(out=cvb[i2], in_=cv[:, c, :]),
                       reads=[("cv", c)], writes=[("cvb", i2)])
                sc.add("tensor", lambda e, c=c, i2=i2: e.matmul(self.ps[5][:, 0:T3], lhsT=self.onesD, rhs=cvb[i2],
                                                               start=(c == 0), stop=(c == 7)),
                       reads=[("cvb", i2), "onesD"], writes=[("ps", 5)])
                sc.add("tensor", lambda e, c=c, i2=i2: e.matmul(self.ps[6][:, 0:T3], lhsT=self.onesD, rhs=sqv[i2],
                                                               start=(c == 0), stop=(c == 7)),
                       reads=[("sqv", i2), "onesD"], writes=[("ps", 6)])
            for s2 in range(2):
                pT_ = psT if s2 == 0 else psT0
                bkt = 7 if s2 == 0 else 0
                for fc in range(4):
                    sc.add("tensor", lambda e, fc=fc, s2=s2, pT_=pT_: e.transpose(
                        pT_[:, fc * 128:(fc + 1) * 128], ya[s2][:, fc * 128:(fc + 1) * 128], self.identbf),
                        reads=[("ya", s2), "identbf"], writes=[("ps", bkt)])
                sc.add("scalar", lambda e, s2=s2, pT_=pT_: e.copy(out=yaT[:, :, s2 * 128:(s2 + 1) * 128],
                                                               in_=pT_[:, 0:512].rearrange("p (a b) -> p a b", a=4)),
                       reads=[("ps", bkt)], writes=[("yaT", s2)])
            pcnt = 0
            w, wname = self.wunit(l, "ap")
            w3 = w.rearrange("p (k f) -> p k f", k=4)
            for oc in range(8):
                bk = 3 + pcnt % 2
                pcnt += 1
                pp = self.ps[bk]
                for kc in range(4):
                    sc.add("tensor", lambda e, pp=pp, kc=kc, oc=oc, w3=w3: e.matmul(
                        pp[:, 0:T3], lhsT=w3[:, kc, oc * 128:(oc + 1) * 128], rhs=yaT[:, kc, :],
                        start=(kc == 0), stop=(kc == 3)),
                        reads=[wname, ("yaT", 0), ("yaT", 1)], writes=[("ps", bk)])
                sc.add("vector", lambda e, pp=pp, oc=oc: e.tensor_tensor(out=merged[:, oc, :], in0=pp[:, 0:T3], in1=gts[:, oc, :], op=ALU.mult),
                       reads=[("ps", bk), "gts"], writes=[("merged", oc)])
            sc.add("vector", lambda e: e.tensor_copy(out=mu, in_=self.ps[5][:, 0:T3]), reads=[("ps", 5)], writes=["mu"])
            sc.add("vector", lambda e: e.tensor_tensor(out=tmpv, in0=mu, in1=mu, op=ALU.mult), reads=["mu"], writes=["tmpv"])
            sc.add("vector", lambda e: e.tensor_tensor(out=tmpv, in0=self.ps[6][:, 0:T3], in1=tmpv, op=ALU.subtract),
                   reads=[("ps", 6), "tmpv"], writes=["tmpv"])
            self.emit_rstd(rs, tmpv, "tmpv", "rs")
            sc.add("vector", lambda e: e.scalar_tensor_tensor(out=nmr, in0=mu, scalar=-1.0, in1=rs, op0=ALU.mult, op1=ALU.mult),
                   reads=["mu", "rs"], writes=["nmr"])
            for c in range(8):
                i2 = c % 2
                sc.add("gpsimd", lambda e, c=c, i2=i2: e.tensor_tensor(out=tn[i2], in0=cv[:, c, :], in1=rs, op=ALU.mult),
                       reads=[("cv", c), "rs"], writes=[("tn", i2)])
                sc.add("vector", lambda e, i2=i2: e.tensor_tensor(out=tn[i2], in0=tn[i2], in1=nmr, op=ALU.add),
                       reads=[("tn", i2), "nmr"], writes=[("tn", i2)])
                gc_ = cbase + 120 + c
                bc_ = cbase + 128 + c
                sc.add("scalar", lambda e, c=c, i2=i2, gc_=gc_, bc_=bc_: e.activation(
                    out=sact[:, c, :], in_=tn[i2], func=AF.Silu, bias=self.cst[:, bc_:bc_ + 1], scale=self.cst[:, gc_:gc_ + 1]),
                    reads=[("tn", i2), "cst"], writes=[("sact", c)])
            for hf in range(2):
                gsrc2 = self.g_s[1024 + hf * 512:1024 + (hf + 1) * 512, tok].rearrange("(c p) t -> p c t", p=128)
                sc.add("sync", lambda e, gsrc2=gsrc2, hf=hf: e.dma_start(out=gts[:, hf * 4:hf * 4 + 4, :], in_=gsrc2),
                       reads=["g_s"], writes=["gts"], dma_key=("gts", hf))
            for u in range(2):
                w, wname = self.wunit(l, f"cp{u}")
                w3 = w.rearrange("p (k f) -> p k f", k=8)
                for q in range(4):
                    oc = u * 4 + q
                    bk = 3 + pcnt % 2
                    pcnt += 1
                    pp = self.ps[bk]
                    for kc in range(8):
                        sc.add("tensor", lambda e, pp=pp, kc=kc, q=q, w3=w3: e.matmul(
                            pp[:, 0:T3], lhsT=w3[:, kc, q * 128:(q + 1) * 128], rhs=sact[:, kc, :],
                            start=(kc == 0), stop=(kc == 7)),
                            reads=[wname, ("sact", kc)], writes=[("ps", bk)])
                    bcp = cbase + 136 + oc
                    i2 = oc % 2
                    sc.add("vector", lambda e, pp=pp, oc=oc, bcp=bcp, i2=i2: e.scalar_tensor_tensor(
                        out=tmpm[i2], in0=pp[:, 0:T3], scalar=self.cst[:, bcp:bcp + 1], in1=gts[:, oc, :],
                        op0=ALU.add, op1=ALU.mult),
                        reads=[("ps", bk), "gts", "cst"], writes=[("tmpm", i2)])
                    sc.add("gpsimd", lambda e, oc=oc, i2=i2: e.tensor_tensor(out=merged[:, oc, :], in0=tmpm[i2], in1=merged[:, oc, :], op=ALU.add),
                           reads=[("tmpm", i2), ("merged", oc)], writes=[("merged", oc)])
            for u in range(2):
                w, wname = self.wunit(l, f"wo{u}")
                w3 = w.rearrange("p (k f) -> p k f", k=8)
                for q in range(4):
                    oc = u * 4 + q
                    bk = 3 + pcnt % 2
                    pcnt += 1
                    pp = self.ps[bk]
                    for kc in range(8):
                        sc.add("tensor", lambda e, pp=pp, kc=kc, q=q, w3=w3: e.matmul(
                            pp[:, 0:T3], lhsT=w3[:, kc, q * 128:(q + 1) * 128], rhs=merged[:, kc, :],
                            start=(kc == 0), stop=(kc == 7)),
                            reads=[wname] + [("merged", k_) for k_ in range(8)], writes=[("ps", bk)])
                    sc.add("vector", lambda e, pp=pp, oc=oc, tok=tok: e.tensor_tensor(
                        out=self.xT[:, oc, tok], in0=pp[:, 0:T3], in1=self.xT[:, oc, tok], op=ALU.add),
                        reads=[("ps", bk), ("x", oc, t // 2)], writes=[("x", oc, t // 2)])
        sc.fence()

    def phase_final_norm(self, l, last):
        sc = self.sc
        sq = [self.carve(0, [128, T], BF16), self.carve(1024, [128, T], BF16)]
        rstd = self.carve(2048, [128, T], F32)
        ostg = [self.carve(4096 + i * 4096, [128, 1024], F32) for i in range(2)]
        cb = C_L + l * CL_N + 24
        for t in range(NT):
            tok = slice(t * T, (t + 1) * T)
            for kc in range(KC):
                s2 = kc % 2
                sc.add("scalar", lambda e, kc=kc, s2=s2, tok=tok: e.activation(out=sq[s2], in_=self.xT[:, kc, tok], func=AF.Square),
                       reads=[("x", kc, t)], writes=[("sq", s2)])
                sc.add("tensor", lambda e, kc=kc, s2=s2: e.matmul(self.ps[0][:], lhsT=self.onesD, rhs=sq[s2],
                                                                  start=(kc == 0), stop=(kc == KC - 1)),
                       reads=[("sq", s2), "onesD"], writes=[("ps", 0)])
            self.emit_rstd(rstd, self.ps[0][:], ("ps", 0), "rstd")
            for kc in range(KC):
                sc.add("vector", lambda e, kc=kc, tok=tok: e.scalar_tensor_tensor(
                    out=self.xT[:, kc, tok], in0=self.xT[:, kc, tok], scalar=self.cst[:, cb + kc:cb + kc + 1],
                    in1=rstd, op0=ALU.mult, op1=ALU.mult),
                    reads=[("x", kc, t), "rstd", "cst"], writes=[("x", kc, t)])
            if not last:
                continue
            for s4 in range(4):
                i = t * 4 + s4
                osl = i % 2
                for hb in range(2):
                    bank = 1 + (2 * i + hb) % 4
                    pst = self.ps[bank]
                    for q in range(4):
                        kc = hb * 4 + q
                        sc.add("tensor", lambda e, pst=pst, q=q, kc=kc, i=i: e.transpose(
                            pst[:, q * 128:(q + 1) * 128], self.xT[:, kc, i * 128:(i + 1) * 128], self.ident),
                            reads=[("x", kc, t), "cst"], writes=[("ps", bank)])
                    dst = ostg[osl][:, hb * 512:(hb + 1) * 512]
                    if hb == 0:
                        fn = lambda e, dst=dst, pst=pst: e.copy(out=dst, in_=pst[:])
                        eng = "scalar"
                    else:
                        fn = lambda e, dst=dst, pst=pst: e.tensor_copy(out=dst, in_=pst[:])
                        eng = "vector"
                    sc.add(eng, fn, reads=[("ps", bank)], writes=[("ostg", osl, hb)])
                sc.add("sync", lambda e, osl=osl, i=i: e.dma_start(out=self.y_out[i * 128:(i + 1) * 128, :], in_=ostg[osl]),
                       reads=[("ostg", osl, 0), ("ostg", osl, 1)], writes=[("yout", i)], dma_key=("ostg", osl))
        sc.fence()


_CACHE = {}


def _get_prog(cfg_key, cfg):
    if cfg_key not in _CACHE:
        p = Prog(cfg)
        p.build()
        _CACHE[cfg_key] = p
    return _CACHE[cfg_key]


def kernel(**inputs):
    cfg = inputs.pop("_cfg", None) or {}
    ncores = cfg.get("ncores", NCORES)
    inp = {k: np.asarray(v) for k, v in inputs.items()}
    wpack, _ = _pack_weights(inp)
    cpack = _pack_consts(inp)
    cmask = _pack_masks()
    bvh = np.ascontiguousarray(np.asarray(inp["b_in"], np.float32)[:, 3072:4608])
    cosT, sinT = _rope_tables()
    prog = _get_prog(tuple(sorted(cfg.items())), cfg)
    assert wpack.size == prog.wtotal
    x = np.asarray(inp["x"], np.float32)
    in_maps = []
    for b in range(ncores):
        in_maps.append({"x": np.ascontiguousarray(x[b]), "wpack": wpack, "cpack": cpack, "cmask": cmask,
                        "bv": bvh, "cosT": cosT, "sinT": sinT})
    if cfg.get("trace"):
        res = run_bass_kernel_spmd(prog.nc, in_maps, core_ids=list(range(ncores)), trace=True)
        print("EXEC_TIME_NS", res.exec_time_ns)
    else:
        res = run_bass_kernel_spmd(prog.nc, in_maps, core_ids=list(range(ncores)))
    if cfg.get("dbg"):
        global _DBG
        _DBG = res.results
    out = np.stack([np.asarray(r["y"], np.float32).reshape(S, D) for r in res.results], axis=0)
    return out
```
